# Optimizing a Trainium2 kernel written in Bass

```python
import math
import jax, jax.numpy as jnp
from jax import lax
import numpy as np

D_MODEL = 1024
BATCH = 32
SEQ = 2048
DEPTH = 2

D_CONV_BR = D_MODEL
CONV_A_WIDTH = 3
SSM_EXPAND = 2
D_SSM = SSM_EXPAND * D_MODEL
SSM_HEAD_DIM = 64
SSM_HEADS = D_SSM // SSM_HEAD_DIM
SSM_GROUPS = 4
HEADS_PER_GROUP = SSM_HEADS // SSM_GROUPS
SSM_STATE = 128
SSM_CONV_WIDTH = 4
SSM_CHUNK = 128
D_XBC = D_SSM + 2 * SSM_GROUPS * SSM_STATE

SPLIT_SIZES = (
    D_CONV_BR,
    D_CONV_BR,
    D_CONV_BR,
    D_CONV_BR,
    D_SSM,
    D_XBC,
    SSM_HEADS,
    D_MODEL,
    D_MODEL,
)
N_IN = sum(SPLIT_SIZES)

DEEPNORM_ALPHA = (2 * DEPTH) ** 0.25
DEEPNORM_BETA = (8 * DEPTH) ** -0.25
LN_EPS = 1e-5
RMS_EPS = 1e-5
DT_MIN = 1e-3
DT_MAX = 1e-1
A_INIT_MIN = 1.0
A_INIT_MAX = 16.0

kernel_name = "hybrid_shortconv_ssd_gated_merge_deepnorm"


def layer_norm(x, g, b):
    xf = x.astype(jnp.float32)
    mu = jnp.mean(xf, axis=-1, keepdims=True)
    var = jnp.mean(jnp.square(xf - mu), axis=-1, keepdims=True)
    return ((xf - mu) * lax.rsqrt(var + LN_EPS) * g + b).astype(x.dtype)


def causal_depthwise_conv(x, w):
    k = w.shape[0]
    return lax.conv_general_dilated(
        x, w[:, None, :].astype(x.dtype), window_strides=(1,), padding=[(k - 1, 0)],
        dimension_numbers=("NWC", "WIO", "NWC"), feature_group_count=x.shape[-1])


def gated_group_rmsnorm(y, z, g):
    h = (y * jax.nn.silu(z)).astype(jnp.float32)
    hg = h.reshape(*h.shape[:-1], SSM_GROUPS, -1)
    hg = hg * lax.rsqrt(jnp.mean(jnp.square(hg), axis=-1, keepdims=True) + RMS_EPS)
    return (hg.reshape(h.shape) * g).astype(y.dtype)


def ssd_chunked(x, dt, a, b_mat, c_mat):
    bsz, seq = x.shape[0], x.shape[1]
    q = SSM_CHUNK
    nc = seq // q
    g, r, p, n = SSM_GROUPS, HEADS_PER_GROUP, SSM_HEAD_DIM, SSM_STATE
    x = x.astype(jnp.float32)
    dt = dt.astype(jnp.float32)
    xd = (x * dt[..., None]).reshape(bsz, nc, q, g, r, p)
    a_cum = jnp.cumsum((dt * a).reshape(bsz, nc, q, g, r), axis=2)
    bc = b_mat.astype(jnp.float32).reshape(bsz, nc, q, g, n)
    cc = c_mat.astype(jnp.float32).reshape(bsz, nc, q, g, n)

    causal = jnp.tril(jnp.ones((q, q), dtype=bool))
    seg = a_cum[:, :, :, None] - a_cum[:, :, None, :]
    decay_in = jnp.exp(jnp.where(causal[:, :, None, None], seg, -jnp.inf))
    cb = jnp.einsum("bclgn,bcsgn->bclsg", cc, bc)
    y_diag = jnp.einsum("bclsgr,bcsgrp->bclgrp", cb[..., None] * decay_in, xd)

    decay_to_end = jnp.exp(a_cum[:, :, -1:] - a_cum)
    chunk_states = jnp.einsum("bcsgn,bcsgrp->bcgrpn", bc, xd * decay_to_end[..., None])
    chunk_decay = jnp.exp(a_cum[:, :, -1])

    def step(state, inp):
        dec, new = inp
        return state * dec[..., None, None] + new, state

    init = jnp.zeros((bsz, g, r, p, n), jnp.float32)
    _, prev = lax.scan(step, init, (jnp.moveaxis(chunk_decay, 1, 0), jnp.moveaxis(chunk_states, 1, 0)))
    prev = jnp.moveaxis(prev, 0, 1)

    y_off = jnp.einsum("bclgn,bcgrpn->bclgrp", cc, prev) * jnp.exp(a_cum)[..., None]
    return (y_diag + y_off).reshape(bsz, seq, SSM_HEADS * p)


def setup_inputs(seed: int = 0) -> dict:
    key = jax.random.key(seed)
    ks = jax.random.split(key, 20)
    f32 = jnp.float32
    nrm = lambda k, shape, s: jax.random.normal(k, shape, f32) * s
    x = jax.random.normal(ks[0], (BATCH, SEQ, D_MODEL), f32)
    ln_in_g = 1.0 + nrm(ks[1], (D_MODEL,), 0.02)
    ln_in_b = nrm(ks[2], (D_MODEL,), 0.02)
    w_in = nrm(ks[3], (DEPTH, D_MODEL, N_IN), D_MODEL ** -0.5)
    conv_a_w = nrm(ks[4], (DEPTH, CONV_A_WIDTH, D_CONV_BR), CONV_A_WIDTH ** -0.5)
    w_a_out = nrm(ks[5], (DEPTH, D_CONV_BR, D_MODEL), DEEPNORM_BETA * D_CONV_BR ** -0.5)
    conv_s_w = nrm(ks[6], (DEPTH, SSM_CONV_WIDTH, D_XBC), SSM_CONV_WIDTH ** -0.5)
    conv_s_b = nrm(ks[7], (DEPTH, D_XBC), 0.01)
    u = jax.random.uniform(ks[8], (DEPTH, SSM_HEADS), f32)
    dt0 = jnp.exp(u * (math.log(DT_MAX) - math.log(DT_MIN)) + math.log(DT_MIN))
    dt_bias = dt0 + jnp.log(-jnp.expm1(-dt0))
    a_log = jnp.log(jax.random.uniform(ks[9], (DEPTH, SSM_HEADS), f32, A_INIT_MIN, A_INIT_MAX))
    d_skip = 1.0 + nrm(ks[10], (DEPTH, SSM_HEADS), 0.1)
    norm_s_g = 1.0 + nrm(ks[11], (DEPTH, D_SSM), 0.02)
    w_s_out = nrm(ks[12], (DEPTH, D_SSM, D_MODEL), DEEPNORM_BETA * D_SSM ** -0.5)
    w_o = nrm(ks[13], (DEPTH, D_MODEL, D_MODEL), DEEPNORM_BETA * D_MODEL ** -0.5)
    ln_g = 1.0 + nrm(ks[14], (DEPTH, D_MODEL), 0.02)
    ln_b = nrm(ks[15], (DEPTH, D_MODEL), 0.02)
    return {"x": x, "ln_in_g": ln_in_g, "ln_in_b": ln_in_b, "w_in": w_in,
            "conv_a_w": conv_a_w, "w_a_out": w_a_out, "conv_s_w": conv_s_w, "conv_s_b": conv_s_b,
            "dt_bias": dt_bias, "a_log": a_log, "d_skip": d_skip, "norm_s_g": norm_s_g,
            "w_s_out": w_s_out, "w_o": w_o, "ln_g": ln_g, "ln_b": ln_b}


def reference(x, ln_in_g, ln_in_b, w_in, conv_a_w, w_a_out, conv_s_w, conv_s_b,
              dt_bias, a_log, d_skip, norm_s_g, w_s_out, w_o, ln_g, ln_b):
    bsz, seq = x.shape[0], x.shape[1]
    offsets = np.cumsum(SPLIT_SIZES)[:-1].tolist()
    bc_split = [D_SSM, D_SSM + SSM_GROUPS * SSM_STATE]
    h = layer_norm(x, ln_in_g, ln_in_b)
    for i in range(DEPTH):
        proj = h @ w_in[i]
        u, b_gate, c_gate, z_a, z_s, xbc, dt_raw, g_a, g_s = jnp.split(proj, offsets, axis=-1)

        y_a = jax.nn.silu(z_a) * b_gate * causal_depthwise_conv(c_gate * u, conv_a_w[i])
        y_a = y_a @ w_a_out[i]

        xbc = jax.nn.silu(causal_depthwise_conv(xbc, conv_s_w[i]) + conv_s_b[i])
        xs, bs, cs = jnp.split(xbc, bc_split, axis=-1)
        dt = jax.nn.softplus(dt_raw.astype(jnp.float32) + dt_bias[i].astype(jnp.float32))
        a = -jnp.exp(a_log[i].astype(jnp.float32))
        xs_h = xs.reshape(bsz, seq, SSM_HEADS, SSM_HEAD_DIM)
        y_s = ssd_chunked(xs_h, dt, a,
                          bs.reshape(bsz, seq, SSM_GROUPS, SSM_STATE),
                          cs.reshape(bsz, seq, SSM_GROUPS, SSM_STATE))
        y_s = y_s + (xs_h.astype(jnp.float32) * d_skip[i].astype(jnp.float32)[:, None]).reshape(bsz, seq, D_SSM)
        y_s = gated_group_rmsnorm(y_s.astype(h.dtype), z_s, norm_s_g[i]) @ w_s_out[i]

        mixed = jax.nn.sigmoid(g_a) * y_a + jax.nn.sigmoid(g_s) * y_s
        out = mixed @ w_o[i]
        h = layer_norm(DEEPNORM_ALPHA * h + out, ln_g[i], ln_b[i])
    return h
```

```python
import numpy as np
from contextlib import ExitStack
import concourse.bass as bass
import concourse.mybir as mybir
from concourse.bass_utils import run_bass_kernel_spmd

F32 = mybir.dt.float32
BF16 = mybir.dt.bfloat16
AF = mybir.ActivationFunctionType
ALU = mybir.AluOpType

D = 1024
DEPTH = 2
SEQ = 2048
BATCH = 32
NCORES = 8
NIN = 11296
OFF_U, OFF_B, OFF_C, OFF_ZA, OFF_ZS, OFF_XBC, OFF_DT, OFF_GA, OFF_GS = (
    0, 1024, 2048, 3072, 4096, 6144, 9216, 9248, 10272)
ALPHA = float((2 * DEPTH) ** 0.25)
LN_EPS = 1e-5
RMS_EPS = 1e-5
NSLOT = 3
SLW = 516

CP_CAW, CP_CSW, CP_CSB, CP_NSG, CP_DTB, CP_ALOG, CP_DSK, CP_N = 0, 48, 240, 288, 320, 384, 448, 512


class Res:
    __slots__ = ("name", "w", "r", "excl")

    def __init__(self, name, excl=False):
        self.name = name
        self.w = None
        self.r = {}
        self.excl = excl


class Sched:
    def __init__(self):
        self.engs = ("pe", "act", "dve", "pool", "sp")
        self.prog = {e: [] for e in self.engs}
        self.cnt = {}
        self.waited = {e: {} for e in self.engs}
        self.bank_i = 0
        self.dma_keys = set()

    def op(self, eng, fn, reads=(), writes=(), dma=None, inc=True):
        deps = {}

        def add(tok, raw):
            if tok is None:
                return
            sk, v = tok
            if sk == eng and dma is None:
                if eng == "pe" or not raw:
                    return
            if deps.get(sk, 0) < v:
                deps[sk] = v

        for r in reads:
            add(r.w, True)
            if r.excl:
                for sk, v in r.r.items():
                    add((sk, v), False)
        for w in writes:
            add(w.w, False)
            for sk, v in w.r.items():
                add((sk, v), False)
        wd = self.waited[eng]
        waits = []
        for sk, v in deps.items():
            if wd.get(sk, 0) < v:
                wd[sk] = v
                waits.append((sk, v))
        if dma is None:
            sk, step = eng, 1
        else:
            sk, step = dma, 16
            self.dma_keys.add(dma)
        n = self.cnt.get(sk, 0) + step
        self.cnt[sk] = n
        tok = (sk, n)
        self.prog[eng].append([waits, fn, sk, n])
        for r in reads:
            if r.r.get(sk, 0) < n:
                r.r[sk] = n
        for w in writes:
            w.w = tok
            w.r = {}
        return tok

    def wait_all(self, eng, toks):
        wd = self.waited[eng]
        waits = []
        for sk, v in toks:
            if wd.get(sk, 0) < v:
                wd[sk] = v
                waits.append((sk, v))
        self.prog[eng].append([waits, None, None, 0])

    def finalize(self):
        import bisect
        ref = {}
        for e in self.engs:
            for waits, fn, sk, n in self.prog[e]:
                for (wsk, v) in waits:
                    if wsk not in self.dma_keys:
                        ref.setdefault(wsk, set()).add(v)
        rank = {k: sorted(v) for k, v in ref.items()}
        out = {e: [] for e in self.engs}
        for e in self.engs:
            for waits, fn, sk, n in self.prog[e]:
                w2 = []
                for (wsk, v) in waits:
                    if wsk in self.dma_keys:
                        w2.append((wsk, v))
                    else:
                        w2.append((wsk, bisect.bisect_left(rank[wsk], v) + 1))
                if fn is None:
                    out[e].append((w2, None, None, 0))
                elif sk in self.dma_keys:
                    out[e].append((w2, fn, sk, 16))
                else:
                    out[e].append((w2, fn, sk, 1 if n in ref.get(sk, ()) else 0))
        self.final = out
        return out


def build(NSEQ=4, NT=4, TCH=4, dump=None, n_layers=DEPTH, stop=None):
    T = TCH * 128
    cur = {"ti": 0}
    if stop is not None and ":" in stop:
        stop_ti, stop_ph = int(stop.split(":")[0]), stop.split(":")[1]
        stop_l = int(stop.split(":")[2]) if stop.count(":") > 1 else 0
    else:
        stop_ti, stop_ph, stop_l = 0, stop, 0

    def stopped(ph):
        return stop_ph == ph and cur["ti"] == stop_ti and cur.get("l", 0) == stop_l
    NTOK = NSEQ * NT * T
    NB = TCH * 32
    nc = bass.Bass("TRN2", target_bir_lowering=False)
    S = Sched()

    def dram(name, shape, dt, kind):
        return nc.dram_tensor(name, list(shape), dt, kind=kind).ap()

    x_d = dram("x", [NTOK, D], F32, "ExternalInput")
    out_d = dram("out", [NTOK, D], F32, "ExternalOutput")
    win_d = dram("w_in", [DEPTH, D, NIN], F32, "ExternalInput")
    wa_d = dram("w_a_out", [DEPTH, D, D], F32, "ExternalInput")
    ws_d = dram("w_s_out", [DEPTH, 2 * D, D], F32, "ExternalInput")
    wo_d = dram("w_o", [DEPTH, D, D], F32, "ExternalInput")
    cpar_d = dram("cpar", [128, CP_N], F32, "ExternalInput")
    lnpar_d = dram("lnpar", [3, 2 * D], F32, "ExternalInput")
    tri_d = dram("tri", [128, 128], F32, "ExternalInput")
    ident_d = dram("ident", [128, 128], F32, "ExternalInput")
    esel_d = dram("esel", [96, 32 * 128], F32, "ExternalInput")
    win_b = dram("win_b", [DEPTH, D, NIN], BF16, "Internal")
    wa_b = dram("wa_b", [DEPTH, D, D], BF16, "Internal")
    ws_b = dram("ws_b", [DEPTH, 2 * D, D], BF16, "Internal")
    wo_b = dram("wo_b", [DEPTH, D, D], BF16, "Internal")
    dump_d = {}
    if dump:
        for name, shape in dump.items():
            dump_d[name] = dram("dbg_" + name, shape, F32, "ExternalOutput")

    es = ExitStack()

    def sb(name, shape, dt):
        return es.enter_context(nc.sbuf_tensor(name, list(shape), dt))

    cpar = sb("cpar_s", [128, CP_N], F32)
    lnbuf = sb("lnbuf", [128, 2 * D], F32)
    tri_f = sb("tri_f", [128, 128], F32)
    ones_f = sb("ones_f", [128, 128], F32)
    ident_b = sb("ident_b", [128, 128], BF16)
    esel = sb("esel_s", [96, 32 * 128], BF16)
    abc = sb("abc", [128, DEPTH * 32], F32)
    wsl = [sb(f"wsl{i}", [128, 8, 512], BF16) for i in range(NSLOT)]
    xst = [sb(f"xst{i}", [128, D], F32) for i in range(2)]
    H = sb("H", [128, TCH, D], F32)
    hT = sb("hT", [128, 8, T], BF16)
    hb = sb("hb", [128, D], BF16)
    ya_in = sb("ya_in", [128, 8, T], BF16)
    SZ = sb("SZ", [128, TCH, 2048], BF16)
    X = sb("X", [128, TCH * 2048], BF16)
    BT = sb("BT", [128, 4, T], BF16)
    CT = sb("CT", [128, 4, T], BF16)
    Btok = sb("Btok", [128, TCH, 512], BF16)
    S32 = [sb(f"S32_{l}", [128, 2048], F32) for l in range(DEPTH)]
    Sbf = sb("Sbf", [128, 2048], BF16)
    YNT = sb("YNT", [128, 16, T], BF16)
    hist_a = sb("hist_a", [128, DEPTH, 8, 2], F32)
    hist_s = sb("hist_s", [128, DEPTH, 24, 3], F32)
    arena = sb("arena", [128, 14 * SLW], F32)
    dtv = sb("dtv", [128, NB], F32)
    dte_ = sb("dte", [128, NB], F32)
    dt_ = sb("dt", [128, NB], F32)
    dtA = sb("dtA", [128, NB], F32)
    dtA3 = sb("dtA3", [128, 3, TCH, 96], BF16)
    dsp = sb("dsp", [128, 3, NB], BF16)
    dr1 = sb("dr1", [128, NB], F32)
    dr2 = sb("dr2", [128, NB], F32)
    tri_b = sb("tri_b", [128, 128], BF16)
    ones_b = sb("ones_b", [128, 128], BF16)
    negA = sb("negA", [128, NB], F32)
    Ecum = sb("Ecum", [128, NB], F32)
    CDs = sb("CD", [128, NB], F32)
    dtend = sb("dtend", [128, NB], F32)
    A3 = sb("A3", [96, T], BF16)
    A3m = sb("A3m", [96, T], BF16)
    stt = sb("stt", [128, 2, 6], F32)
    mv = sb("mv", [128, 4], F32)
    ss = sb("ss", [128, 8], F32)
    mhalf = sb("mhalf", [128, 8], F32)
    banks = [es.enter_context(nc.psum_tensor(f"bank{i}", [128, 512], F32)) for i in range(8)]

    R = {}

    def res(name):
        if name not in R:
            R[name] = Res(name)
        return R[name]

    bres = [res(f"bank{i}") for i in range(8)]
    for b_ in bres:
        b_.excl = True
    slres = [res(f"wsl{i}") for i in range(NSLOT)]
    ares = [res(f"ar{i}") for i in range(14)]

    def aslot(i, w=SLW):
        return arena[:, i * SLW:i * SLW + w]

    def aslot_bf(i, lo, n):
        v = arena[:, i * SLW:i * SLW + 512].bitcast(BF16)
        return v[:, lo:lo + n]

    def newbank():
        i = S.bank_i % 8
        S.bank_i += 1
        return banks[i], bres[i]

    def cp(off, n):
        return cpar[:, off:off + n]

    cst_res = [res("cpar"), res("tri"), res("ident"), res("esel"), res("ones"), res("abc")]
    S.op("pool", lambda e: e.dma_start(out=cpar[:, :], in_=cpar_d[:, :]), writes=[res("cpar")], dma="cst")
    S.op("pool", lambda e: e.dma_start(out=tri_f[:, :], in_=tri_d[:, :]), writes=[res("tri")], dma="cst")
    S.op("pool", lambda e: e.dma_start(out=ident_b[:, :], in_=ident_d[:, :]), writes=[res("ident")], dma="cst")
    S.op("pool", lambda e: e.dma_start(out=tri_b[:, :], in_=tri_d[:, :]), writes=[res("trib")], dma="cst")
    S.op("pool", lambda e: e.dma_start(out=esel[:, :], in_=esel_d[:, :]), writes=[res("esel")], dma="cst")
    tot = ("cst", S.cnt["cst"])
    for nm in ("cpar", "tri", "ident", "esel", "trib"):
        res(nm).w = tot
    S.op("pool", lambda e: e.memset(ones_f[:, :], 1.0), writes=[res("ones")])
    S.op("pool", lambda e: e.memset(mhalf[:, :], -0.5), writes=[res("mhalf")])
    S.op("pool", lambda e: e.memset(ones_b[:, :], 1.0), writes=[res("onesb")])
    S.op("act", lambda e: e.activation(out=abc[:, :], in_=cp(CP_ALOG, 64), func=AF.Exp),
         reads=[res("cpar")], writes=[res("abc")])
    S.op("dve", lambda e: e.tensor_scalar(out=abc[:, :], in0=abc[:, :], scalar1=-1.0, scalar2=None, op0=ALU.mult),
         reads=[res("abc")], writes=[res("abc")])

    def cast(l, grp, dst, src):
        S.op("pool", lambda e: e.dma_start(out=dst, in_=src), writes=[res(f"cv{l}{grp}")], dma=f"cv{l}{grp}")

    for l in range(n_layers):
        for (grp, c0, c1) in ((0, 0, OFF_XBC), (1, OFF_XBC, NIN)):
            for kc in range(8):
                cast(l, grp, win_b[l, kc * 128:(kc + 1) * 128, c0:c1], win_d[l, kc * 128:(kc + 1) * 128, c0:c1])
        for kc in range(8):
            cast(l, 2, wa_b[l, kc * 128:(kc + 1) * 128, :], wa_d[l, kc * 128:(kc + 1) * 128, :])
        for kc in range(16):
            cast(l, 2, ws_b[l, kc * 128:(kc + 1) * 128, :], ws_d[l, kc * 128:(kc + 1) * 128, :])
        for kc in range(8):
            cast(l, 3, wo_b[l, kc * 128:(kc + 1) * 128, :], wo_d[l, kc * 128:(kc + 1) * 128, :])

    def layer_blocks(l):
        bl = []
        for nm, off in (("u", OFF_U), ("C", OFF_C), ("B", OFF_B), ("z", OFF_ZA)):
            for q in range(2):
                bl.append((nm + str(q), win_b, l, 0, off + q * 512, 512, 0))
        for q in range(4):
            bl.append((f"zs{q}", win_b, l, 0, OFF_ZS + q * 512, 512, 0))
        bl.append(("dt", win_b, l, 0, OFF_DT, 32, 1))
        for q in range(6):
            bl.append((f"xbc{q}", win_b, l, 0, OFF_XBC + q * 512, 512, 1))
        for h in range(2):
            bl.append((f"ga{h}", win_b, l, 0, OFF_GA + h * 512, 512, 1))
            bl.append((f"gs{h}", win_b, l, 0, OFF_GS + h * 512, 512, 1))
            bl.append((f"wa{h}", wa_b, l, 0, h * 512, 512, 2))
            bl.append((f"wsa{h}", ws_b, l, 0, h * 512, 512, 2))
            bl.append((f"wsb{h}", ws_b, l, 1024, h * 512, 512, 2))
        for h in range(2):
            bl.append((f"wo{h}", wo_b, l, 0, h * 512, 512, 3))
        return bl

    tiles = [(s, t) for s in range(NSEQ) for t in range(NT)]
    gblocks = []
    for _ in tiles:
        for l in range(n_layers):
            gblocks.extend(layer_blocks(l))
    st = {"next_load": 0, "next_use": 0}

    def prefetch():
        i = st["next_load"]
        if i >= len(gblocks):
            return
        st["next_load"] += 1
        nm, src, l, r0, c0, ncol, grp = gblocks[i]
        s = i % NSLOT
        src_ap = src[l, r0:r0 + 1024, c0:c0 + ncol].rearrange("(kc p) c -> p kc c", p=128)
        dst_ap = wsl[s][:, :, 0:ncol]
        S.op("sp", lambda e: e.dma_start(out=dst_ap, in_=src_ap), reads=[res(f"cv{l}{grp}")],
             writes=[slres[s]], dma=f"wl{s}")

    def use_block(name):
        i = st["next_use"]
        assert gblocks[i][0] == name, (gblocks[i][0], name)
        st["next_use"] += 1
        s = i % NSLOT
        return wsl[s], slres[s]

    def done_block():
        prefetch()

    for _ in range(NSLOT):
        prefetch()

    def mmgroup(out_ap, bres_, pairs, reads, first_start=True, last_inc=True):
        n = len(pairs)
        for i, (lt, rh) in enumerate(pairs):
            S.op("pe", lambda e, lt=lt, rh=rh, i=i: e.matmul(out_ap, lhsT=lt, rhs=rh, start=(first_start and i == 0),
                                                          stop=(i == n - 1)),
                 reads=reads, writes=[bres_], inc=(last_inc and i == n - 1))

    def layernorm(src_fn, src_res, c, lnres, eps, l_next):
        Hc = H[:, c, :]
        Hres = res(f"H{c}")
        for k in range(2):
            S.op("dve", lambda e, k=k: e.bn_stats(out=stt[:, k, :], in_=src_fn(k)), reads=src_res, writes=[res("stt")])
        S.op("dve", lambda e: e.bn_aggr(out=mv[:, 0:2], in_=stt[:, :, :]), reads=[res("stt")], writes=[res("mv")])
        S.op("dve", lambda e: e.tensor_scalar(out=mv[:, 3:4], in0=mv[:, 1:2], scalar1=float(eps), scalar2=None,
                                              op0=ALU.add), reads=[res("mv")], writes=[res("mv3")])
        S.op("pool", lambda e: e.tensor_tensor(out=mv[:, 2:3], in0=mv[:, 3:4], in1=mhalf[:, 0:1], op=ALU.pow),
             reads=[res("mv3"), res("mhalf")], writes=[res("mv2")])
        for k in range(2):
            S.op("dve", lambda e, k=k: e.tensor_scalar(out=Hc[:, k * 512:(k + 1) * 512], in0=src_fn(k),
                                                       scalar1=mv[:, 0:1], scalar2=mv[:, 2:3],
                                                       op0=ALU.subtract, op1=ALU.mult),
                 reads=src_res + [res("mv"), res("mv2")], writes=[Hres])
        S.op("pool", lambda e: e.tensor_tensor(out=Hc, in0=Hc, in1=lnbuf[:, 0:D], op=ALU.mult),
             reads=[Hres, lnres], writes=[Hres])
        S.op("pool", lambda e: e.tensor_tensor(out=Hc, in0=Hc, in1=lnbuf[:, D:2 * D], op=ALU.add),
             reads=[Hres, lnres], writes=[Hres])
        if l_next:
            S.op("act", lambda e: e.activation(out=hb[:, :], in_=Hc, func=AF.Copy), reads=[Hres], writes=[res("hb")])
            bk, br = newbank()
            bv = bk[:, :].bitcast(BF16)
            for kc in range(8):
                S.op("pe", lambda e, kc=kc: e.transpose(out=bv[:, kc * 128:(kc + 1) * 128],
                                                        in_=hb[:, kc * 128:(kc + 1) * 128], identity=ident_b[:, :]),
                     reads=[res("hb"), res("ident")], writes=[br], inc=(kc == 7))
            S.op("act", lambda e: e.activation(out=hT[:, :, c * 128:(c + 1) * 128],
                                               in_=bv.rearrange("p (a b) -> p a b", a=8), func=AF.Copy),
                 reads=[br], writes=[res(f"hT{c}")])

    def load_ln(i):
        src = lnpar_d[i:i + 1, :].partition_broadcast(128)
        S.op("pool", lambda e: e.dma_start(out=lnbuf[:, :], in_=src), writes=[res("lnbuf")], dma="lnl")

    hTres = [res(f"hT{c}") for c in range(TCH)]
    Xres = [res(f"X{c}") for c in range(TCH)]
    mixres = Xres[0:max(1, T // 256)]
    mixedT = X[:, 0:8 * T].rearrange("p (j t) -> p j t", j=8)

    def dumpit(name, ap, rs):
        if dump and name in dump_d and not st.get("dumped_" + name):
            st["dumped_" + name] = True
            S.op("pool", lambda e: e.dma_start(out=dump_d[name], in_=ap), reads=rs, dma="dbg")

    def layer(l, seq_first, seq_last_tile, is_last_layer, row0):
        caw = cp(CP_CAW, 48).rearrange("p (l c k) -> p l c k", l=2, c=8)
        csw = cp(CP_CSW, 192).rearrange("p (l c k) -> p l c k", l=2, c=24)
        csb = cp(CP_CSB, 48).rearrange("p (l c) -> p l c", l=2)
        nsg = cp(CP_NSG, 32).rearrange("p (l c) -> p l c", l=2)
        dtb = cp(CP_DTB, 64)[:, l * 32:(l + 1) * 32]
        dsk = cp(CP_DSK, 64)[:, l * 32:(l + 1) * 32]
        A_l = abc[:, l * 32:(l + 1) * 32]
        cres = res("cpar")

        def inproj_fm(slot, sres, q):
            bk, br = newbank()
            mmgroup(bk[:, 0:T], br, [(slot[:, kc, q * 128:(q + 1) * 128], hT[:, kc, :]) for kc in range(8)],
                    reads=[sres] + hTres)
            return bk, br

        for q2 in range(2):
            slot, sres = use_block(f"u{q2}")
            for q in range(4):
                j = q2 * 4 + q
                bk, br = inproj_fm(slot, sres, q)
                S.op("act", lambda e, bk=bk, j=j: e.activation(out=aslot(j, T), in_=bk[:, 0:T], func=AF.Copy),
                     reads=[br], writes=[ares[j]])
            done_block()
        for q2 in range(2):
            slot, sres = use_block(f"C{q2}")
            for q in range(4):
                j = q2 * 4 + q
                bk, br = inproj_fm(slot, sres, q)
                ci = 8 + (j % 2)
                cu = aslot(ci)
                hres = res(f"hista{l}_{j}")
                S.op("pool", lambda e, cu=cu, j=j: e.tensor_copy(out=cu[:, 0:2], in_=hist_a[:, l, j, :]),
                     reads=[hres], writes=[ares[ci]])
                S.op("dve", lambda e, cu=cu, bk=bk, j=j: e.tensor_tensor(out=cu[:, 2:2 + T], in0=bk[:, 0:T],
                                                                        in1=aslot(j, T), op=ALU.mult),
                     reads=[br, ares[j]], writes=[ares[ci]])
                S.op("pool", lambda e, cu=cu, j=j: e.tensor_copy(out=hist_a[:, l, j, :], in_=cu[:, T:T + 2]),
                     reads=[ares[ci]], writes=[hres])
                S.op("dve", lambda e, cu=cu, j=j: e.tensor_scalar(out=aslot(j, T), in0=cu[:, 0:T],
                                                                  scalar1=caw[:, l, j, 0:1], scalar2=None, op0=ALU.mult),
                     reads=[ares[ci], cres], writes=[ares[j]])
                for k in (1, 2):
                    S.op("dve", lambda e, cu=cu, j=j, k=k: e.scalar_tensor_tensor(
                        out=aslot(j, T), in0=cu[:, k:k + T], scalar=caw[:, l, j, k:k + 1], in1=aslot(j, T),
                        op0=ALU.mult, op1=ALU.add), reads=[ares[ci], ares[j], cres], writes=[ares[j]])
            done_block()
        for q2 in range(2):
            slot, sres = use_block(f"B{q2}")
            for q in range(4):
                j = q2 * 4 + q
                bk, br = inproj_fm(slot, sres, q)
                S.op("dve", lambda e, bk=bk, j=j: e.tensor_tensor(out=aslot(j, T), in0=bk[:, 0:T], in1=aslot(j, T),
                                                                  op=ALU.mult), reads=[br, ares[j]], writes=[ares[j]])
            done_block()
        for q2 in range(2):
            slot, sres = use_block(f"z{q2}")
            for q in range(4):
                j = q2 * 4 + q
                bk, br = inproj_fm(slot, sres, q)
                si = 10 + (j % 2)
                S.op("act", lambda e, bk=bk, si=si: e.activation(out=aslot(si, T), in_=bk[:, 0:T], func=AF.Silu),
                     reads=[br], writes=[ares[si]])
                S.op("dve", lambda e, j=j, si=si: e.tensor_tensor(out=ya_in[:, j, :], in0=aslot(j, T), in1=aslot(si, T),
                                                                  op=ALU.mult),
                     reads=[ares[j], ares[si]], writes=[res(f"ya{j}")])
            done_block()
        dumpit("ya_in", ya_in[:, :, :], [res(f"ya{j}") for j in range(8)])
        if stopped("A"):
            return

        for q in range(4):
            slot, sres = use_block(f"zs{q}")
            for c in range(TCH):
                bk, br = newbank()
                mmgroup(bk[:, :], br, [(hT[:, kc, c * 128:(c + 1) * 128], slot[:, kc, :]) for kc in range(8)],
                        reads=[sres, hTres[c]])
                S.op("act", lambda e, bk=bk, c=c, q=q: e.activation(out=SZ[:, c, q * 512:(q + 1) * 512], in_=bk[:, :],
                                                                    func=AF.Silu), reads=[br], writes=[res(f"SZ{c}")])
            done_block()
        if stopped("B1"):
            return
        slot, sres = use_block("dt")
        bk, br = newbank()
        for c in range(TCH):
            mmgroup(bk[:, c * 32:(c + 1) * 32], br,
                    [(hT[:, kc, c * 128:(c + 1) * 128], slot[:, kc, 0:32]) for kc in range(8)],
                    reads=[sres, hTres[c]], last_inc=(c == TCH - 1))
        done_block()
        dres = res("dtsmall")
        S.op("dve", lambda e, bk=bk: e.tensor_tensor(out=dtv[:, :].rearrange("p (c h) -> p c h", c=TCH),
                                                     in0=bk[:, 0:NB].rearrange("p (c h) -> p c h", c=TCH),
                                                     in1=dtb.unsqueeze(1).broadcast_to([128, TCH, 32]), op=ALU.add),
             reads=[br, cres], writes=[res("dtv")])
        S.op("act", lambda e: e.activation(out=dtv[:, :], in_=dtv[:, :], func=AF.Exp), reads=[res("dtv")],
             writes=[res("dtv")])
        S.op("act", lambda e: e.activation(out=dt_[:, :], in_=dtv[:, :], func=AF.Ln, bias=1.0), reads=[res("dtv")],
             writes=[res("dt")])
        S.op("dve", lambda e: e.tensor_tensor(out=dtA[:, :].rearrange("p (c h) -> p c h", c=TCH),
                                              in0=dt_[:, :].rearrange("p (c h) -> p c h", c=TCH),
                                              in1=A_l.unsqueeze(1).broadcast_to([128, TCH, 32]), op=ALU.mult),
             reads=[res("dt"), res("abc")], writes=[res("dtA")])
        S.op("dve", lambda e: e.tensor_copy(out=dsp[:, 0, :], in_=dtA[:, :]), reads=[res("dtA")], writes=[res("dsp0")])
        S.op("dve", lambda e: e.tensor_tensor(out=dr1[:, :], in0=dtA[:, :], in1=dsp[:, 0, :], op=ALU.subtract),
             reads=[res("dtA"), res("dsp0")], writes=[res("dr1")])
        S.op("dve", lambda e: e.tensor_copy(out=dsp[:, 1, :], in_=dr1[:, :]), reads=[res("dr1")], writes=[res("dsp1")])
        S.op("dve", lambda e: e.tensor_tensor(out=dr2[:, :], in0=dr1[:, :], in1=dsp[:, 1, :], op=ALU.subtract),
             reads=[res("dr1"), res("dsp1")], writes=[res("dr2")])
        S.op("dve", lambda e: e.tensor_copy(out=dsp[:, 2, :], in_=dr2[:, :]), reads=[res("dr2")], writes=[res("dsp2")])
        for i3 in range(3):
            S.op("dve", lambda e, i3=i3: e.tensor_copy(
                out=dtA3[:, i3, :, :].rearrange("p c (r h) -> p c r h", r=3),
                in_=dsp[:, i3, :].rearrange("p (c h) -> p c h", c=TCH).unsqueeze(2).broadcast_to([128, TCH, 3, 32])),
                 reads=[res(f"dsp{i3}")], writes=[res("dtA3")])
        dspres = [res("dsp0"), res("dsp1"), res("dsp2")]
        if stopped("B3a"):
            return
        bk1, br1 = newbank()
        for c in range(TCH):
            for i3 in range(3):
                S.op("pe", lambda e, c=c, i3=i3, bk1=bk1: e.matmul(bk1[:, c * 32:(c + 1) * 32], lhsT=tri_b[:, :],
                                                                  rhs=dsp[:, i3, c * 32:(c + 1) * 32], start=(i3 == 0),
                                                                  stop=(i3 == 2)),
                     reads=[res("trib")] + dspres, writes=[br1], inc=False)
        for c in range(TCH):
            for i3 in range(3):
                S.op("pe", lambda e, c=c, i3=i3, bk1=bk1: e.matmul(bk1[:, 256 + c * 32:256 + (c + 1) * 32],
                                                                  lhsT=ones_b[:, :], rhs=dsp[:, i3, c * 32:(c + 1) * 32],
                                                                  start=(i3 == 0), stop=(i3 == 2)),
                     reads=[res("onesb")] + dspres, writes=[br1], inc=(c == TCH - 1 and i3 == 2))
        bk2, br2 = newbank()
        for c in range(TCH):
            for i3 in range(3):
                S.op("pe", lambda e, c=c, i3=i3, bk2=bk2: e.matmul(bk2[0:96, c * 128:(c + 1) * 128],
                                                                  lhsT=dtA3[:, i3, c, :], rhs=tri_b[:, :],
                                                                  start=(i3 == 0), stop=(i3 == 2)),
                     reads=[res("trib"), res("dtA3")], writes=[br2], inc=(c == TCH - 1 and i3 == 2))
        if stopped("B3b"):
            return
        S.op("dve", lambda e, bk1=bk1: e.tensor_scalar(out=negA[:, :], in0=bk1[:, 0:NB], scalar1=-1.0, scalar2=None,
                                                       op0=ALU.mult), reads=[br1], writes=[res("negA")])
        S.op("act", lambda e, bk1=bk1: e.activation(out=Ecum[:, :], in_=bk1[:, 0:NB], func=AF.Exp), reads=[br1],
             writes=[res("Ecum")])
        S.op("act", lambda e, bk1=bk1: e.activation(out=CDs[:, :], in_=bk1[:, 256:256 + NB], func=AF.Exp), reads=[br1],
             writes=[res("CD")])
        S.op("dve", lambda e, bk1=bk1: e.tensor_tensor(out=dtend[:, :], in0=bk1[:, 256:256 + NB], in1=negA[:, :],
                                                       op=ALU.add), reads=[br1, res("negA")], writes=[res("dtend")])
        S.op("act", lambda e: e.activation(out=dte_[:, :], in_=dtend[:, :], func=AF.Exp), reads=[res("dtend")],
             writes=[res("dte")])
        dumpit("negA", negA[:, :], [res("negA")])
        if stopped("B3"):
            return
        S.op("act", lambda e, bk2=bk2: e.activation(out=A3[0:96, :], in_=bk2[0:96, 0:T], func=AF.Copy), reads=[br2],
             writes=[res("A3")])
        r1 = arena[0:96, 12 * SLW:12 * SLW + T]
        r2 = arena[0:96, 13 * SLW:13 * SLW + T]
        S.op("dve", lambda e, bk2=bk2: e.tensor_tensor(out=r1[0:96, :], in0=bk2[0:96, 0:T], in1=A3[0:96, :],
                                                       op=ALU.subtract), reads=[br2, res("A3")], writes=[ares[12]])
        S.op("dve", lambda e: e.tensor_copy(out=A3m[0:96, :], in_=r1[0:96, :]), reads=[ares[12]], writes=[res("A3m")])
        S.op("dve", lambda e: e.tensor_tensor(out=r2[0:96, :], in0=r1[0:96, :], in1=A3m[0:96, :], op=ALU.subtract),
             reads=[ares[12], res("A3m")], writes=[ares[13]])
        S.op("dve", lambda e: e.tensor_copy(out=A3[64:96, :], in_=r2[64:96, :]), reads=[ares[13]], writes=[res("A3")])
        S.op("dve", lambda e: e.tensor_copy(out=A3[32:64, :], in_=A3m[32:64, :]), reads=[res("A3m")], writes=[res("A3")])
        dumpit("dt", dt_[:, :], [res("dt")])
        dumpit("negA", negA[:, :], [res("negA")])
        if stopped("B"):
            return

        for q2 in range(6):
            slot, sres = use_block(f"xbc{q2}")
            for q in range(4):
                j = q2 * 4 + q
                bk, br = inproj_fm(slot, sres, q)
                ri = j % 2
                raw = aslot(ri)
                hres = res(f"hists{l}_{j}")
                S.op("pool", lambda e, raw=raw, j=j: e.tensor_copy(out=raw[:, 0:3], in_=hist_s[:, l, j, :]),
                     reads=[hres], writes=[ares[ri]])
                S.op("act", lambda e, raw=raw, bk=bk: e.activation(out=raw[:, 3:3 + T], in_=bk[:, 0:T], func=AF.Copy),
                     reads=[br], writes=[ares[ri]])
                S.op("pool", lambda e, raw=raw, j=j: e.tensor_copy(out=hist_s[:, l, j, :], in_=raw[:, T:T + 3]),
                     reads=[ares[ri]], writes=[hres])
                ai = 2 + ri
                acc = aslot(ai, T)
                S.op("dve", lambda e, raw=raw, acc=acc, j=j: e.tensor_scalar(
                    out=acc, in0=raw[:, 0:T], scalar1=csw[:, l, j, 0:1], scalar2=csb[:, l, j:j + 1],
                    op0=ALU.mult, op1=ALU.add), reads=[ares[ri], cres], writes=[ares[ai]])
                for k in (1, 2, 3):
                    S.op("dve", lambda e, raw=raw, acc=acc, j=j, k=k: e.scalar_tensor_tensor(
                        out=acc, in0=raw[:, k:k + T], scalar=csw[:, l, j, k:k + 1], in1=acc,
                        op0=ALU.mult, op1=ALU.add), reads=[ares[ri], ares[ai], cres], writes=[ares[ai]])
                if j < 16:
                    xi = 4 + ri
                    xc = aslot_bf(xi, 0, T)
                    S.op("act", lambda e, acc=acc, xc=xc: e.activation(out=xc, in_=acc, func=AF.Silu),
                         reads=[ares[ai]], writes=[ares[xi]])
                    tbk, tbr = newbank()
                    tv = tbk[:, :].bitcast(BF16)
                    for c in range(TCH):
                        S.op("pe", lambda e, c=c, tv=tv, xc=xc: e.transpose(
                            out=tv[:, c * 128:(c + 1) * 128], in_=xc[:, c * 128:(c + 1) * 128], identity=ident_b[:, :]),
                             reads=[ares[xi], res("ident")], writes=[tbr], inc=(c == TCH - 1))
                    Xv = X[:, :].rearrange("p (c f) -> p c f", c=TCH)
                    S.op("dve", lambda e, tv=tv, j=j, Xv=Xv: e.tensor_copy(
                        out=Xv[:, :, j * 128:(j + 1) * 128], in_=tv[:, 0:T].rearrange("p (c f) -> p c f", c=TCH)),
                         reads=[tbr], writes=Xres)
                elif j < 20:
                    g = j - 16
                    S.op("act", lambda e, acc=acc, g=g: e.activation(out=BT[:, g, :], in_=acc, func=AF.Silu),
                         reads=[ares[ai]], writes=[res(f"BT{g}")])
                    tbk, tbr = newbank()
                    tv = tbk[:, :].bitcast(BF16)
                    for c in range(TCH):
                        S.op("pe", lambda e, c=c, tv=tv, g=g: e.transpose(
                            out=tv[:, c * 128:(c + 1) * 128], in_=BT[:, g, c * 128:(c + 1) * 128],
                            identity=ident_b[:, :]),
                             reads=[res(f"BT{g}"), res("ident")], writes=[tbr], inc=(c == TCH - 1))
                    S.op("dve", lambda e, tv=tv, g=g: e.tensor_copy(
                        out=Btok[:, :, g * 128:(g + 1) * 128], in_=tv[:, 0:T].rearrange("p (c f) -> p c f", c=TCH)),
                         reads=[tbr], writes=[res("Btok")])
                else:
                    g = j - 20
                    S.op("act", lambda e, acc=acc, g=g: e.activation(out=CT[:, g, :], in_=acc, func=AF.Silu),
                         reads=[ares[ai]], writes=[res(f"CT{g}")])
            done_block()
        dumpit("X", X[:, :], Xres)
        dumpit("BT", BT[:, :, :], [res(f"BT{g}") for g in range(4)])
        dumpit("CT", CT[:, :, :], [res(f"CT{g}") for g in range(4)])
        if stopped("C"):
            return

        Sl = S32[l]
        Sres = [res(f"S32_{l}_{g}") for g in range(4)]
        sbres = [res(f"Sbf{g}") for g in range(4)]
        for g in range(4):
            S.op("pool", lambda e, g=g: e.tensor_copy(out=Sbf[:, g * 512:(g + 1) * 512], in_=Sl[:, g * 512:(g + 1) * 512]),
                 reads=[Sres[g]], writes=[sbres[g]])
        for c in range(TCH):
            cs = slice(c * 128, (c + 1) * 128)
            last_chunk = seq_last_tile and c == TCH - 1
            cbk, cbr = newbank()
            for g in range(4):
                S.op("pe", lambda e, g=g, cbk=cbk, cs=cs: e.matmul(cbk[:, g * 128:(g + 1) * 128], lhsT=BT[:, g, cs],
                                                                   rhs=CT[:, g, cs], start=True, stop=True),
                     reads=[res(f"BT{g}"), res(f"CT{g}")], writes=[cbr], inc=(g == 3))
            cbi = c % 2
            CBm = aslot(cbi, 512)
            S.op("dve", lambda e, cbk=cbk, CBm=CBm: e.tensor_tensor(
                out=CBm.rearrange("p (g l) -> p g l", g=4), in0=cbk[:, :].rearrange("p (g l) -> p g l", g=4),
                in1=tri_f[:, :].unsqueeze(1).broadcast_to([128, 4, 128]), op=ALU.mult),
                 reads=[cbr, res("tri")], writes=[ares[cbi]])
            for g in range(4):
                gi = g % 2
                hs = slice(c * 32 + g * 8, c * 32 + (g + 1) * 8)
                Xg = X[:, c * 2048 + g * 512:c * 2048 + (g + 1) * 512].rearrange("p (h d) -> p h d", h=8)
                XDg = aslot_bf(10 + gi, 0, 512)
                XSg = aslot_bf(10 + gi, 512, 512)
                XDDg = aslot_bf(12 + gi, 0, 512)
                yn = aslot_bf(12 + gi, 512, 512)
                r_xd, r_xdd = ares[10 + gi], ares[12 + gi]
                S.op("pool", lambda e, Xg=Xg, XDg=XDg, hs=hs: e.tensor_tensor(
                    out=XDg.rearrange("p (h d) -> p h d", h=8), in0=Xg,
                    in1=dt_[:, hs].unsqueeze(2).broadcast_to([128, 8, 64]), op=ALU.mult),
                     reads=[Xres[c], res("dt")], writes=[r_xd])
                S.op("pool", lambda e, Xg=Xg, XSg=XSg, g=g: e.tensor_tensor(
                    out=XSg.rearrange("p (h d) -> p h d", h=8), in0=Xg,
                    in1=dsk[:, g * 8:(g + 1) * 8].unsqueeze(2).broadcast_to([128, 8, 64]), op=ALU.mult),
                     reads=[Xres[c], cres], writes=[r_xd])
                if not last_chunk:
                    S.op("pool", lambda e, XDg=XDg, XDDg=XDDg, hs=hs: e.tensor_tensor(
                        out=XDDg.rearrange("p (h d) -> p h d", h=8), in0=XDg.rearrange("p (h d) -> p h d", h=8),
                        in1=dte_[:, hs].unsqueeze(2).broadcast_to([128, 8, 64]), op=ALU.mult),
                         reads=[r_xd, res("dte")], writes=[r_xdd])
                MT = aslot_bf(4 + gi, 0, 1024)
                for half in range(2):
                    lbk, lbr = newbank()
                    for hh in range(4):
                        h = g * 8 + half * 4 + hh
                        S.op("pe", lambda e, lbk=lbk, hh=hh, h=h, cs=cs: e.matmul(
                            lbk[:, hh * 128:(hh + 1) * 128], lhsT=esel[:, h * 128:(h + 1) * 128], rhs=A3[:, cs],
                            start=True, stop=True), reads=[res("esel"), res("A3")], writes=[lbr], inc=(hh == 3))
                    li = 2 + half
                    LT = aslot(li, 512)
                    for hh in range(4):
                        h = g * 8 + half * 4 + hh
                        S.op("act", lambda e, lbk=lbk, LT=LT, hh=hh, h=h, c=c: e.activation(
                            out=LT[:, hh * 128:(hh + 1) * 128], in_=lbk[:, hh * 128:(hh + 1) * 128], func=AF.Exp,
                            bias=negA[:, c * 32 + h:c * 32 + h + 1]), reads=[lbr, res("negA")], writes=[ares[li]])
                    S.op("dve", lambda e, LT=LT, MT=MT, half=half, CBm=CBm, g=g: e.scalar_tensor_tensor(
                        out=MT[:, half * 512:(half + 1) * 512].rearrange("p (h l) -> p h l", h=4),
                        in0=LT.rearrange("p (h l) -> p h l", h=4), scalar=1.0,
                        in1=CBm[:, g * 128:(g + 1) * 128].unsqueeze(1).broadcast_to([128, 4, 128]),
                        op0=ALU.min, op1=ALU.mult), reads=[ares[li], ares[cbi]], writes=[ares[4 + gi]])
                ybk, ybr = newbank()
                S.op("pe", lambda e, ybk=ybk, XSg=XSg: e.matmul(ybk[:, :], lhsT=ident_b[:, :], rhs=XSg, start=True,
                                                               stop=False, skip_group_check=True),
                     reads=[res("ident"), r_xd], writes=[ybr], inc=False)
                for hh in range(8):
                    S.op("pe", lambda e, ybk=ybk, MT=MT, XDg=XDg, hh=hh: e.matmul(
                        ybk[:, hh * 64:(hh + 1) * 64], lhsT=MT[:, hh * 128:(hh + 1) * 128],
                        rhs=XDg[:, hh * 64:(hh + 1) * 64], start=False, stop=True, skip_group_check=True),
                         reads=[ares[4 + gi], r_xd], writes=[ybr], inc=(hh == 7))
                obk, obr = newbank()
                S.op("pe", lambda e, obk=obk, g=g, cs=cs: e.matmul(obk[:, :], lhsT=CT[:, g, cs],
                                                                   rhs=Sbf[:, g * 512:(g + 1) * 512], start=True, stop=True),
                     reads=[res(f"CT{g}"), sbres[g]], writes=[obr])
                t1 = aslot(6 + gi, 512)
                t3 = aslot(8 + gi, 512)
                S.op("dve", lambda e, obk=obk, t1=t1, hs=hs: e.tensor_tensor(
                    out=t1.rearrange("p (h d) -> p h d", h=8), in0=obk[:, :].rearrange("p (h d) -> p h d", h=8),
                    in1=Ecum[:, hs].unsqueeze(2).broadcast_to([128, 8, 64]), op=ALU.mult),
                     reads=[obr, res("Ecum")], writes=[ares[6 + gi]])
                S.op("dve", lambda e, ybk=ybk, t1=t1: e.tensor_tensor(out=t1, in0=ybk[:, :], in1=t1, op=ALU.add),
                     reads=[ybr, ares[6 + gi]], writes=[ares[6 + gi]])
                if c == 0 and g == 0:
                    dumpit("ypre", t1, [ares[6 + gi]])
                S.op("pool", lambda e, t1=t1, t3=t3, c=c, g=g: e.tensor_tensor(
                    out=t3, in0=t1, in1=SZ[:, c, g * 512:(g + 1) * 512], op=ALU.mult),
                     reads=[ares[6 + gi], res(f"SZ{c}")], writes=[ares[8 + gi]])
                ssr = res(f"ss{gi}")
                S.op("pool", lambda e, gi=gi: e.memset(ss[:, gi:gi + 1], 0.0), writes=[ssr])
                S.op("act", lambda e, t3=t3, yn=yn, gi=gi: e.activation(out=yn, in_=t3, func=AF.Square,
                                                                       scale=float(512.0 ** -0.5),
                                                                       accum_out=ss[:, gi:gi + 1]),
                     reads=[ares[8 + gi], ssr], writes=[r_xdd, ssr])
                S.op("dve", lambda e, gi=gi: e.tensor_scalar(out=ss[:, 2 + gi:3 + gi], in0=ss[:, gi:gi + 1],
                                                             scalar1=RMS_EPS, scalar2=None, op0=ALU.add),
                     reads=[ssr], writes=[res(f"rt{gi}")])
                S.op("pool", lambda e, gi=gi: e.tensor_tensor(out=ss[:, 4 + gi:5 + gi], in0=ss[:, 2 + gi:3 + gi],
                                                              in1=mhalf[:, 0:1], op=ALU.pow),
                     reads=[res(f"rt{gi}"), res("mhalf")], writes=[res(f"rs{gi}")])
                S.op("act", lambda e, t3=t3, yn=yn, gi=gi: e.activation(out=yn, in_=t3, func=AF.Copy,
                                                                       scale=ss[:, 4 + gi:5 + gi]),
                     reads=[ares[8 + gi], res(f"rs{gi}")], writes=[r_xdd])
                tbk, tbr = newbank()
                tv = tbk[:, :].bitcast(BF16)
                for k in range(4):
                    S.op("pe", lambda e, k=k, tv=tv, yn=yn: e.transpose(
                        out=tv[:, k * 128:(k + 1) * 128], in_=yn[:, k * 128:(k + 1) * 128], identity=ident_b[:, :]),
                         reads=[r_xdd, res("ident")], writes=[tbr], inc=(k == 3))
                S.op("dve", lambda e, tv=tv, g=g, cs=cs: e.tensor_tensor(
                    out=YNT[:, g * 4:(g + 1) * 4, cs], in0=tv[:, 0:512].rearrange("p (k t) -> p k t", k=4),
                    in1=nsg[:, l, g * 4:(g + 1) * 4].unsqueeze(2).broadcast_to([128, 4, 128]), op=ALU.mult),
                     reads=[tbr, cres], writes=[res(f"YNT{g}")])
                if not last_chunk:
                    sbk, sbr_ = newbank()
                    S.op("pe", lambda e, sbk=sbk, c=c, g=g, XDDg=XDDg: e.matmul(
                        sbk[:, :], lhsT=Btok[:, c, g * 128:(g + 1) * 128], rhs=XDDg, start=True, stop=True),
                         reads=[res("Btok"), r_xdd], writes=[sbr_])
                    Sg = Sl[:, g * 512:(g + 1) * 512]
                    S.op("pool", lambda e, Sg=Sg, hs=hs: e.tensor_tensor(
                        out=Sg.rearrange("p (h d) -> p h d", h=8), in0=Sg.rearrange("p (h d) -> p h d", h=8),
                        in1=CDs[:, hs].unsqueeze(2).broadcast_to([128, 8, 64]), op=ALU.mult),
                         reads=[Sres[g], res("CD")], writes=[Sres[g]])
                    S.op("dve", lambda e, Sg=Sg, sbk=sbk: e.tensor_tensor(out=Sg, in0=sbk[:, :], in1=Sg, op=ALU.add),
                         reads=[sbr_, Sres[g]], writes=[Sres[g]])
                    S.op("pool", lambda e, Sg=Sg, g=g: e.tensor_copy(out=Sbf[:, g * 512:(g + 1) * 512], in_=Sg),
                         reads=[Sres[g]], writes=[sbres[g]])
        dumpit("YNT", YNT[:, :, :], [res(f"YNT{g}") for g in range(4)])
        if stopped("D"):
            return

        for half in range(2):
            slot, sres = use_block(f"ga{half}")
            for q in range(4):
                bk, br = inproj_fm(slot, sres, q)
                S.op("act", lambda e, bk=bk, q=q: e.activation(out=aslot(q, T), in_=bk[:, 0:T], func=AF.Tanh, scale=0.5),
                     reads=[br], writes=[ares[q]])
            done_block()
            slot, sres = use_block(f"gs{half}")
            for q in range(4):
                bk, br = inproj_fm(slot, sres, q)
                S.op("act", lambda e, bk=bk, q=q: e.activation(out=aslot(4 + q, T), in_=bk[:, 0:T], func=AF.Tanh,
                                                               scale=0.5), reads=[br], writes=[ares[4 + q]])
            done_block()
            slot, sres = use_block(f"wa{half}")
            for q in range(4):
                bk, br = newbank()
                mmgroup(bk[:, 0:T], br, [(slot[:, kc, q * 128:(q + 1) * 128], ya_in[:, kc, :]) for kc in range(8)],
                        reads=[sres] + [res(f"ya{j}") for j in range(8)])
                S.op("dve", lambda e, bk=bk, q=q: e.scalar_tensor_tensor(
                    out=aslot(q, T), in0=aslot(q, T), scalar=1.0, in1=bk[:, 0:T], op0=ALU.add, op1=ALU.mult),
                     reads=[br, ares[q]], writes=[ares[q]])
            done_block()
            slota, sresa = use_block(f"wsa{half}")
            slotb, sresb = use_block(f"wsb{half}")
            for q in range(4):
                bk, br = newbank()
                pairs = [(slota[:, kc, q * 128:(q + 1) * 128], YNT[:, kc, :]) for kc in range(8)] + \
                        [(slotb[:, kc, q * 128:(q + 1) * 128], YNT[:, 8 + kc, :]) for kc in range(8)]
                mmgroup(bk[:, 0:T], br, pairs, reads=[sresa, sresb] + [res(f"YNT{g}") for g in range(4)])
                S.op("dve", lambda e, bk=bk, q=q: e.scalar_tensor_tensor(
                    out=aslot(4 + q, T), in0=aslot(4 + q, T), scalar=1.0, in1=bk[:, 0:T], op0=ALU.add, op1=ALU.mult),
                     reads=[br, ares[4 + q]], writes=[ares[4 + q]])
                j = half * 4 + q
                S.op("pool", lambda e, q=q, j=j: e.tensor_tensor(out=mixedT[:, j, :], in0=aslot(q, T),
                                                                 in1=aslot(4 + q, T), op=ALU.add),
                     reads=[ares[q], ares[4 + q]], writes=mixres)
            done_block()
            done_block()
        dumpit("mixed", mixedT, mixres)
        slot0, sres0 = use_block("wo0")
        slot1, sres1 = use_block("wo1")
        lnres = res("lnbuf")
        for c in range(TCH):
            bks = []
            for half, (slot, sres) in enumerate(((slot0, sres0), (slot1, sres1))):
                bk, br = newbank()
                mmgroup(bk[:, :], br, [(mixedT[:, kc, c * 128:(c + 1) * 128], slot[:, kc, :]) for kc in range(8)],
                        reads=[sres] + mixres)
                bks.append((bk, br))
            Hres = res(f"H{c}")
            for half, (bk, br) in enumerate(bks):
                S.op("dve", lambda e, bk=bk, half=half, c=c: e.scalar_tensor_tensor(
                    out=aslot(8 + half, 512), in0=H[:, c, half * 512:(half + 1) * 512], scalar=2.0 * ALPHA,
                    in1=bk[:, :], op0=ALU.mult, op1=ALU.add), reads=[br, Hres], writes=[ares[8 + half]])
            layernorm(lambda k: aslot(8 + k, 512), [ares[8], ares[9]], c, lnres, 4.0 * LN_EPS,
                      l_next=not is_last_layer)
            if is_last_layer:
                S.op("pool", lambda e, c=c: e.dma_start(out=out_d[row0 + c * 128:row0 + (c + 1) * 128, :], in_=H[:, c, :]),
                     reads=[Hres], writes=[res(f"out{c}")], dma=f"st{c}")
            else:
                if c == 0:
                    dumpit("h1", H[:, 0, :], [Hres])
        done_block()
        done_block()

    def load_x(ti, c):
        s_, t_ = tiles[ti]
        row = (s_ * NT + t_) * T + c * 128
        i = (ti * TCH + c) % 2
        S.op("pool", lambda e: e.dma_start(out=xst[i][:, :], in_=x_d[row:row + 128, :]), writes=[res(f"xst{i}")],
             dma=f"xl{i}")
        return i

    for ti, (s_, t_) in enumerate(tiles):
        cur["ti"] = ti
        row0 = (s_ * NT + t_) * T
        if t_ == 0:
            for l in range(n_layers):
                S.op("pool", lambda e, l=l: e.memset(S32[l][:, :], 0.0), writes=[res(f"S32_{l}_{g}") for g in range(4)])
                S.op("pool", lambda e, l=l: e.memset(hist_a[:, l, :, :], 0.0),
                     writes=[res(f"hista{l}_{j}") for j in range(8)])
                S.op("pool", lambda e, l=l: e.memset(hist_s[:, l, :, :], 0.0),
                     writes=[res(f"hists{l}_{j}") for j in range(24)])
        load_ln(0)
        for c in range(TCH):
            i = load_x(ti, c)
            layernorm(lambda k, i=i: xst[i][:, k * 512:(k + 1) * 512], [res(f"xst{i}")], c, res("lnbuf"), LN_EPS,
                      l_next=True)
        if ti == 0:
            dumpit("h0", H[:, 0, :], [res("H0")])
        if stopped("ln"):
            break
        for l in range(n_layers):
            load_ln(1 + l)
            cur["l"] = l
            layer(l, t_ == 0, t_ == NT - 1, l == n_layers - 1, row0)
            cur["l"] = 0
            if stop_ph is not None and ti == stop_ti and stop_ph != "L0" and l == stop_l:
                break
            if stopped("L0"):
                break
        if stop_ph is not None and ti == stop_ti:
            break

    S.wait_all("pool", [(k, v) for k, v in S.cnt.items() if k.startswith("st") or k == "dbg"])

    sems = {k: es.enter_context(nc.semaphore(k)) for k in S.cnt.keys()}
    for e in ("pe", "act", "dve", "pool"):
        if e not in sems:
            sems[e] = es.enter_context(nc.semaphore(e))

    S.finalize()

    def replay(name, eng):
        for waits, fn, sk, step in S.final[name]:
            for (wsk, v) in waits:
                eng.wait_ge(sems[wsk], v)
            if fn is None:
                continue
            ins = fn(eng)
            if step:
                ins.then_inc(sems[sk], step)

    with nc.Block() as block:
        @block.tensor
        def _(pe):
            replay("pe", pe)

        @block.scalar
        def _(a):
            replay("act", a)

        @block.vector
        def _(v):
            replay("dve", v)

        @block.gpsimd
        def _(g):
            replay("pool", g)

        @block.sync
        def _(s):
            replay("sp", s)

    es.close()
    return nc, S


def host_consts(inp):
    f = np.float32
    cpar = np.zeros((128, CP_N), f)
    p = np.arange(128)
    caw = np.asarray(inp["conv_a_w"], f)
    csw = np.asarray(inp["conv_s_w"], f)
    csb = np.asarray(inp["conv_s_b"], f)
    nsg = np.asarray(inp["norm_s_g"], f)
    cpar[:, CP_CAW:CP_CAW + 48] = caw.reshape(2, 3, 8, 128).transpose(3, 0, 2, 1).reshape(128, 48)
    cpar[:, CP_CSW:CP_CSW + 192] = csw.reshape(2, 4, 24, 128).transpose(3, 0, 2, 1).reshape(128, 192)
    cpar[:, CP_CSB:CP_CSB + 48] = csb.reshape(2, 24, 128).transpose(2, 0, 1).reshape(128, 48)
    cpar[:, CP_NSG:CP_NSG + 32] = nsg.reshape(2, 16, 128).transpose(2, 0, 1).reshape(128, 32)
    cpar[:, CP_DTB:CP_DTB + 64] = np.broadcast_to(np.asarray(inp["dt_bias"], f).reshape(1, 64), (128, 64))
    cpar[:, CP_ALOG:CP_ALOG + 64] = np.broadcast_to(np.asarray(inp["a_log"], f).reshape(1, 64), (128, 64))
    cpar[:, CP_DSK:CP_DSK + 64] = np.broadcast_to(np.asarray(inp["d_skip"], f).reshape(1, 64), (128, 64))
    lnpar = np.zeros((3, 2 * D), f)
    lnpar[0, :D] = inp["ln_in_g"]
    lnpar[0, D:] = inp["ln_in_b"]
    for l in range(DEPTH):
        lnpar[1 + l, :D] = inp["ln_g"][l]
        lnpar[1 + l, D:] = inp["ln_b"][l]
    tri = (p[:, None] <= p[None, :]).astype(f)
    ident = np.eye(128, dtype=f)
    k = np.arange(96)
    esel = np.zeros((96, 32, 128), f)
    esel[k, k % 32, :] = 1.0
    return {"cpar": cpar, "lnpar": lnpar, "tri": tri, "ident": ident, "esel": esel.reshape(96, 32 * 128)}


_CACHE = {}


def kernel(**inputs):
    x = np.ascontiguousarray(np.asarray(inputs["x"], np.float32))
    nseq = BATCH // NCORES
    TCH = 4
    key = ("full", nseq, TCH)
    if key not in _CACHE:
        _CACHE[key] = build(NSEQ=nseq, NT=SEQ // (TCH * 128), TCH=TCH)[0]
    nc = _CACHE[key]
    consts = host_consts(inputs)
    shared = {
        "w_in": np.ascontiguousarray(np.asarray(inputs["w_in"], np.float32)),
        "w_a_out": np.ascontiguousarray(np.asarray(inputs["w_a_out"], np.float32)),
        "w_s_out": np.ascontiguousarray(np.asarray(inputs["w_s_out"], np.float32)),
        "w_o": np.ascontiguousarray(np.asarray(inputs["w_o"], np.float32)),
    }
    shared.update(consts)
    in_maps = []
    for i in range(NCORES):
        m = dict(shared)
        m["x"] = x[i * nseq:(i + 1) * nseq].reshape(nseq * SEQ, D)
        in_maps.append(m)
    res = run_bass_kernel_spmd(nc, in_maps, core_ids=list(range(NCORES)))
    outs = [np.asarray(r["out"], np.float32).reshape(nseq, SEQ, D) for r in res.results]
    return np.concatenate(outs, axis=0)
```

```python
import numpy as np
from contextlib import ExitStack
import concourse.bass as bass
import concourse.mybir as mybir
from concourse.bass_utils import run_bass_kernel_spmd

F32 = mybir.dt.float32
BF16 = mybir.dt.bfloat16
AF = mybir.ActivationFunctionType
ALU = mybir.AluOpType

D = 1024
DEPTH = 2
SEQ = 2048
BATCH = 32
NCORES = 8
NIN = 11296
OFF_U, OFF_B, OFF_C, OFF_ZA, OFF_ZS, OFF_XBC, OFF_DT, OFF_GA, OFF_GS = (
    0, 1024, 2048, 3072, 4096, 6144, 9216, 9248, 10272)
ALPHA = float((2 * DEPTH) ** 0.25)
LN_EPS = 1e-5
RMS_EPS = 1e-5
NSLOT = 3
SLW = 516

CP_CAW, CP_CSW, CP_CSB, CP_NSG, CP_DTB, CP_ALOG, CP_DSK, CP_N = 0, 48, 240, 288, 320, 384, 448, 512


class Res:
    __slots__ = ("name", "w", "r", "excl")

    def __init__(self, name, excl=False):
        self.name = name
        self.w = None
        self.r = {}
        self.excl = excl


class Sched:
    def __init__(self):
        self.engs = ("pe", "act", "dve", "pool", "sp")
        self.prog = {e: [] for e in self.engs}
        self.cnt = {}
        self.waited = {e: {} for e in self.engs}
        self.bank_i = 0
        self.dma_keys = set()

    def op(self, eng, fn, reads=(), writes=(), dma=None, inc=True):
        deps = {}

        def add(tok, raw):
            if tok is None:
                return
            sk, v = tok
            if sk == eng and dma is None:
                if eng == "pe" or not raw:
                    return
            if deps.get(sk, 0) < v:
                deps[sk] = v

        for r in reads:
            add(r.w, True)
            if r.excl:
                for sk, v in r.r.items():
                    add((sk, v), False)
        for w in writes:
            add(w.w, False)
            for sk, v in w.r.items():
                add((sk, v), False)
        wd = self.waited[eng]
        waits = []
        for sk, v in deps.items():
            if wd.get(sk, 0) < v:
                wd[sk] = v
                waits.append((sk, v))
        if dma is None:
            sk, step = eng, 1
        else:
            sk, step = dma, 16
            self.dma_keys.add(dma)
        n = self.cnt.get(sk, 0) + step
        self.cnt[sk] = n
        tok = (sk, n)
        self.prog[eng].append([waits, fn, sk, n])
        for r in reads:
            if r.r.get(sk, 0) < n:
                r.r[sk] = n
        for w in writes:
            w.w = tok
            w.r = {}
        return tok

    def wait_all(self, eng, toks):
        wd = self.waited[eng]
        waits = []
        for sk, v in toks:
            if wd.get(sk, 0) < v:
                wd[sk] = v
                waits.append((sk, v))
        self.prog[eng].append([waits, None, None, 0])

    def finalize(self):
        import bisect
        ref = {}
        for e in self.engs:
            for waits, fn, sk, n in self.prog[e]:
                for (wsk, v) in waits:
                    if wsk not in self.dma_keys:
                        ref.setdefault(wsk, set()).add(v)
        rank = {k: sorted(v) for k, v in ref.items()}
        out = {e: [] for e in self.engs}
        for e in self.engs:
            for waits, fn, sk, n in self.prog[e]:
                w2 = []
                for (wsk, v) in waits:
                    if wsk in self.dma_keys:
                        w2.append((wsk, v))
                    else:
                        w2.append((wsk, bisect.bisect_left(rank[wsk], v) + 1))
                if fn is None:
                    out[e].append((w2, None, None, 0))
                elif sk in self.dma_keys:
                    out[e].append((w2, fn, sk, 16))
                else:
                    out[e].append((w2, fn, sk, 1 if n in ref.get(sk, ()) else 0))
        self.final = out
        return out


def build(NSEQ=4, NT=4, TCH=4, dump=None, n_layers=DEPTH, stop=None):
    T = TCH * 128
    cur = {"ti": 0}
    if stop is not None and ":" in stop:
        stop_ti, stop_ph = int(stop.split(":")[0]), stop.split(":")[1]
        stop_l = int(stop.split(":")[2]) if stop.count(":") > 1 else 0
    else:
        stop_ti, stop_ph, stop_l = 0, stop, 0

    def stopped(ph):
        return stop_ph == ph and cur["ti"] == stop_ti and cur.get("l", 0) == stop_l
    NTOK = NSEQ * NT * T
    NB = TCH * 32
    nc = bass.Bass("TRN2", target_bir_lowering=False)
    S = Sched()

    def dram(name, shape, dt, kind):
        return nc.dram_tensor(name, list(shape), dt, kind=kind).ap()

    x_d = dram("x", [NTOK, D], F32, "ExternalInput")
    out_d = dram("out", [NTOK, D], F32, "ExternalOutput")
    win_d = dram("w_in", [DEPTH, D, NIN], F32, "ExternalInput")
    wa_d = dram("w_a_out", [DEPTH, D, D], F32, "ExternalInput")
    ws_d = dram("w_s_out", [DEPTH, 2 * D, D], F32, "ExternalInput")
    wo_d = dram("w_o", [DEPTH, D, D], F32, "ExternalInput")
    cpar_d = dram("cpar", [128, CP_N], F32, "ExternalInput")
    lnpar_d = dram("lnpar", [3, 2 * D], F32, "ExternalInput")
    tri_d = dram("tri", [128, 128], F32, "ExternalInput")
    ident_d = dram("ident", [128, 128], F32, "ExternalInput")
    esel_d = dram("esel", [96, 32 * 128], F32, "ExternalInput")
    win_b = dram("win_b", [DEPTH, D, NIN], BF16, "Internal")
    wa_b = dram("wa_b", [DEPTH, D, D], BF16, "Internal")
    ws_b = dram("ws_b", [DEPTH, 2 * D, D], BF16, "Internal")
    wo_b = dram("wo_b", [DEPTH, D, D], BF16, "Internal")
    dump_d = {}
    if dump:
        for name, shape in dump.items():
            dump_d[name] = dram("dbg_" + name, shape, F32, "ExternalOutput")

    es = ExitStack()

    def sb(name, shape, dt):
        return es.enter_context(nc.sbuf_tensor(name, list(shape), dt))

    cpar = sb("cpar_s", [128, CP_N], F32)
    lnbuf = sb("lnbuf", [128, 2 * D], F32)
    tri_f = sb("tri_f", [128, 128], F32)
    ones_f = sb("ones_f", [128, 128], F32)
    ident_b = sb("ident_b", [128, 128], BF16)
    esel = sb("esel_s", [96, 32 * 128], BF16)
    abc = sb("abc", [128, DEPTH * 32], F32)
    wsl = [sb(f"wsl{i}", [128, 8, 512], BF16) for i in range(NSLOT)]
    xst = [sb(f"xst{i}", [128, D], F32) for i in range(2)]
    H = sb("H", [128, TCH, D], F32)
    hT = sb("hT", [128, 8, T], BF16)
    hb = sb("hb", [128, D], BF16)
    ya_in = sb("ya_in", [128, 8, T], BF16)
    SZ = sb("SZ", [128, TCH, 2048], BF16)
    X = sb("X", [128, TCH * 2048], BF16)
    BT = sb("BT", [128, 4, T], BF16)
    CT = sb("CT", [128, 4, T], BF16)
    Btok = sb("Btok", [128, TCH, 512], BF16)
    S32 = [sb(f"S32_{l}", [128, 2048], F32) for l in range(DEPTH)]
    Sbf = sb("Sbf", [128, 2048], BF16)
    YNT = sb("YNT", [128, 16, T], BF16)
    hist_a = sb("hist_a", [128, DEPTH, 8, 2], F32)
    hist_s = sb("hist_s", [128, DEPTH, 24, 3], F32)
    arena = sb("arena", [128, 14 * SLW], F32)
    dtv = sb("dtv", [128, NB], F32)
    dte_ = sb("dte", [128, NB], F32)
    dt_ = sb("dt", [128, NB], F32)
    dtA = sb("dtA", [128, NB], F32)
    dtdte = sb("dtdte", [128, NB], F32)
    dtA3 = sb("dtA3", [128, 3, TCH, 96], BF16)
    dsp = sb("dsp", [128, 3, NB], BF16)
    dr1 = sb("dr1", [128, NB], F32)
    dr2 = sb("dr2", [128, NB], F32)
    tri_b = sb("tri_b", [128, 128], BF16)
    ones_b = sb("ones_b", [128, 128], BF16)
    negA = sb("negA", [128, NB], F32)
    Ecum = sb("Ecum", [128, NB], F32)
    CDs = sb("CD", [128, NB], F32)
    dtend = sb("dtend", [128, NB], F32)
    A3 = sb("A3", [96, T], BF16)
    A3m = sb("A3m", [96, T], BF16)
    stt = sb("stt", [128, 2, 6], F32)
    mv = sb("mv", [128, 4], F32)
    ss = sb("ss", [128, 8], F32)
    mhalf = sb("mhalf", [128, 8], F32)
    banks = [es.enter_context(nc.psum_tensor(f"bank{i}", [128, 512], F32)) for i in range(8)]

    R = {}

    def res(name):
        if name not in R:
            R[name] = Res(name)
        return R[name]

    bres = [res(f"bank{i}") for i in range(8)]
    for b_ in bres:
        b_.excl = True
    slres = [res(f"wsl{i}") for i in range(NSLOT)]
    ares = [res(f"ar{i}") for i in range(14)]

    def aslot(i, w=SLW):
        return arena[:, i * SLW:i * SLW + w]

    def aslot_bf(i, lo, n):
        v = arena[:, i * SLW:i * SLW + 512].bitcast(BF16)
        return v[:, lo:lo + n]

    bank_pool = {"ids": list(range(8))}

    def newbank():
        ids = bank_pool["ids"]
        i = ids[S.bank_i % len(ids)]
        S.bank_i += 1
        return banks[i], bres[i]

    def cp(off, n):
        return cpar[:, off:off + n]

    cst_res = [res("cpar"), res("tri"), res("ident"), res("esel"), res("ones"), res("abc")]
    S.op("pool", lambda e: e.dma_start(out=cpar[:, :], in_=cpar_d[:, :]), writes=[res("cpar")], dma="cst")
    S.op("pool", lambda e: e.dma_start(out=tri_f[:, :], in_=tri_d[:, :]), writes=[res("tri")], dma="cst")
    S.op("pool", lambda e: e.dma_start(out=ident_b[:, :], in_=ident_d[:, :]), writes=[res("ident")], dma="cst")
    S.op("pool", lambda e: e.dma_start(out=tri_b[:, :], in_=tri_d[:, :]), writes=[res("trib")], dma="cst")
    S.op("pool", lambda e: e.dma_start(out=esel[:, :], in_=esel_d[:, :]), writes=[res("esel")], dma="cst")
    tot = ("cst", S.cnt["cst"])
    for nm in ("cpar", "tri", "ident", "esel", "trib"):
        res(nm).w = tot
    S.op("pool", lambda e: e.memset(ones_f[:, :], 1.0), writes=[res("ones")])
    S.op("pool", lambda e: e.memset(mhalf[:, :], -0.5), writes=[res("mhalf")])
    S.op("pool", lambda e: e.memset(ones_b[:, :], 1.0), writes=[res("onesb")])
    S.op("act", lambda e: e.activation(out=abc[:, :], in_=cp(CP_ALOG, 64), func=AF.Exp),
         reads=[res("cpar")], writes=[res("abc")])
    S.op("dve", lambda e: e.tensor_scalar(out=abc[:, :], in0=abc[:, :], scalar1=-1.0, scalar2=None, op0=ALU.mult),
         reads=[res("abc")], writes=[res("abc")])

    def cast(l, grp, dst, src):
        S.op("pool", lambda e: e.dma_start(out=dst, in_=src), writes=[res(f"cv{l}{grp}")], dma=f"cv{l}{grp}")

    for l in range(n_layers):
        for (grp, c0, c1) in ((0, 0, OFF_XBC), (1, OFF_XBC, NIN)):
            for kc in range(8):
                cast(l, grp, win_b[l, kc * 128:(kc + 1) * 128, c0:c1], win_d[l, kc * 128:(kc + 1) * 128, c0:c1])
        for kc in range(8):
            cast(l, 2, wa_b[l, kc * 128:(kc + 1) * 128, :], wa_d[l, kc * 128:(kc + 1) * 128, :])
        for kc in range(16):
            cast(l, 2, ws_b[l, kc * 128:(kc + 1) * 128, :], ws_d[l, kc * 128:(kc + 1) * 128, :])
        for kc in range(8):
            cast(l, 3, wo_b[l, kc * 128:(kc + 1) * 128, :], wo_d[l, kc * 128:(kc + 1) * 128, :])

    def layer_blocks(l):
        bl = []
        for nm, off in (("u", OFF_U), ("C", OFF_C), ("B", OFF_B), ("z", OFF_ZA)):
            for q in range(2):
                bl.append((nm + str(q), win_b, l, 0, off + q * 512, 512, 0))
        for q in range(6):
            bl.append((f"xbc{q}", win_b, l, 0, OFF_XBC + q * 512, 512, 1))
            if q < 4:
                bl.append((f"zs{q}", win_b, l, 0, OFF_ZS + q * 512, 512, 0))
            elif q == 4:
                bl.append(("dt", win_b, l, 0, OFF_DT, 32, 1))
        for h in range(2):
            bl.append((f"ga{h}", win_b, l, 0, OFF_GA + h * 512, 512, 1))
            bl.append((f"gs{h}", win_b, l, 0, OFF_GS + h * 512, 512, 1))
            bl.append((f"wa{h}", wa_b, l, 0, h * 512, 512, 2))
            bl.append((f"wsa{h}", ws_b, l, 0, h * 512, 512, 2))
            bl.append((f"wsb{h}", ws_b, l, 1024, h * 512, 512, 2))
        for h in range(2):
            bl.append((f"wo{h}", wo_b, l, 0, h * 512, 512, 3))
        return bl

    tiles = [(s, t) for s in range(NSEQ) for t in range(NT)]
    gblocks = []
    for _ in tiles:
        for l in range(n_layers):
            gblocks.extend(layer_blocks(l))
    st = {"next_load": 0, "next_use": 0}

    def prefetch():
        i = st["next_load"]
        if i >= len(gblocks):
            return
        st["next_load"] += 1
        nm, src, l, r0, c0, ncol, grp = gblocks[i]
        s = i % NSLOT
        src_ap = src[l, r0:r0 + 1024, c0:c0 + ncol].rearrange("(kc p) c -> p kc c", p=128)
        dst_ap = wsl[s][:, :, 0:ncol]
        S.op("sp", lambda e: e.dma_start(out=dst_ap, in_=src_ap), reads=[res(f"cv{l}{grp}")],
             writes=[slres[s]], dma=f"wl{s}")

    def use_block(name):
        i = st["next_use"]
        assert gblocks[i][0] == name, (gblocks[i][0], name)
        st["next_use"] += 1
        s = i % NSLOT
        return wsl[s], slres[s]

    def done_block():
        prefetch()

    for _ in range(NSLOT):
        prefetch()

    def mmgroup(out_ap, bres_, pairs, reads, first_start=True, last_inc=True):
        n = len(pairs)
        for i, (lt, rh) in enumerate(pairs):
            S.op("pe", lambda e, lt=lt, rh=rh, i=i: e.matmul(out_ap, lhsT=lt, rhs=rh, start=(first_start and i == 0),
                                                          stop=(i == n - 1)),
                 reads=reads, writes=[bres_], inc=(last_inc and i == n - 1))

    def layernorm(src_fn, src_res, c, lnres, eps, l_next):
        Hc = H[:, c, :]
        Hres = res(f"H{c}")
        for k in range(2):
            S.op("dve", lambda e, k=k: e.bn_stats(out=stt[:, k, :], in_=src_fn(k)), reads=src_res, writes=[res("stt")])
        S.op("dve", lambda e: e.bn_aggr(out=mv[:, 0:2], in_=stt[:, :, :]), reads=[res("stt")], writes=[res("mv")])
        S.op("dve", lambda e: e.tensor_scalar(out=mv[:, 3:4], in0=mv[:, 1:2], scalar1=float(eps), scalar2=None,
                                              op0=ALU.add), reads=[res("mv")], writes=[res("mv3")])
        S.op("pool", lambda e: e.tensor_tensor(out=mv[:, 2:3], in0=mv[:, 3:4], in1=mhalf[:, 0:1], op=ALU.pow),
             reads=[res("mv3"), res("mhalf")], writes=[res("mv2")])
        for k in range(2):
            S.op("dve", lambda e, k=k: e.tensor_scalar(out=Hc[:, k * 512:(k + 1) * 512], in0=src_fn(k),
                                                       scalar1=mv[:, 0:1], scalar2=mv[:, 2:3],
                                                       op0=ALU.subtract, op1=ALU.mult),
                 reads=src_res + [res("mv"), res("mv2")], writes=[Hres])
        S.op("pool", lambda e: e.tensor_tensor(out=Hc, in0=Hc, in1=lnbuf[:, 0:D], op=ALU.mult),
             reads=[Hres, lnres], writes=[Hres])
        S.op("pool", lambda e: e.tensor_tensor(out=Hc, in0=Hc, in1=lnbuf[:, D:2 * D], op=ALU.add),
             reads=[Hres, lnres], writes=[Hres])
        if l_next:
            S.op("act", lambda e: e.activation(out=hb[:, :], in_=Hc, func=AF.Copy), reads=[Hres], writes=[res("hb")])
            bk, br = newbank()
            bv = bk[:, :].bitcast(BF16)
            for kc in range(8):
                S.op("pe", lambda e, kc=kc: e.transpose(out=bv[:, kc * 128:(kc + 1) * 128],
                                                        in_=hb[:, kc * 128:(kc + 1) * 128], identity=ident_b[:, :]),
                     reads=[res("hb"), res("ident")], writes=[br], inc=(kc == 7))
            S.op("act", lambda e: e.activation(out=hT[:, :, c * 128:(c + 1) * 128],
                                               in_=bv.rearrange("p (a b) -> p a b", a=8), func=AF.Copy),
                 reads=[br], writes=[res(f"hT{c}")])

    def load_ln(i):
        src = lnpar_d[i:i + 1, :].partition_broadcast(128)
        S.op("pool", lambda e: e.dma_start(out=lnbuf[:, :], in_=src), writes=[res("lnbuf")], dma="lnl")

    hTres = [res(f"hT{c}") for c in range(TCH)]
    Xres = [res(f"X{c}") for c in range(TCH)]
    mixres = Xres[0:max(1, T // 256)]
    mixedT = X[:, 0:8 * T].rearrange("p (j t) -> p j t", j=8)

    def dumpit(name, ap, rs):
        if dump and name in dump_d and not st.get("dumped_" + name):
            st["dumped_" + name] = True
            S.op("pool", lambda e: e.dma_start(out=dump_d[name], in_=ap), reads=rs, dma="dbg")

    def layer(l, seq_first, seq_last_tile, is_last_layer, row0):
        caw = cp(CP_CAW, 48).rearrange("p (l c k) -> p l c k", l=2, c=8)
        csw = cp(CP_CSW, 192).rearrange("p (l c k) -> p l c k", l=2, c=24)
        csb = cp(CP_CSB, 48).rearrange("p (l c) -> p l c", l=2)
        nsg = cp(CP_NSG, 32).rearrange("p (l c) -> p l c", l=2)
        dtb = cp(CP_DTB, 64)[:, l * 32:(l + 1) * 32]
        dsk = cp(CP_DSK, 64)[:, l * 32:(l + 1) * 32]
        A_l = abc[:, l * 32:(l + 1) * 32]
        cres = res("cpar")

        def inproj_fm(slot, sres, q):
            bk, br = newbank()
            mmgroup(bk[:, 0:T], br, [(slot[:, kc, q * 128:(q + 1) * 128], hT[:, kc, :]) for kc in range(8)],
                    reads=[sres] + hTres)
            return bk, br

        for q2 in range(2):
            slot, sres = use_block(f"u{q2}")
            for q in range(4):
                j = q2 * 4 + q
                bk, br = inproj_fm(slot, sres, q)
                S.op("act", lambda e, bk=bk, j=j: e.activation(out=aslot(j, T), in_=bk[:, 0:T], func=AF.Copy),
                     reads=[br], writes=[ares[j]])
            done_block()
        for q2 in range(2):
            slot, sres = use_block(f"C{q2}")
            for q in range(4):
                j = q2 * 4 + q
                bk, br = inproj_fm(slot, sres, q)
                ci = 8 + (j % 2)
                cu = aslot(ci)
                hres = res(f"hista{l}_{j}")
                S.op("pool", lambda e, cu=cu, j=j: e.tensor_copy(out=cu[:, 0:2], in_=hist_a[:, l, j, :]),
                     reads=[hres], writes=[ares[ci]])
                S.op("dve", lambda e, cu=cu, bk=bk, j=j: e.tensor_tensor(out=cu[:, 2:2 + T], in0=bk[:, 0:T],
                                                                        in1=aslot(j, T), op=ALU.mult),
                     reads=[br, ares[j]], writes=[ares[ci]])
                S.op("pool", lambda e, cu=cu, j=j: e.tensor_copy(out=hist_a[:, l, j, :], in_=cu[:, T:T + 2]),
                     reads=[ares[ci]], writes=[hres])
                S.op("dve", lambda e, cu=cu, j=j: e.tensor_scalar(out=aslot(j, T), in0=cu[:, 0:T],
                                                                  scalar1=caw[:, l, j, 0:1], scalar2=None, op0=ALU.mult),
                     reads=[ares[ci], cres], writes=[ares[j]])
                for k in (1, 2):
                    S.op("dve", lambda e, cu=cu, j=j, k=k: e.scalar_tensor_tensor(
                        out=aslot(j, T), in0=cu[:, k:k + T], scalar=caw[:, l, j, k:k + 1], in1=aslot(j, T),
                        op0=ALU.mult, op1=ALU.add), reads=[ares[ci], ares[j], cres], writes=[ares[j]])
            done_block()
        for q2 in range(2):
            slot, sres = use_block(f"B{q2}")
            for q in range(4):
                j = q2 * 4 + q
                bk, br = inproj_fm(slot, sres, q)
                S.op("dve", lambda e, bk=bk, j=j: e.tensor_tensor(out=aslot(j, T), in0=bk[:, 0:T], in1=aslot(j, T),
                                                                  op=ALU.mult), reads=[br, ares[j]], writes=[ares[j]])
            done_block()
        for q2 in range(2):
            slot, sres = use_block(f"z{q2}")
            for q in range(4):
                j = q2 * 4 + q
                bk, br = inproj_fm(slot, sres, q)
                si = 10 + (j % 2)
                S.op("act", lambda e, bk=bk, si=si: e.activation(out=aslot(si, T), in_=bk[:, 0:T], func=AF.Silu),
                     reads=[br], writes=[ares[si]])
                S.op("dve", lambda e, j=j, si=si: e.tensor_tensor(out=ya_in[:, j, :], in0=aslot(j, T), in1=aslot(si, T),
                                                                  op=ALU.mult),
                     reads=[ares[j], ares[si]], writes=[res(f"ya{j}")])
            done_block()
        dumpit("ya_in", ya_in[:, :, :], [res(f"ya{j}") for j in range(8)])
        if stopped("A"):
            return

        def phaseB_block(q):
            slot, sres = use_block(f"zs{q}")
            for c in range(TCH):
                bk, br = newbank()
                mmgroup(bk[:, :], br, [(hT[:, kc, c * 128:(c + 1) * 128], slot[:, kc, :]) for kc in range(8)],
                        reads=[sres, hTres[c]])
                S.op("act", lambda e, bk=bk, c=c, q=q: e.activation(out=SZ[:, c, q * 512:(q + 1) * 512], in_=bk[:, :],
                                                                    func=AF.Silu), reads=[br], writes=[res(f"SZ{c}")])
            done_block()
        def phaseB_dt():
            slot, sres = use_block("dt")
            bk, br = newbank()
            for c in range(TCH):
                mmgroup(bk[:, c * 32:(c + 1) * 32], br,
                        [(hT[:, kc, c * 128:(c + 1) * 128], slot[:, kc, 0:32]) for kc in range(8)],
                        reads=[sres, hTres[c]], last_inc=(c == TCH - 1))
            done_block()
            dres = res("dtsmall")
            S.op("dve", lambda e, bk=bk: e.tensor_tensor(out=dtv[:, :].rearrange("p (c h) -> p c h", c=TCH),
                                                         in0=bk[:, 0:NB].rearrange("p (c h) -> p c h", c=TCH),
                                                         in1=dtb.unsqueeze(1).broadcast_to([128, TCH, 32]), op=ALU.add),
                 reads=[br, cres], writes=[res("dtv")])
            S.op("act", lambda e: e.activation(out=dtv[:, :], in_=dtv[:, :], func=AF.Exp), reads=[res("dtv")],
                 writes=[res("dtv")])
            S.op("act", lambda e: e.activation(out=dt_[:, :], in_=dtv[:, :], func=AF.Ln, bias=1.0), reads=[res("dtv")],
                 writes=[res("dt")])
            S.op("dve", lambda e: e.tensor_tensor(out=dtA[:, :].rearrange("p (c h) -> p c h", c=TCH),
                                                  in0=dt_[:, :].rearrange("p (c h) -> p c h", c=TCH),
                                                  in1=A_l.unsqueeze(1).broadcast_to([128, TCH, 32]), op=ALU.mult),
                 reads=[res("dt"), res("abc")], writes=[res("dtA")])
            S.op("dve", lambda e: e.tensor_copy(out=dsp[:, 0, :], in_=dtA[:, :]), reads=[res("dtA")], writes=[res("dsp0")])
            S.op("dve", lambda e: e.tensor_tensor(out=dr1[:, :], in0=dtA[:, :], in1=dsp[:, 0, :], op=ALU.subtract),
                 reads=[res("dtA"), res("dsp0")], writes=[res("dr1")])
            S.op("dve", lambda e: e.tensor_copy(out=dsp[:, 1, :], in_=dr1[:, :]), reads=[res("dr1")], writes=[res("dsp1")])
            S.op("dve", lambda e: e.tensor_tensor(out=dr2[:, :], in0=dr1[:, :], in1=dsp[:, 1, :], op=ALU.subtract),
                 reads=[res("dr1"), res("dsp1")], writes=[res("dr2")])
            S.op("dve", lambda e: e.tensor_copy(out=dsp[:, 2, :], in_=dr2[:, :]), reads=[res("dr2")], writes=[res("dsp2")])
            for i3 in range(3):
                S.op("dve", lambda e, i3=i3: e.tensor_copy(
                    out=dtA3[:, i3, :, :].rearrange("p c (r h) -> p c r h", r=3),
                    in_=dsp[:, i3, :].rearrange("p (c h) -> p c h", c=TCH).unsqueeze(2).broadcast_to([128, TCH, 3, 32])),
                     reads=[res(f"dsp{i3}")], writes=[res("dtA3")])
            dspres = [res("dsp0"), res("dsp1"), res("dsp2")]
            bk1, br1 = newbank()
            for c in range(TCH):
                for i3 in range(3):
                    S.op("pe", lambda e, c=c, i3=i3, bk1=bk1: e.matmul(bk1[:, c * 32:(c + 1) * 32], lhsT=tri_b[:, :],
                                                                      rhs=dsp[:, i3, c * 32:(c + 1) * 32], start=(i3 == 0),
                                                                      stop=(i3 == 2)),
                         reads=[res("trib")] + dspres, writes=[br1], inc=False)
            for c in range(TCH):
                for i3 in range(3):
                    S.op("pe", lambda e, c=c, i3=i3, bk1=bk1: e.matmul(bk1[:, 256 + c * 32:256 + (c + 1) * 32],
                                                                      lhsT=ones_b[:, :], rhs=dsp[:, i3, c * 32:(c + 1) * 32],
                                                                      start=(i3 == 0), stop=(i3 == 2)),
                         reads=[res("onesb")] + dspres, writes=[br1], inc=(c == TCH - 1 and i3 == 2))
            bk2, br2 = newbank()
            for c in range(TCH):
                for i3 in range(3):
                    S.op("pe", lambda e, c=c, i3=i3, bk2=bk2: e.matmul(bk2[0:96, c * 128:(c + 1) * 128],
                                                                      lhsT=dtA3[:, i3, c, :], rhs=tri_b[:, :],
                                                                      start=(i3 == 0), stop=(i3 == 2)),
                         reads=[res("trib"), res("dtA3")], writes=[br2], inc=(c == TCH - 1 and i3 == 2))
            S.op("dve", lambda e, bk1=bk1: e.tensor_scalar(out=negA[:, :], in0=bk1[:, 0:NB], scalar1=-1.0, scalar2=None,
                                                           op0=ALU.mult), reads=[br1], writes=[res("negA")])
            S.op("act", lambda e, bk1=bk1: e.activation(out=Ecum[:, :], in_=bk1[:, 0:NB], func=AF.Exp), reads=[br1],
                 writes=[res("Ecum")])
            S.op("act", lambda e, bk1=bk1: e.activation(out=CDs[:, :], in_=bk1[:, 256:256 + NB], func=AF.Exp), reads=[br1],
                 writes=[res("CD")])
            S.op("dve", lambda e, bk1=bk1: e.tensor_tensor(out=dtend[:, :], in0=bk1[:, 256:256 + NB], in1=negA[:, :],
                                                           op=ALU.add), reads=[br1, res("negA")], writes=[res("dtend")])
            S.op("act", lambda e: e.activation(out=dte_[:, :], in_=dtend[:, :], func=AF.Exp), reads=[res("dtend")],
                 writes=[res("dte")])
            S.op("dve", lambda e: e.tensor_tensor(out=dtdte[:, :], in0=dt_[:, :], in1=dte_[:, :], op=ALU.mult),
                 reads=[res("dt"), res("dte")], writes=[res("dtdte")])
            dumpit("negA", negA[:, :], [res("negA")])
            S.op("act", lambda e, bk2=bk2: e.activation(out=A3[0:96, :], in_=bk2[0:96, 0:T], func=AF.Copy), reads=[br2],
                 writes=[res("A3")])
            r1 = arena[0:96, 12 * SLW:12 * SLW + T]
            r2 = arena[0:96, 13 * SLW:13 * SLW + T]
            S.op("dve", lambda e, bk2=bk2: e.tensor_tensor(out=r1[0:96, :], in0=bk2[0:96, 0:T], in1=A3[0:96, :],
                                                           op=ALU.subtract), reads=[br2, res("A3")], writes=[ares[12]])
            S.op("dve", lambda e: e.tensor_copy(out=A3m[0:96, :], in_=r1[0:96, :]), reads=[ares[12]], writes=[res("A3m")])
            S.op("dve", lambda e: e.tensor_tensor(out=r2[0:96, :], in0=r1[0:96, :], in1=A3m[0:96, :], op=ALU.subtract),
                 reads=[ares[12], res("A3m")], writes=[ares[13]])
            S.op("dve", lambda e: e.tensor_copy(out=A3[64:96, :], in_=r2[64:96, :]), reads=[ares[13]], writes=[res("A3")])
            S.op("dve", lambda e: e.tensor_copy(out=A3[32:64, :], in_=A3m[32:64, :]), reads=[res("A3m")], writes=[res("A3")])
            dumpit("dt", dt_[:, :], [res("dt")])
            dumpit("negA", negA[:, :], [res("negA")])

        pend = []

        def phaseC_block(q2):
            slot, sres = use_block(f"xbc{q2}")
            for q in range(4):
                j = q2 * 4 + q
                bk, br = inproj_fm(slot, sres, q)
                while len(pend) > 2:
                    pend.pop(0)()
                ri = j % 2
                raw = aslot(ri)
                hres = res(f"hists{l}_{j}")
                S.op("pool", lambda e, raw=raw, j=j: e.tensor_copy(out=raw[:, 0:3], in_=hist_s[:, l, j, :]),
                     reads=[hres], writes=[ares[ri]])
                S.op("act", lambda e, raw=raw, bk=bk: e.activation(out=raw[:, 3:3 + T], in_=bk[:, 0:T], func=AF.Copy),
                     reads=[br], writes=[ares[ri]])
                S.op("pool", lambda e, raw=raw, j=j: e.tensor_copy(out=hist_s[:, l, j, :], in_=raw[:, T:T + 3]),
                     reads=[ares[ri]], writes=[hres])
                ai = 2 + ri
                acc = aslot(ai, T)
                S.op("act", lambda e, bk=bk, acc=acc, j=j: e.activation(
                    out=acc, in_=bk[:, 0:T], func=AF.Identity, scale=csw[:, l, j, 3:4], bias=csb[:, l, j:j + 1]),
                     reads=[br, cres], writes=[ares[ai]])
                for k in (0, 1, 2):
                    S.op("dve", lambda e, raw=raw, acc=acc, j=j, k=k: e.scalar_tensor_tensor(
                        out=acc, in0=raw[:, k:k + T], scalar=csw[:, l, j, k:k + 1], in1=acc,
                        op0=ALU.mult, op1=ALU.add), reads=[ares[ri], ares[ai], cres], writes=[ares[ai]])
                if j < 16:
                    xi = 4 + (j % 4)
                    xc = aslot_bf(xi, 0, T)
                    S.op("act", lambda e, acc=acc, xc=xc: e.activation(out=xc, in_=acc, func=AF.Silu),
                         reads=[ares[ai]], writes=[ares[xi]])

                    def tr_x(j=j, xi=xi, xc=xc):
                        tbk, tbr = newbank()
                        tv = tbk[:, :].bitcast(BF16)
                        for c in range(TCH):
                            S.op("pe", lambda e, c=c: e.transpose(
                                out=tv[:, c * 128:(c + 1) * 128], in_=xc[:, c * 128:(c + 1) * 128],
                                identity=ident_b[:, :]), reads=[ares[xi], res("ident")], writes=[tbr])
                        Xv = X[:, :].rearrange("p (c f) -> p c f", c=TCH)
                        S.op("dve", lambda e: e.tensor_copy(
                            out=Xv[:, :, j * 128:(j + 1) * 128], in_=tv[:, 0:T].rearrange("p (c f) -> p c f", c=TCH)),
                             reads=[tbr], writes=Xres)
                    pend.append(tr_x)
                elif j < 20:
                    g = j - 16
                    S.op("act", lambda e, acc=acc, g=g: e.activation(out=BT[:, g, :], in_=acc, func=AF.Silu),
                         reads=[ares[ai]], writes=[res(f"BT{g}")])

                    def tr_b(g=g):
                        tbk, tbr = newbank()
                        tv = tbk[:, :].bitcast(BF16)
                        for c in range(TCH):
                            S.op("pe", lambda e, c=c: e.transpose(
                                out=tv[:, c * 128:(c + 1) * 128], in_=BT[:, g, c * 128:(c + 1) * 128],
                                identity=ident_b[:, :]), reads=[res(f"BT{g}"), res("ident")], writes=[tbr])
                        S.op("dve", lambda e: e.tensor_copy(
                            out=Btok[:, :, g * 128:(g + 1) * 128], in_=tv[:, 0:T].rearrange("p (c f) -> p c f", c=TCH)),
                             reads=[tbr], writes=[res("Btok")])
                    pend.append(tr_b)
                else:
                    g = j - 20
                    S.op("act", lambda e, acc=acc, g=g: e.activation(out=CT[:, g, :], in_=acc, func=AF.Silu),
                         reads=[ares[ai]], writes=[res(f"CT{g}")])
            done_block()
        for q2 in range(6):
            phaseC_block(q2)
            if q2 < 4:
                phaseB_block(q2)
            elif q2 == 4:
                phaseB_dt()
        while pend:
            pend.pop(0)()
        dumpit("X", X[:, :], Xres)
        dumpit("BT", BT[:, :, :], [res(f"BT{g}") for g in range(4)])
        dumpit("CT", CT[:, :, :], [res(f"CT{g}") for g in range(4)])
        if stopped("C"):
            return

        Sl = S32[l]
        Sres = [res(f"S32_{l}_{g}") for g in range(4)]
        sbres = [res(f"Sbf{g}") for g in range(4)]
        for g in range(4):
            S.op("pool", lambda e, g=g: e.tensor_copy(out=Sbf[:, g * 512:(g + 1) * 512], in_=Sl[:, g * 512:(g + 1) * 512]),
                 reads=[Sres[g]], writes=[sbres[g]])
        ctx = {}

        def chunk_prep(c):
            cs = slice(c * 128, (c + 1) * 128)
            cbk, cbr = newbank()
            for g in range(4):
                S.op("pe", lambda e, g=g, cbk=cbk, cs=cs: e.matmul(cbk[:, g * 128:(g + 1) * 128], lhsT=BT[:, g, cs],
                                                                   rhs=CT[:, g, cs], start=True, stop=True),
                     reads=[res(f"BT{g}"), res(f"CT{g}")], writes=[cbr])
            cbi = c % 2
            CBm = aslot(cbi, 512)
            S.op("dve", lambda e, cbk=cbk, CBm=CBm: e.tensor_tensor(
                out=CBm.rearrange("p (g l) -> p g l", g=4), in0=cbk[:, :].rearrange("p (g l) -> p g l", g=4),
                in1=tri_f[:, :].unsqueeze(1).broadcast_to([128, 4, 128]), op=ALU.mult),
                 reads=[cbr, res("tri")], writes=[ares[cbi]])

        def stage1(c, g):
            cs = slice(c * 128, (c + 1) * 128)
            last_chunk = seq_last_tile and c == TCH - 1
            cbi = c % 2
            CBm = aslot(cbi, 512)
            gi = g % 2
            hs = slice(c * 32 + g * 8, c * 32 + (g + 1) * 8)
            Xg = X[:, c * 2048 + g * 512:c * 2048 + (g + 1) * 512].rearrange("p (h d) -> p h d", h=8)
            XDg = aslot_bf(10 + gi, 0, 512)
            XSg = aslot_bf(10 + gi, 512, 512)
            XDDg = aslot_bf(12 + gi, 0, 512)
            r_xd, r_xdd = ares[10 + gi], ares[12 + gi]
            S.op("pool", lambda e: e.tensor_tensor(
                out=XDg.rearrange("p (h d) -> p h d", h=8), in0=Xg,
                in1=dt_[:, hs].unsqueeze(2).broadcast_to([128, 8, 64]), op=ALU.mult),
                 reads=[Xres[c], res("dt")], writes=[r_xd])
            S.op("pool", lambda e: e.tensor_tensor(
                out=XSg.rearrange("p (h d) -> p h d", h=8), in0=Xg,
                in1=dsk[:, g * 8:(g + 1) * 8].unsqueeze(2).broadcast_to([128, 8, 64]), op=ALU.mult),
                 reads=[Xres[c], cres], writes=[r_xd])
            MT = aslot_bf(4 + gi, 0, 1024)
            for half in range(2):
                lbk, lbr = newbank()
                for hh in range(4):
                    h = g * 8 + half * 4 + hh
                    S.op("pe", lambda e, lbk=lbk, hh=hh, h=h: e.matmul(
                        lbk[:, hh * 128:(hh + 1) * 128], lhsT=esel[:, h * 128:(h + 1) * 128], rhs=A3[:, cs],
                        start=True, stop=True), reads=[res("esel"), res("A3")], writes=[lbr])
                li = 2 + half
                LT = aslot(li, 512)
                for hh in range(4):
                    h = g * 8 + half * 4 + hh
                    S.op("act", lambda e, lbk=lbk, LT=LT, hh=hh, h=h: e.activation(
                        out=LT[:, hh * 128:(hh + 1) * 128], in_=lbk[:, hh * 128:(hh + 1) * 128], func=AF.Exp,
                        bias=negA[:, c * 32 + h:c * 32 + h + 1]), reads=[lbr, res("negA")], writes=[ares[li]])
                S.op("dve", lambda e, LT=LT, half=half: e.scalar_tensor_tensor(
                    out=MT[:, half * 512:(half + 1) * 512].rearrange("p (h l) -> p h l", h=4),
                    in0=LT.rearrange("p (h l) -> p h l", h=4), scalar=1.0,
                    in1=CBm[:, g * 128:(g + 1) * 128].unsqueeze(1).broadcast_to([128, 4, 128]),
                    op0=ALU.min, op1=ALU.mult), reads=[ares[li], ares[cbi]], writes=[ares[4 + gi]])
        def stage1b(c, g):
            cs = slice(c * 128, (c + 1) * 128)
            gi = g % 2
            XDg = aslot_bf(10 + gi, 0, 512)
            XSg = aslot_bf(10 + gi, 512, 512)
            MT = aslot_bf(4 + gi, 0, 1024)
            r_xd = ares[10 + gi]
            ybk, ybr = banks[gi * 2], bres[gi * 2]
            S.op("pe", lambda e: e.matmul(ybk[:, :], lhsT=ident_b[:, :], rhs=XSg, start=True,
                                          stop=False, skip_group_check=True),
                 reads=[res("ident"), r_xd], writes=[ybr])
            for hh in range(8):
                S.op("pe", lambda e, hh=hh: e.matmul(
                    ybk[:, hh * 64:(hh + 1) * 64], lhsT=MT[:, hh * 128:(hh + 1) * 128],
                    rhs=XDg[:, hh * 64:(hh + 1) * 64], start=False, stop=True, skip_group_check=True),
                     reads=[ares[4 + gi], r_xd], writes=[ybr])
            obk, obr = banks[gi * 2 + 1], bres[gi * 2 + 1]
            S.op("pe", lambda e: e.matmul(obk[:, :], lhsT=CT[:, g, cs],
                                          rhs=Sbf[:, g * 512:(g + 1) * 512], start=True, stop=True),
                 reads=[res(f"CT{g}"), sbres[g]], writes=[obr])
            ctx[(c, g)] = (ybk, ybr, obk, obr)

        def stage2(c, g):
            cs = slice(c * 128, (c + 1) * 128)
            last_chunk = seq_last_tile and c == TCH - 1
            gi = g % 2
            hs = slice(c * 32 + g * 8, c * 32 + (g + 1) * 8)
            ybk, ybr, obk, obr = ctx.pop((c, g))
            yn = aslot_bf(12 + gi, 512, 512)
            r_yn = res(f"yn{gi}")
            t1 = aslot(6 + gi, 512)
            t3 = aslot(8 + gi, 512)
            S.op("dve", lambda e: e.tensor_tensor(
                out=t1.rearrange("p (h d) -> p h d", h=8), in0=obk[:, :].rearrange("p (h d) -> p h d", h=8),
                in1=Ecum[:, hs].unsqueeze(2).broadcast_to([128, 8, 64]), op=ALU.mult),
                 reads=[obr, res("Ecum")], writes=[ares[6 + gi]])
            S.op("dve", lambda e: e.tensor_tensor(out=t1, in0=ybk[:, :], in1=t1, op=ALU.add),
                 reads=[ybr, ares[6 + gi]], writes=[ares[6 + gi]])
            if c == 0 and g == 0:
                dumpit("ypre", t1, [ares[6 + gi]])
            S.op("pool", lambda e: e.tensor_tensor(
                out=t3, in0=t1, in1=SZ[:, c, g * 512:(g + 1) * 512], op=ALU.mult),
                 reads=[ares[6 + gi], res(f"SZ{c}")], writes=[ares[8 + gi]])
            ssr = res(f"ss{gi}")
            S.op("pool", lambda e: e.memset(ss[:, gi:gi + 1], 0.0), writes=[ssr])
            S.op("act", lambda e: e.activation(out=yn, in_=t3, func=AF.Square, scale=float(512.0 ** -0.5),
                                               accum_out=ss[:, gi:gi + 1]),
                 reads=[ares[8 + gi], ssr], writes=[r_yn, ssr])
            S.op("dve", lambda e: e.tensor_scalar(out=ss[:, 2 + gi:3 + gi], in0=ss[:, gi:gi + 1],
                                                  scalar1=RMS_EPS, scalar2=None, op0=ALU.add),
                 reads=[ssr], writes=[res(f"rt{gi}")])
            S.op("pool", lambda e: e.tensor_tensor(out=ss[:, 4 + gi:5 + gi], in0=ss[:, 2 + gi:3 + gi],
                                                   in1=mhalf[:, 0:1], op=ALU.pow),
                 reads=[res(f"rt{gi}"), res("mhalf")], writes=[res(f"rs{gi}")])
            S.op("act", lambda e: e.activation(out=yn, in_=t3, func=AF.Copy, scale=ss[:, 4 + gi:5 + gi]),
                 reads=[ares[8 + gi], res(f"rs{gi}")], writes=[r_yn])
        def stage2b(c, g):
            cs = slice(c * 128, (c + 1) * 128)
            last_chunk = seq_last_tile and c == TCH - 1
            gi = g % 2
            hs = slice(c * 32 + g * 8, c * 32 + (g + 1) * 8)
            Xg = X[:, c * 2048 + g * 512:c * 2048 + (g + 1) * 512].rearrange("p (h d) -> p h d", h=8)
            XDDg = aslot_bf(12 + gi, 0, 512)
            yn = aslot_bf(12 + gi, 512, 512)
            r_xdd = ares[12 + gi]
            r_yn = res(f"yn{gi}")
            tbk, tbr = newbank()
            tv = tbk[:, :].bitcast(BF16)
            for k in range(4):
                S.op("pe", lambda e, k=k: e.transpose(
                    out=tv[:, k * 128:(k + 1) * 128], in_=yn[:, k * 128:(k + 1) * 128], identity=ident_b[:, :]),
                     reads=[r_yn, res("ident")], writes=[tbr])
            S.op("dve", lambda e: e.tensor_tensor(
                out=YNT[:, g * 4:(g + 1) * 4, cs], in0=tv[:, 0:512].rearrange("p (k t) -> p k t", k=4),
                in1=nsg[:, l, g * 4:(g + 1) * 4].unsqueeze(2).broadcast_to([128, 4, 128]), op=ALU.mult),
                 reads=[tbr, cres], writes=[res(f"YNT{g}")])
            if not last_chunk:
                S.op("pool", lambda e: e.tensor_tensor(
                    out=XDDg.rearrange("p (h d) -> p h d", h=8), in0=Xg,
                    in1=dtdte[:, hs].unsqueeze(2).broadcast_to([128, 8, 64]), op=ALU.mult),
                     reads=[Xres[c], res("dtdte")], writes=[r_xdd])
                sbk, sbr_ = newbank()
                S.op("pe", lambda e: e.matmul(
                    sbk[:, :], lhsT=Btok[:, c, g * 128:(g + 1) * 128], rhs=XDDg, start=True, stop=True),
                     reads=[res("Btok"), r_xdd], writes=[sbr_])
                Sg = Sl[:, g * 512:(g + 1) * 512]
                S.op("pool", lambda e: e.tensor_tensor(
                    out=Sg.rearrange("p (h d) -> p h d", h=8), in0=Sg.rearrange("p (h d) -> p h d", h=8),
                    in1=CDs[:, hs].unsqueeze(2).broadcast_to([128, 8, 64]), op=ALU.mult),
                     reads=[Sres[g], res("CD")], writes=[Sres[g]])
                S.op("dve", lambda e: e.tensor_tensor(out=Sg, in0=sbk[:, :], in1=Sg, op=ALU.add),
                     reads=[sbr_, Sres[g]], writes=[Sres[g]])
                S.op("pool", lambda e: e.tensor_copy(out=Sbf[:, g * 512:(g + 1) * 512], in_=Sg),
                     reads=[Sres[g]], writes=[sbres[g]])

        seq_cg = [(c, g) for c in range(TCH) for g in range(4)]
        bank_pool["ids"] = [4, 5, 6, 7]
        nst = len(seq_cg)
        for i in range(nst + 3):
            if i < nst:
                if seq_cg[i][1] == 0:
                    chunk_prep(seq_cg[i][0])
                stage1(*seq_cg[i])
            if 0 <= i - 1 < nst:
                stage1b(*seq_cg[i - 1])
            if 0 <= i - 2 < nst:
                stage2(*seq_cg[i - 2])
            if 0 <= i - 3 < nst:
                stage2b(*seq_cg[i - 3])
        bank_pool["ids"] = list(range(8))
        dumpit("YNT", YNT[:, :, :], [res(f"YNT{g}") for g in range(4)])
        if stopped("D"):
            return

        for half in range(2):
            slot, sres = use_block(f"ga{half}")
            for q in range(4):
                bk, br = inproj_fm(slot, sres, q)
                S.op("act", lambda e, bk=bk, q=q: e.activation(out=aslot(q, T), in_=bk[:, 0:T], func=AF.Tanh, scale=0.5),
                     reads=[br], writes=[ares[q]])
            done_block()
            slot, sres = use_block(f"gs{half}")
            for q in range(4):
                bk, br = inproj_fm(slot, sres, q)
                S.op("act", lambda e, bk=bk, q=q: e.activation(out=aslot(4 + q, T), in_=bk[:, 0:T], func=AF.Tanh,
                                                               scale=0.5), reads=[br], writes=[ares[4 + q]])
            done_block()
            slot, sres = use_block(f"wa{half}")
            for q in range(4):
                bk, br = newbank()
                mmgroup(bk[:, 0:T], br, [(slot[:, kc, q * 128:(q + 1) * 128], ya_in[:, kc, :]) for kc in range(8)],
                        reads=[sres] + [res(f"ya{j}") for j in range(8)])
                S.op("dve", lambda e, bk=bk, q=q: e.scalar_tensor_tensor(
                    out=aslot(q, T), in0=aslot(q, T), scalar=1.0, in1=bk[:, 0:T], op0=ALU.add, op1=ALU.mult),
                     reads=[br, ares[q]], writes=[ares[q]])
            done_block()
            slota, sresa = use_block(f"wsa{half}")
            slotb, sresb = use_block(f"wsb{half}")
            for q in range(4):
                bk, br = newbank()
                pairs = [(slota[:, kc, q * 128:(q + 1) * 128], YNT[:, kc, :]) for kc in range(8)] + \
                        [(slotb[:, kc, q * 128:(q + 1) * 128], YNT[:, 8 + kc, :]) for kc in range(8)]
                mmgroup(bk[:, 0:T], br, pairs, reads=[sresa, sresb] + [res(f"YNT{g}") for g in range(4)])
                S.op("dve", lambda e, bk=bk, q=q: e.scalar_tensor_tensor(
                    out=aslot(4 + q, T), in0=aslot(4 + q, T), scalar=1.0, in1=bk[:, 0:T], op0=ALU.add, op1=ALU.mult),
                     reads=[br, ares[4 + q]], writes=[ares[4 + q]])
                j = half * 4 + q
                S.op("pool", lambda e, q=q, j=j: e.tensor_tensor(out=mixedT[:, j, :], in0=aslot(q, T),
                                                                 in1=aslot(4 + q, T), op=ALU.add),
                     reads=[ares[q], ares[4 + q]], writes=mixres)
            done_block()
            done_block()
        dumpit("mixed", mixedT, mixres)
        slot0, sres0 = use_block("wo0")
        slot1, sres1 = use_block("wo1")
        lnres = res("lnbuf")
        for c in range(TCH):
            bks = []
            for half, (slot, sres) in enumerate(((slot0, sres0), (slot1, sres1))):
                bk, br = newbank()
                mmgroup(bk[:, :], br, [(mixedT[:, kc, c * 128:(c + 1) * 128], slot[:, kc, :]) for kc in range(8)],
                        reads=[sres] + mixres)
                bks.append((bk, br))
            Hres = res(f"H{c}")
            for half, (bk, br) in enumerate(bks):
                S.op("dve", lambda e, bk=bk, half=half, c=c: e.scalar_tensor_tensor(
                    out=aslot(8 + half, 512), in0=H[:, c, half * 512:(half + 1) * 512], scalar=2.0 * ALPHA,
                    in1=bk[:, :], op0=ALU.mult, op1=ALU.add), reads=[br, Hres], writes=[ares[8 + half]])
            layernorm(lambda k: aslot(8 + k, 512), [ares[8], ares[9]], c, lnres, 4.0 * LN_EPS,
                      l_next=not is_last_layer)
            if is_last_layer:
                S.op("pool", lambda e, c=c: e.dma_start(out=out_d[row0 + c * 128:row0 + (c + 1) * 128, :], in_=H[:, c, :]),
                     reads=[Hres], writes=[res(f"out{c}")], dma=f"st{c}")
            else:
                if c == 0:
                    dumpit("h1", H[:, 0, :], [Hres])
        done_block()
        done_block()

    def load_x(ti, c):
        s_, t_ = tiles[ti]
        row = (s_ * NT + t_) * T + c * 128
        i = (ti * TCH + c) % 2
        S.op("pool", lambda e: e.dma_start(out=xst[i][:, :], in_=x_d[row:row + 128, :]), writes=[res(f"xst{i}")],
             dma=f"xl{i}")
        return i

    for ti, (s_, t_) in enumerate(tiles):
        cur["ti"] = ti
        row0 = (s_ * NT + t_) * T
        if t_ == 0:
            for l in range(n_layers):
                S.op("pool", lambda e, l=l: e.memset(S32[l][:, :], 0.0), writes=[res(f"S32_{l}_{g}") for g in range(4)])
                S.op("pool", lambda e, l=l: e.memset(hist_a[:, l, :, :], 0.0),
                     writes=[res(f"hista{l}_{j}") for j in range(8)])
                S.op("pool", lambda e, l=l: e.memset(hist_s[:, l, :, :], 0.0),
                     writes=[res(f"hists{l}_{j}") for j in range(24)])
        load_ln(0)
        for c in range(TCH):
            i = load_x(ti, c)
            layernorm(lambda k, i=i: xst[i][:, k * 512:(k + 1) * 512], [res(f"xst{i}")], c, res("lnbuf"), LN_EPS,
                      l_next=True)
        if ti == 0:
            dumpit("h0", H[:, 0, :], [res("H0")])
        if stopped("ln"):
            break
        for l in range(n_layers):
            load_ln(1 + l)
            cur["l"] = l
            layer(l, t_ == 0, t_ == NT - 1, l == n_layers - 1, row0)
            cur["l"] = 0
            if stop_ph is not None and ti == stop_ti and stop_ph != "L0" and l == stop_l:
                break
            if stopped("L0"):
                break
        if stop_ph is not None and ti == stop_ti:
            break

    S.wait_all("pool", [(k, v) for k, v in S.cnt.items() if k.startswith("st") or k == "dbg"])

    sems = {k: es.enter_context(nc.semaphore(k)) for k in S.cnt.keys()}
    for e in ("pe", "act", "dve", "pool"):
        if e not in sems:
            sems[e] = es.enter_context(nc.semaphore(e))

    S.finalize()

    def replay(name, eng):
        for waits, fn, sk, step in S.final[name]:
            for (wsk, v) in waits:
                eng.wait_ge(sems[wsk], v)
            if fn is None:
                continue
            ins = fn(eng)
            if step:
                ins.then_inc(sems[sk], step)

    with nc.Block() as block:
        @block.tensor
        def _(pe):
            replay("pe", pe)

        @block.scalar
        def _(a):
            replay("act", a)

        @block.vector
        def _(v):
            replay("dve", v)

        @block.gpsimd
        def _(g):
            replay("pool", g)

        @block.sync
        def _(s):
            replay("sp", s)

    es.close()
    return nc, S


def host_consts(inp):
    f = np.float32
    cpar = np.zeros((128, CP_N), f)
    p = np.arange(128)
    caw = np.asarray(inp["conv_a_w"], f)
    csw = np.asarray(inp["conv_s_w"], f)
    csb = np.asarray(inp["conv_s_b"], f)
    nsg = np.asarray(inp["norm_s_g"], f)
    cpar[:, CP_CAW:CP_CAW + 48] = caw.reshape(2, 3, 8, 128).transpose(3, 0, 2, 1).reshape(128, 48)
    cpar[:, CP_CSW:CP_CSW + 192] = csw.reshape(2, 4, 24, 128).transpose(3, 0, 2, 1).reshape(128, 192)
    cpar[:, CP_CSB:CP_CSB + 48] = csb.reshape(2, 24, 128).transpose(2, 0, 1).reshape(128, 48)
    cpar[:, CP_NSG:CP_NSG + 32] = nsg.reshape(2, 16, 128).transpose(2, 0, 1).reshape(128, 32)
    cpar[:, CP_DTB:CP_DTB + 64] = np.broadcast_to(np.asarray(inp["dt_bias"], f).reshape(1, 64), (128, 64))
    cpar[:, CP_ALOG:CP_ALOG + 64] = np.broadcast_to(np.asarray(inp["a_log"], f).reshape(1, 64), (128, 64))
    cpar[:, CP_DSK:CP_DSK + 64] = np.broadcast_to(np.asarray(inp["d_skip"], f).reshape(1, 64), (128, 64))
    lnpar = np.zeros((3, 2 * D), f)
    lnpar[0, :D] = inp["ln_in_g"]
    lnpar[0, D:] = inp["ln_in_b"]
    for l in range(DEPTH):
        lnpar[1 + l, :D] = inp["ln_g"][l]
        lnpar[1 + l, D:] = inp["ln_b"][l]
    tri = (p[:, None] <= p[None, :]).astype(f)
    ident = np.eye(128, dtype=f)
    k = np.arange(96)
    esel = np.zeros((96, 32, 128), f)
    esel[k, k % 32, :] = 1.0
    return {"cpar": cpar, "lnpar": lnpar, "tri": tri, "ident": ident, "esel": esel.reshape(96, 32 * 128)}


_CACHE = {}


def kernel(**inputs):
    x = np.ascontiguousarray(np.asarray(inputs["x"], np.float32))
    nseq = BATCH // NCORES
    TCH = 4
    key = ("full", nseq, TCH)
    if key not in _CACHE:
        _CACHE[key] = build(NSEQ=nseq, NT=SEQ // (TCH * 128), TCH=TCH)[0]
    nc = _CACHE[key]
    consts = host_consts(inputs)
    shared = {
        "w_in": np.ascontiguousarray(np.asarray(inputs["w_in"], np.float32)),
        "w_a_out": np.ascontiguousarray(np.asarray(inputs["w_a_out"], np.float32)),
        "w_s_out": np.ascontiguousarray(np.asarray(inputs["w_s_out"], np.float32)),
        "w_o": np.ascontiguousarray(np.asarray(inputs["w_o"], np.float32)),
    }
    shared.update(consts)
    in_maps = []
    for i in range(NCORES):
        m = dict(shared)
        m["x"] = x[i * nseq:(i + 1) * nseq].reshape(nseq * SEQ, D)
        in_maps.append(m)
    res = run_bass_kernel_spmd(nc, in_maps, core_ids=list(range(NCORES)))
    outs = [np.asarray(r["out"], np.float32).reshape(nseq, SEQ, D) for r in res.results]
    return np.concatenate(outs, axis=0)
```

```python
import numpy as np
from contextlib import ExitStack
import concourse.bass as bass
import concourse.mybir as mybir
from concourse.bass_utils import run_bass_kernel_spmd

F32 = mybir.dt.float32
BF16 = mybir.dt.bfloat16
AF = mybir.ActivationFunctionType
ALU = mybir.AluOpType

D = 1024
DEPTH = 2
SEQ = 2048
BATCH = 32
NCORES = 8
NIN = 11296
OFF_U, OFF_B, OFF_C, OFF_ZA, OFF_ZS, OFF_XBC, OFF_DT, OFF_GA, OFF_GS = (
    0, 1024, 2048, 3072, 4096, 6144, 9216, 9248, 10272)
ALPHA = float((2 * DEPTH) ** 0.25)
LN_EPS = 1e-5
RMS_EPS = 1e-5
NSLOT = 3
SLW = 516

CP_CAW, CP_CSW, CP_CSB, CP_NSG, CP_DTB, CP_ALOG, CP_DSK, CP_N = 0, 48, 240, 288, 320, 384, 448, 512


class Res:
    __slots__ = ("name", "w", "r", "excl")

    def __init__(self, name, excl=False):
        self.name = name
        self.w = None
        self.r = {}
        self.excl = excl


class Sched:
    def __init__(self):
        self.engs = ("pe", "act", "dve", "pool", "sp")
        self.prog = {e: [] for e in self.engs}
        self.cnt = {}
        self.waited = {e: {} for e in self.engs}
        self.bank_i = 0
        self.dma_keys = set()

    def op(self, eng, fn, reads=(), writes=(), dma=None, inc=True):
        deps = {}

        def add(tok, raw):
            if tok is None:
                return
            sk, v = tok
            if sk == eng and dma is None:
                if eng == "pe" or not raw:
                    return
            if deps.get(sk, 0) < v:
                deps[sk] = v

        for r in reads:
            add(r.w, True)
            if r.excl:
                for sk, v in r.r.items():
                    add((sk, v), False)
        for w in writes:
            add(w.w, False)
            for sk, v in w.r.items():
                add((sk, v), False)
        wd = self.waited[eng]
        waits = []
        for sk, v in deps.items():
            if wd.get(sk, 0) < v:
                wd[sk] = v
                waits.append((sk, v))
        if dma is None:
            sk, step = eng, 1
        else:
            sk, step = dma, 16
            self.dma_keys.add(dma)
        n = self.cnt.get(sk, 0) + step
        self.cnt[sk] = n
        tok = (sk, n)
        self.prog[eng].append([waits, fn, sk, n])
        for r in reads:
            if r.r.get(sk, 0) < n:
                r.r[sk] = n
        for w in writes:
            w.w = tok
            w.r = {}
        return tok

    def wait_all(self, eng, toks):
        wd = self.waited[eng]
        waits = []
        for sk, v in toks:
            if wd.get(sk, 0) < v:
                wd[sk] = v
                waits.append((sk, v))
        self.prog[eng].append([waits, None, None, 0])

    def finalize(self):
        import bisect
        ref = {}
        for e in self.engs:
            for waits, fn, sk, n in self.prog[e]:
                for (wsk, v) in waits:
                    if wsk not in self.dma_keys:
                        ref.setdefault(wsk, set()).add(v)
        rank = {k: sorted(v) for k, v in ref.items()}
        out = {e: [] for e in self.engs}
        for e in self.engs:
            for waits, fn, sk, n in self.prog[e]:
                w2 = []
                for (wsk, v) in waits:
                    if wsk in self.dma_keys:
                        w2.append((wsk, v))
                    else:
                        w2.append((wsk, bisect.bisect_left(rank[wsk], v) + 1))
                if fn is None:
                    out[e].append((w2, None, None, 0))
                elif sk in self.dma_keys:
                    out[e].append((w2, fn, sk, 16))
                else:
                    out[e].append((w2, fn, sk, 1 if n in ref.get(sk, ()) else 0))
        self.final = out
        return out


def build(NSEQ=4, NT=4, TCH=4, dump=None, n_layers=DEPTH, stop=None):
    T = TCH * 128
    cur = {"ti": 0}
    if stop is not None and ":" in stop:
        stop_ti, stop_ph = int(stop.split(":")[0]), stop.split(":")[1]
        stop_l = int(stop.split(":")[2]) if stop.count(":") > 1 else 0
    else:
        stop_ti, stop_ph, stop_l = 0, stop, 0

    def stopped(ph):
        return stop_ph == ph and cur["ti"] == stop_ti and cur.get("l", 0) == stop_l
    NTOK = NSEQ * NT * T
    NB = TCH * 32
    nc = bass.Bass("TRN2", target_bir_lowering=False)
    S = Sched()

    def dram(name, shape, dt, kind):
        return nc.dram_tensor(name, list(shape), dt, kind=kind).ap()

    x_d = dram("x", [NTOK, D], F32, "ExternalInput")
    out_d = dram("out", [NTOK, D], F32, "ExternalOutput")
    win_d = dram("w_in", [DEPTH, D, NIN], F32, "ExternalInput")
    wa_d = dram("w_a_out", [DEPTH, D, D], F32, "ExternalInput")
    ws_d = dram("w_s_out", [DEPTH, 2 * D, D], F32, "ExternalInput")
    wo_d = dram("w_o", [DEPTH, D, D], F32, "ExternalInput")
    cpar_d = dram("cpar", [128, CP_N], F32, "ExternalInput")
    lnpar_d = dram("lnpar", [3, 2 * D], F32, "ExternalInput")
    tri_d = dram("tri", [128, 128], F32, "ExternalInput")
    ident_d = dram("ident", [128, 128], F32, "ExternalInput")
    esel_d = dram("esel", [96, 32 * 128], F32, "ExternalInput")
    win_b = dram("win_b", [DEPTH, D, NIN], BF16, "Internal")
    wa_b = dram("wa_b", [DEPTH, D, D], BF16, "Internal")
    ws_b = dram("ws_b", [DEPTH, 2 * D, D], BF16, "Internal")
    wo_b = dram("wo_b", [DEPTH, D, D], BF16, "Internal")
    dump_d = {}
    if dump:
        for name, shape in dump.items():
            dump_d[name] = dram("dbg_" + name, shape, F32, "ExternalOutput")

    es = ExitStack()

    def sb(name, shape, dt):
        return es.enter_context(nc.sbuf_tensor(name, list(shape), dt))

    cpar = sb("cpar_s", [128, CP_N], F32)
    lnbuf = sb("lnbuf", [128, 2 * D], F32)
    tri_f = sb("tri_f", [128, 128], F32)
    ones_f = sb("ones_f", [128, 128], F32)
    ident_b = sb("ident_b", [128, 128], BF16)
    esel = sb("esel_s", [96, 32 * 128], BF16)
    abc = sb("abc", [128, DEPTH * 32], F32)
    wsl = [sb(f"wsl{i}", [128, 8, 512], BF16) for i in range(NSLOT)]
    xst = [sb(f"xst{i}", [128, D], F32) for i in range(2)]
    H = sb("H", [128, TCH, D], F32)
    hT = sb("hT", [128, 8, T], BF16)
    hb = sb("hb", [128, D], BF16)
    ya_in = sb("ya_in", [128, 8, T], BF16)
    SZ = sb("SZ", [128, TCH, 2048], BF16)
    X = sb("X", [128, TCH * 2048], BF16)
    BT = sb("BT", [128, 4, T], BF16)
    CT = sb("CT", [128, 4, T], BF16)
    Btok = sb("Btok", [128, TCH, 512], BF16)
    S32 = [sb(f"S32_{l}", [128, 2048], F32) for l in range(DEPTH)]
    Sbf = sb("Sbf", [128, 2048], BF16)
    YNT = sb("YNT", [128, 16, T], BF16)
    hist_a = sb("hist_a", [128, DEPTH, 8, 2], F32)
    hist_s = sb("hist_s", [128, DEPTH, 24, 3], F32)
    arena = sb("arena", [128, 14 * SLW], F32)
    dtv = sb("dtv", [128, NB], F32)
    dte_ = sb("dte", [128, NB], F32)
    dt_ = sb("dt", [128, NB], F32)
    dtA = sb("dtA", [128, NB], F32)
    dtdte = sb("dtdte", [128, NB], F32)
    dtA3 = sb("dtA3", [128, 3, TCH, 96], BF16)
    dsp = sb("dsp", [128, 3, NB], BF16)
    dr1 = sb("dr1", [128, NB], F32)
    dr2 = sb("dr2", [128, NB], F32)
    tri_b = sb("tri_b", [128, 128], BF16)
    ones_b = sb("ones_b", [128, 128], BF16)
    negA = sb("negA", [128, NB], F32)
    Ecum = sb("Ecum", [128, NB], F32)
    CDs = sb("CD", [128, NB], F32)
    dtend = sb("dtend", [128, NB], F32)
    A3 = sb("A3", [96, T], BF16)
    A3m = sb("A3m", [96, T], BF16)
    stt = sb("stt", [128, 2, 6], F32)
    mv = sb("mv", [128, 4], F32)
    ss = sb("ss", [128, 8], F32)
    mhalf = sb("mhalf", [128, 8], F32)
    banks = [es.enter_context(nc.psum_tensor(f"bank{i}", [128, 512], F32)) for i in range(8)]

    R = {}

    def res(name):
        if name not in R:
            R[name] = Res(name)
        return R[name]

    bres = [res(f"bank{i}") for i in range(8)]
    for b_ in bres:
        b_.excl = True
    slres = [res(f"wsl{i}") for i in range(NSLOT)]
    ares = [res(f"ar{i}") for i in range(14)]

    def aslot(i, w=SLW):
        return arena[:, i * SLW:i * SLW + w]

    def aslot_bf(i, lo, n):
        v = arena[:, i * SLW:i * SLW + 512].bitcast(BF16)
        return v[:, lo:lo + n]

    bank_pool = {"ids": list(range(8))}

    def newbank():
        ids = bank_pool["ids"]
        i = ids[S.bank_i % len(ids)]
        S.bank_i += 1
        return banks[i], bres[i]

    def cp(off, n):
        return cpar[:, off:off + n]

    cst_res = [res("cpar"), res("tri"), res("ident"), res("esel"), res("ones"), res("abc")]
    S.op("pool", lambda e: e.dma_start(out=cpar[:, :], in_=cpar_d[:, :]), writes=[res("cpar")], dma="cst")
    S.op("pool", lambda e: e.dma_start(out=tri_f[:, :], in_=tri_d[:, :]), writes=[res("tri")], dma="cst")
    S.op("pool", lambda e: e.dma_start(out=ident_b[:, :], in_=ident_d[:, :]), writes=[res("ident")], dma="cst")
    S.op("pool", lambda e: e.dma_start(out=tri_b[:, :], in_=tri_d[:, :]), writes=[res("trib")], dma="cst")
    S.op("pool", lambda e: e.dma_start(out=esel[:, :], in_=esel_d[:, :]), writes=[res("esel")], dma="cst")
    tot = ("cst", S.cnt["cst"])
    for nm in ("cpar", "tri", "ident", "esel", "trib"):
        res(nm).w = tot
    S.op("pool", lambda e: e.memset(ones_f[:, :], 1.0), writes=[res("ones")])
    S.op("pool", lambda e: e.memset(mhalf[:, :], -0.5), writes=[res("mhalf")])
    S.op("pool", lambda e: e.memset(ones_b[:, :], 1.0), writes=[res("onesb")])
    S.op("act", lambda e: e.activation(out=abc[:, :], in_=cp(CP_ALOG, 64), func=AF.Exp),
         reads=[res("cpar")], writes=[res("abc")])
    S.op("dve", lambda e: e.tensor_scalar(out=abc[:, :], in0=abc[:, :], scalar1=-1.0, scalar2=None, op0=ALU.mult),
         reads=[res("abc")], writes=[res("abc")])

    def cast(l, grp, dst, src):
        S.op("pool", lambda e: e.dma_start(out=dst, in_=src), writes=[res(f"cv{l}{grp}")], dma=f"cv{l}{grp}")

    for l in range(n_layers):
        for (grp, c0, c1) in ((0, 0, OFF_XBC), (1, OFF_XBC, NIN)):
            for kc in range(8):
                cast(l, grp, win_b[l, kc * 128:(kc + 1) * 128, c0:c1], win_d[l, kc * 128:(kc + 1) * 128, c0:c1])
        for kc in range(8):
            cast(l, 2, wa_b[l, kc * 128:(kc + 1) * 128, :], wa_d[l, kc * 128:(kc + 1) * 128, :])
        for kc in range(16):
            cast(l, 2, ws_b[l, kc * 128:(kc + 1) * 128, :], ws_d[l, kc * 128:(kc + 1) * 128, :])
        for kc in range(8):
            cast(l, 3, wo_b[l, kc * 128:(kc + 1) * 128, :], wo_d[l, kc * 128:(kc + 1) * 128, :])

    def layer_blocks(l):
        bl = []
        for nm, off in (("u", OFF_U), ("C", OFF_C), ("B", OFF_B), ("z", OFF_ZA)):
            for q in range(2):
                bl.append((nm + str(q), win_b, l, 0, off + q * 512, 512, 0))
        for q in range(6):
            bl.append((f"xbc{q}", win_b, l, 0, OFF_XBC + q * 512, 512, 1))
            if q < 4:
                bl.append((f"zs{q}", win_b, l, 0, OFF_ZS + q * 512, 512, 0))
            elif q == 4:
                bl.append(("dt", win_b, l, 0, OFF_DT, 32, 1))
        for h in range(2):
            bl.append((f"ga{h}", win_b, l, 0, OFF_GA + h * 512, 512, 1))
            bl.append((f"gs{h}", win_b, l, 0, OFF_GS + h * 512, 512, 1))
            bl.append((f"wa{h}", wa_b, l, 0, h * 512, 512, 2))
            bl.append((f"wsa{h}", ws_b, l, 0, h * 512, 512, 2))
            bl.append((f"wsb{h}", ws_b, l, 1024, h * 512, 512, 2))
        for h in range(2):
            bl.append((f"wo{h}", wo_b, l, 0, h * 512, 512, 3))
        return bl

    tiles = [(s, t) for s in range(NSEQ) for t in range(NT)]
    gblocks = []
    for _ in tiles:
        for l in range(n_layers):
            gblocks.extend(layer_blocks(l))
    st = {"next_load": 0, "next_use": 0}

    def prefetch():
        i = st["next_load"]
        if i >= len(gblocks):
            return
        st["next_load"] += 1
        nm, src, l, r0, c0, ncol, grp = gblocks[i]
        s = i % NSLOT
        src_ap = src[l, r0:r0 + 1024, c0:c0 + ncol].rearrange("(kc p) c -> p kc c", p=128)
        dst_ap = wsl[s][:, :, 0:ncol]
        S.op("sp", lambda e: e.dma_start(out=dst_ap, in_=src_ap), reads=[res(f"cv{l}{grp}")],
             writes=[slres[s]], dma=f"wl{s}")

    def use_block(name):
        i = st["next_use"]
        assert gblocks[i][0] == name, (gblocks[i][0], name)
        st["next_use"] += 1
        s = i % NSLOT
        return wsl[s], slres[s]

    def done_block():
        prefetch()

    for _ in range(NSLOT):
        prefetch()

    def mmgroup(out_ap, bres_, pairs, reads, first_start=True, last_inc=True):
        n = len(pairs)
        for i, (lt, rh) in enumerate(pairs):
            S.op("pe", lambda e, lt=lt, rh=rh, i=i: e.matmul(out_ap, lhsT=lt, rhs=rh, start=(first_start and i == 0),
                                                          stop=(i == n - 1)),
                 reads=reads, writes=[bres_], inc=(last_inc and i == n - 1))

    def layernorm(src_fn, src_res, c, lnres, eps, l_next):
        Hc = H[:, c, :]
        Hres = res(f"H{c}")
        for k in range(2):
            S.op("dve", lambda e, k=k: e.bn_stats(out=stt[:, k, :], in_=src_fn(k)), reads=src_res, writes=[res("stt")])
        S.op("dve", lambda e: e.bn_aggr(out=mv[:, 0:2], in_=stt[:, :, :]), reads=[res("stt")], writes=[res("mv")])
        S.op("dve", lambda e: e.tensor_scalar(out=mv[:, 3:4], in0=mv[:, 1:2], scalar1=float(eps), scalar2=None,
                                              op0=ALU.add), reads=[res("mv")], writes=[res("mv3")])
        S.op("pool", lambda e: e.tensor_tensor(out=mv[:, 2:3], in0=mv[:, 3:4], in1=mhalf[:, 0:1], op=ALU.pow),
             reads=[res("mv3"), res("mhalf")], writes=[res("mv2")])
        for k in range(2):
            S.op("dve", lambda e, k=k: e.tensor_scalar(out=Hc[:, k * 512:(k + 1) * 512], in0=src_fn(k),
                                                       scalar1=mv[:, 0:1], scalar2=mv[:, 2:3],
                                                       op0=ALU.subtract, op1=ALU.mult),
                 reads=src_res + [res("mv"), res("mv2")], writes=[Hres])
        S.op("dve", lambda e: e.tensor_tensor(out=Hc, in0=Hc, in1=lnbuf[:, 0:D], op=ALU.mult),
             reads=[Hres, lnres], writes=[Hres])
        S.op("dve", lambda e: e.tensor_tensor(out=Hc, in0=Hc, in1=lnbuf[:, D:2 * D], op=ALU.add),
             reads=[Hres, lnres], writes=[Hres])
        if l_next:
            S.op("act", lambda e: e.activation(out=hb[:, :], in_=Hc, func=AF.Copy), reads=[Hres], writes=[res("hb")])
            bk, br = newbank()
            bv = bk[:, :].bitcast(BF16)
            for kc in range(8):
                S.op("pe", lambda e, kc=kc: e.transpose(out=bv[:, kc * 128:(kc + 1) * 128],
                                                        in_=hb[:, kc * 128:(kc + 1) * 128], identity=ident_b[:, :]),
                     reads=[res("hb"), res("ident")], writes=[br], inc=(kc == 7))
            S.op("act", lambda e: e.activation(out=hT[:, :, c * 128:(c + 1) * 128],
                                               in_=bv.rearrange("p (a b) -> p a b", a=8), func=AF.Copy),
                 reads=[br], writes=[res(f"hT{c}")])

    def load_ln(i):
        src = lnpar_d[i:i + 1, :].partition_broadcast(128)
        S.op("pool", lambda e: e.dma_start(out=lnbuf[:, :], in_=src), writes=[res("lnbuf")], dma="lnl")

    hTres = [res(f"hT{c}") for c in range(TCH)]
    Xres = [res(f"X{c}") for c in range(TCH)]
    mixres = Xres[0:max(1, T // 256)]
    mixedT = X[:, 0:8 * T].rearrange("p (j t) -> p j t", j=8)

    def dumpit(name, ap, rs):
        if dump and name in dump_d and not st.get("dumped_" + name):
            st["dumped_" + name] = True
            S.op("pool", lambda e: e.dma_start(out=dump_d[name], in_=ap), reads=rs, dma="dbg")

    def layer(l, seq_first, seq_last_tile, is_last_layer, row0):
        caw = cp(CP_CAW, 48).rearrange("p (l c k) -> p l c k", l=2, c=8)
        csw = cp(CP_CSW, 192).rearrange("p (l c k) -> p l c k", l=2, c=24)
        csb = cp(CP_CSB, 48).rearrange("p (l c) -> p l c", l=2)
        nsg = cp(CP_NSG, 32).rearrange("p (l c) -> p l c", l=2)
        dtb = cp(CP_DTB, 64)[:, l * 32:(l + 1) * 32]
        dsk = cp(CP_DSK, 64)[:, l * 32:(l + 1) * 32]
        A_l = abc[:, l * 32:(l + 1) * 32]
        cres = res("cpar")

        def inproj_fm(slot, sres, q):
            bk, br = newbank()
            mmgroup(bk[:, 0:T], br, [(slot[:, kc, q * 128:(q + 1) * 128], hT[:, kc, :]) for kc in range(8)],
                    reads=[sres] + hTres)
            return bk, br

        for q2 in range(2):
            slot, sres = use_block(f"u{q2}")
            for q in range(4):
                j = q2 * 4 + q
                bk, br = inproj_fm(slot, sres, q)
                S.op("act", lambda e, bk=bk, j=j: e.activation(out=aslot(j, T), in_=bk[:, 0:T], func=AF.Copy),
                     reads=[br], writes=[ares[j]])
            done_block()
        for q2 in range(2):
            slot, sres = use_block(f"C{q2}")
            for q in range(4):
                j = q2 * 4 + q
                bk, br = inproj_fm(slot, sres, q)
                ci = 8 + (j % 2)
                cu = aslot(ci)
                hres = res(f"hista{l}_{j}")
                S.op("pool", lambda e, cu=cu, j=j: e.tensor_copy(out=cu[:, 0:2], in_=hist_a[:, l, j, :]),
                     reads=[hres], writes=[ares[ci]])
                S.op("dve", lambda e, cu=cu, bk=bk, j=j: e.tensor_tensor(out=cu[:, 2:2 + T], in0=bk[:, 0:T],
                                                                        in1=aslot(j, T), op=ALU.mult),
                     reads=[br, ares[j]], writes=[ares[ci]])
                S.op("pool", lambda e, cu=cu, j=j: e.tensor_copy(out=hist_a[:, l, j, :], in_=cu[:, T:T + 2]),
                     reads=[ares[ci]], writes=[hres])
                S.op("dve", lambda e, cu=cu, j=j: e.tensor_scalar(out=aslot(j, T), in0=cu[:, 0:T],
                                                                  scalar1=caw[:, l, j, 0:1], scalar2=None, op0=ALU.mult),
                     reads=[ares[ci], cres], writes=[ares[j]])
                for k in (1, 2):
                    S.op("dve", lambda e, cu=cu, j=j, k=k: e.scalar_tensor_tensor(
                        out=aslot(j, T), in0=cu[:, k:k + T], scalar=caw[:, l, j, k:k + 1], in1=aslot(j, T),
                        op0=ALU.mult, op1=ALU.add), reads=[ares[ci], ares[j], cres], writes=[ares[j]])
            done_block()
        for q2 in range(2):
            slot, sres = use_block(f"B{q2}")
            for q in range(4):
                j = q2 * 4 + q
                bk, br = inproj_fm(slot, sres, q)
                S.op("dve", lambda e, bk=bk, j=j: e.tensor_tensor(out=aslot(j, T), in0=bk[:, 0:T], in1=aslot(j, T),
                                                                  op=ALU.mult), reads=[br, ares[j]], writes=[ares[j]])
            done_block()
        for q2 in range(2):
            slot, sres = use_block(f"z{q2}")
            for q in range(4):
                j = q2 * 4 + q
                bk, br = inproj_fm(slot, sres, q)
                si = 10 + (j % 2)
                S.op("act", lambda e, bk=bk, si=si: e.activation(out=aslot(si, T), in_=bk[:, 0:T], func=AF.Silu),
                     reads=[br], writes=[ares[si]])
                S.op("dve", lambda e, j=j, si=si: e.tensor_tensor(out=ya_in[:, j, :], in0=aslot(j, T), in1=aslot(si, T),
                                                                  op=ALU.mult),
                     reads=[ares[j], ares[si]], writes=[res(f"ya{j}")])
            done_block()
        dumpit("ya_in", ya_in[:, :, :], [res(f"ya{j}") for j in range(8)])
        if stopped("A"):
            return

        def phaseB_block(q):
            slot, sres = use_block(f"zs{q}")
            for c in range(TCH):
                bk, br = newbank()
                mmgroup(bk[:, :], br, [(hT[:, kc, c * 128:(c + 1) * 128], slot[:, kc, :]) for kc in range(8)],
                        reads=[sres, hTres[c]])
                S.op("act", lambda e, bk=bk, c=c, q=q: e.activation(out=SZ[:, c, q * 512:(q + 1) * 512], in_=bk[:, :],
                                                                    func=AF.Silu), reads=[br], writes=[res(f"SZ{c}")])
            done_block()
        dt_late_list = []

        def phaseB_dt():
            slot, sres = use_block("dt")
            bk, br = newbank()
            for c in range(TCH):
                mmgroup(bk[:, c * 32:(c + 1) * 32], br,
                        [(hT[:, kc, c * 128:(c + 1) * 128], slot[:, kc, 0:32]) for kc in range(8)],
                        reads=[sres, hTres[c]], last_inc=(c == TCH - 1))
            done_block()
            dres = res("dtsmall")
            S.op("dve", lambda e, bk=bk: e.tensor_tensor(out=dtv[:, :].rearrange("p (c h) -> p c h", c=TCH),
                                                         in0=bk[:, 0:NB].rearrange("p (c h) -> p c h", c=TCH),
                                                         in1=dtb.unsqueeze(1).broadcast_to([128, TCH, 32]), op=ALU.add),
                 reads=[br, cres], writes=[res("dtv")])
            S.op("act", lambda e: e.activation(out=dtv[:, :], in_=dtv[:, :], func=AF.Exp), reads=[res("dtv")],
                 writes=[res("dtv")])
            S.op("act", lambda e: e.activation(out=dt_[:, :], in_=dtv[:, :], func=AF.Ln, bias=1.0), reads=[res("dtv")],
                 writes=[res("dt")])
            S.op("dve", lambda e: e.tensor_tensor(out=dtA[:, :].rearrange("p (c h) -> p c h", c=TCH),
                                                  in0=dt_[:, :].rearrange("p (c h) -> p c h", c=TCH),
                                                  in1=A_l.unsqueeze(1).broadcast_to([128, TCH, 32]), op=ALU.mult),
                 reads=[res("dt"), res("abc")], writes=[res("dtA")])
            S.op("dve", lambda e: e.tensor_copy(out=dsp[:, 0, :], in_=dtA[:, :]), reads=[res("dtA")], writes=[res("dsp0")])
            S.op("dve", lambda e: e.tensor_tensor(out=dr1[:, :], in0=dtA[:, :], in1=dsp[:, 0, :], op=ALU.subtract),
                 reads=[res("dtA"), res("dsp0")], writes=[res("dr1")])
            S.op("dve", lambda e: e.tensor_copy(out=dsp[:, 1, :], in_=dr1[:, :]), reads=[res("dr1")], writes=[res("dsp1")])
            S.op("dve", lambda e: e.tensor_tensor(out=dr2[:, :], in0=dr1[:, :], in1=dsp[:, 1, :], op=ALU.subtract),
                 reads=[res("dr1"), res("dsp1")], writes=[res("dr2")])
            S.op("dve", lambda e: e.tensor_copy(out=dsp[:, 2, :], in_=dr2[:, :]), reads=[res("dr2")], writes=[res("dsp2")])
            for i3 in range(3):
                S.op("dve", lambda e, i3=i3: e.tensor_copy(
                    out=dtA3[:, i3, :, :].rearrange("p c (r h) -> p c r h", r=3),
                    in_=dsp[:, i3, :].rearrange("p (c h) -> p c h", c=TCH).unsqueeze(2).broadcast_to([128, TCH, 3, 32])),
                     reads=[res(f"dsp{i3}")], writes=[res("dtA3")])
            dspres = [res("dsp0"), res("dsp1"), res("dsp2")]

            def dt_late():
                bk1, br1 = newbank()
                for c in range(TCH):
                    for i3 in range(3):
                        S.op("pe", lambda e, c=c, i3=i3, bk1=bk1: e.matmul(bk1[:, c * 32:(c + 1) * 32], lhsT=tri_b[:, :],
                                                                          rhs=dsp[:, i3, c * 32:(c + 1) * 32], start=(i3 == 0),
                                                                          stop=(i3 == 2)),
                             reads=[res("trib")] + dspres, writes=[br1], inc=False)
                for c in range(TCH):
                    for i3 in range(3):
                        S.op("pe", lambda e, c=c, i3=i3, bk1=bk1: e.matmul(bk1[:, 256 + c * 32:256 + (c + 1) * 32],
                                                                          lhsT=ones_b[:, :], rhs=dsp[:, i3, c * 32:(c + 1) * 32],
                                                                          start=(i3 == 0), stop=(i3 == 2)),
                             reads=[res("onesb")] + dspres, writes=[br1], inc=(c == TCH - 1 and i3 == 2))
                bk2, br2 = newbank()
                for c in range(TCH):
                    for i3 in range(3):
                        S.op("pe", lambda e, c=c, i3=i3, bk2=bk2: e.matmul(bk2[0:96, c * 128:(c + 1) * 128],
                                                                          lhsT=dtA3[:, i3, c, :], rhs=tri_b[:, :],
                                                                          start=(i3 == 0), stop=(i3 == 2)),
                             reads=[res("trib"), res("dtA3")], writes=[br2], inc=(c == TCH - 1 and i3 == 2))
                S.op("dve", lambda e, bk1=bk1: e.tensor_scalar(out=negA[:, :], in0=bk1[:, 0:NB], scalar1=-1.0, scalar2=None,
                                                               op0=ALU.mult), reads=[br1], writes=[res("negA")])
                S.op("act", lambda e, bk1=bk1: e.activation(out=Ecum[:, :], in_=bk1[:, 0:NB], func=AF.Exp), reads=[br1],
                     writes=[res("Ecum")])
                S.op("act", lambda e, bk1=bk1: e.activation(out=CDs[:, :], in_=bk1[:, 256:256 + NB], func=AF.Exp), reads=[br1],
                     writes=[res("CD")])
                S.op("dve", lambda e, bk1=bk1: e.tensor_tensor(out=dtend[:, :], in0=bk1[:, 256:256 + NB], in1=negA[:, :],
                                                               op=ALU.add), reads=[br1, res("negA")], writes=[res("dtend")])
                S.op("act", lambda e: e.activation(out=dte_[:, :], in_=dtend[:, :], func=AF.Exp), reads=[res("dtend")],
                     writes=[res("dte")])
                S.op("dve", lambda e: e.tensor_tensor(out=dtdte[:, :], in0=dt_[:, :], in1=dte_[:, :], op=ALU.mult),
                     reads=[res("dt"), res("dte")], writes=[res("dtdte")])
                dumpit("negA", negA[:, :], [res("negA")])
                S.op("act", lambda e, bk2=bk2: e.activation(out=A3[0:96, :], in_=bk2[0:96, 0:T], func=AF.Copy), reads=[br2],
                     writes=[res("A3")])
                r1 = arena[0:96, 12 * SLW:12 * SLW + T]
                r2 = arena[0:96, 13 * SLW:13 * SLW + T]
                S.op("dve", lambda e, bk2=bk2: e.tensor_tensor(out=r1[0:96, :], in0=bk2[0:96, 0:T], in1=A3[0:96, :],
                                                               op=ALU.subtract), reads=[br2, res("A3")], writes=[ares[12]])
                S.op("dve", lambda e: e.tensor_copy(out=A3m[0:96, :], in_=r1[0:96, :]), reads=[ares[12]], writes=[res("A3m")])
                S.op("dve", lambda e: e.tensor_tensor(out=r2[0:96, :], in0=r1[0:96, :], in1=A3m[0:96, :], op=ALU.subtract),
                     reads=[ares[12], res("A3m")], writes=[ares[13]])
                S.op("dve", lambda e: e.tensor_copy(out=A3[64:96, :], in_=r2[64:96, :]), reads=[ares[13]], writes=[res("A3")])
                S.op("dve", lambda e: e.tensor_copy(out=A3[32:64, :], in_=A3m[32:64, :]), reads=[res("A3m")], writes=[res("A3")])
                dumpit("dt", dt_[:, :], [res("dt")])
                dumpit("negA", negA[:, :], [res("negA")])

            dt_late_list.append(dt_late)

        pend = []

        def phaseC_block(q2):
            slot, sres = use_block(f"xbc{q2}")
            for q in range(4):
                j = q2 * 4 + q
                bk, br = inproj_fm(slot, sres, q)
                while len(pend) > 2:
                    pend.pop(0)()
                ri = j % 2
                raw = aslot(ri)
                hres = res(f"hists{l}_{j}")
                S.op("pool", lambda e, raw=raw, j=j: e.tensor_copy(out=raw[:, 0:3], in_=hist_s[:, l, j, :]),
                     reads=[hres], writes=[ares[ri]])
                S.op("act", lambda e, raw=raw, bk=bk: e.activation(out=raw[:, 3:3 + T], in_=bk[:, 0:T], func=AF.Copy),
                     reads=[br], writes=[ares[ri]])
                S.op("pool", lambda e, raw=raw, j=j: e.tensor_copy(out=hist_s[:, l, j, :], in_=raw[:, T:T + 3]),
                     reads=[ares[ri]], writes=[hres])
                ai = 2 + ri
                acc = aslot(ai, T)
                S.op("act", lambda e, bk=bk, acc=acc, j=j: e.activation(
                    out=acc, in_=bk[:, 0:T], func=AF.Identity, scale=csw[:, l, j, 3:4], bias=csb[:, l, j:j + 1]),
                     reads=[br, cres], writes=[ares[ai]])
                for k in (0, 1, 2):
                    S.op("dve", lambda e, raw=raw, acc=acc, j=j, k=k: e.scalar_tensor_tensor(
                        out=acc, in0=raw[:, k:k + T], scalar=csw[:, l, j, k:k + 1], in1=acc,
                        op0=ALU.mult, op1=ALU.add), reads=[ares[ri], ares[ai], cres], writes=[ares[ai]])
                if j < 16:
                    xi = 4 + (j % 4)
                    xc = aslot_bf(xi, 0, T)
                    S.op("act", lambda e, acc=acc, xc=xc: e.activation(out=xc, in_=acc, func=AF.Silu),
                         reads=[ares[ai]], writes=[ares[xi]])

                    def tr_x(j=j, xi=xi, xc=xc):
                        tbk, tbr = newbank()
                        tv = tbk[:, :].bitcast(BF16)
                        for c in range(TCH):
                            S.op("pe", lambda e, c=c: e.transpose(
                                out=tv[:, c * 128:(c + 1) * 128], in_=xc[:, c * 128:(c + 1) * 128],
                                identity=ident_b[:, :]), reads=[ares[xi], res("ident")], writes=[tbr])
                        Xv = X[:, :].rearrange("p (c f) -> p c f", c=TCH)
                        S.op("dve", lambda e: e.tensor_copy(
                            out=Xv[:, :, j * 128:(j + 1) * 128], in_=tv[:, 0:T].rearrange("p (c f) -> p c f", c=TCH)),
                             reads=[tbr], writes=Xres)
                    pend.append(tr_x)
                elif j < 20:
                    g = j - 16
                    S.op("act", lambda e, acc=acc, g=g: e.activation(out=BT[:, g, :], in_=acc, func=AF.Silu),
                         reads=[ares[ai]], writes=[res(f"BT{g}")])

                    def tr_b(g=g):
                        tbk, tbr = newbank()
                        tv = tbk[:, :].bitcast(BF16)
                        for c in range(TCH):
                            S.op("pe", lambda e, c=c: e.transpose(
                                out=tv[:, c * 128:(c + 1) * 128], in_=BT[:, g, c * 128:(c + 1) * 128],
                                identity=ident_b[:, :]), reads=[res(f"BT{g}"), res("ident")], writes=[tbr])
                        S.op("dve", lambda e: e.tensor_copy(
                            out=Btok[:, :, g * 128:(g + 1) * 128], in_=tv[:, 0:T].rearrange("p (c f) -> p c f", c=TCH)),
                             reads=[tbr], writes=[res("Btok")])
                    pend.append(tr_b)
                else:
                    g = j - 20
                    S.op("act", lambda e, acc=acc, g=g: e.activation(out=CT[:, g, :], in_=acc, func=AF.Silu),
                         reads=[ares[ai]], writes=[res(f"CT{g}")])
            done_block()
        for q2 in range(6):
            phaseC_block(q2)
            if q2 < 4:
                phaseB_block(q2)
            elif q2 == 4:
                phaseB_dt()
        while pend:
            pend.pop(0)()
        for f_ in dt_late_list:
            f_()
        dumpit("X", X[:, :], Xres)
        dumpit("BT", BT[:, :, :], [res(f"BT{g}") for g in range(4)])
        dumpit("CT", CT[:, :, :], [res(f"CT{g}") for g in range(4)])
        if stopped("C"):
            return

        Sl = S32[l]
        Sres = [res(f"S32_{l}_{g}") for g in range(4)]
        sbres = [res(f"Sbf{g}") for g in range(4)]
        for g in range(4):
            S.op("pool", lambda e, g=g: e.tensor_copy(out=Sbf[:, g * 512:(g + 1) * 512], in_=Sl[:, g * 512:(g + 1) * 512]),
                 reads=[Sres[g]], writes=[sbres[g]])
        ctx = {}

        def chunk_prep(c):
            cs = slice(c * 128, (c + 1) * 128)
            cbk, cbr = newbank()
            for g in range(4):
                S.op("pe", lambda e, g=g, cbk=cbk, cs=cs: e.matmul(cbk[:, g * 128:(g + 1) * 128], lhsT=BT[:, g, cs],
                                                                   rhs=CT[:, g, cs], start=True, stop=True),
                     reads=[res(f"BT{g}"), res(f"CT{g}")], writes=[cbr])
            cbi = c % 2
            CBm = aslot(cbi, 512)
            S.op("dve", lambda e, cbk=cbk, CBm=CBm: e.tensor_tensor(
                out=CBm.rearrange("p (g l) -> p g l", g=4), in0=cbk[:, :].rearrange("p (g l) -> p g l", g=4),
                in1=tri_f[:, :].unsqueeze(1).broadcast_to([128, 4, 128]), op=ALU.mult),
                 reads=[cbr, res("tri")], writes=[ares[cbi]])

        def stage1(c, g):
            cs = slice(c * 128, (c + 1) * 128)
            last_chunk = seq_last_tile and c == TCH - 1
            cbi = c % 2
            CBm = aslot(cbi, 512)
            gi = g % 2
            hs = slice(c * 32 + g * 8, c * 32 + (g + 1) * 8)
            Xg = X[:, c * 2048 + g * 512:c * 2048 + (g + 1) * 512].rearrange("p (h d) -> p h d", h=8)
            XDg = aslot_bf(10 + gi, 0, 512)
            XSg = aslot_bf(10 + gi, 512, 512)
            XDDg = aslot_bf(12 + gi, 0, 512)
            r_xd, r_xdd = ares[10 + gi], ares[12 + gi]
            S.op("pool", lambda e: e.tensor_tensor(
                out=XDg.rearrange("p (h d) -> p h d", h=8), in0=Xg,
                in1=dt_[:, hs].unsqueeze(2).broadcast_to([128, 8, 64]), op=ALU.mult),
                 reads=[Xres[c], res("dt")], writes=[r_xd])
            S.op("pool", lambda e: e.tensor_tensor(
                out=XSg.rearrange("p (h d) -> p h d", h=8), in0=Xg,
                in1=dsk[:, g * 8:(g + 1) * 8].unsqueeze(2).broadcast_to([128, 8, 64]), op=ALU.mult),
                 reads=[Xres[c], cres], writes=[r_xd])
            MT = aslot_bf(4 + gi, 0, 1024)
            for half in range(2):
                lbk, lbr = newbank()
                for hh in range(4):
                    h = g * 8 + half * 4 + hh
                    S.op("pe", lambda e, lbk=lbk, hh=hh, h=h: e.matmul(
                        lbk[:, hh * 128:(hh + 1) * 128], lhsT=esel[:, h * 128:(h + 1) * 128], rhs=A3[:, cs],
                        start=True, stop=True), reads=[res("esel"), res("A3")], writes=[lbr])
                li = 2 + half
                LT = aslot(li, 512)
                for hh in range(4):
                    h = g * 8 + half * 4 + hh
                    S.op("act", lambda e, lbk=lbk, LT=LT, hh=hh, h=h: e.activation(
                        out=LT[:, hh * 128:(hh + 1) * 128], in_=lbk[:, hh * 128:(hh + 1) * 128], func=AF.Exp,
                        bias=negA[:, c * 32 + h:c * 32 + h + 1]), reads=[lbr, res("negA")], writes=[ares[li]])
                S.op("dve", lambda e, LT=LT, half=half: e.scalar_tensor_tensor(
                    out=MT[:, half * 512:(half + 1) * 512].rearrange("p (h l) -> p h l", h=4),
                    in0=LT.rearrange("p (h l) -> p h l", h=4), scalar=1.0,
                    in1=CBm[:, g * 128:(g + 1) * 128].unsqueeze(1).broadcast_to([128, 4, 128]),
                    op0=ALU.min, op1=ALU.mult), reads=[ares[li], ares[cbi]], writes=[ares[4 + gi]])
        def stage1b(c, g):
            cs = slice(c * 128, (c + 1) * 128)
            gi = g % 2
            XDg = aslot_bf(10 + gi, 0, 512)
            XSg = aslot_bf(10 + gi, 512, 512)
            MT = aslot_bf(4 + gi, 0, 1024)
            r_xd = ares[10 + gi]
            ybk, ybr = banks[gi * 2], bres[gi * 2]
            S.op("pe", lambda e: e.matmul(ybk[:, :], lhsT=ident_b[:, :], rhs=XSg, start=True,
                                          stop=False, skip_group_check=True),
                 reads=[res("ident"), r_xd], writes=[ybr])
            for hh in range(8):
                S.op("pe", lambda e, hh=hh: e.matmul(
                    ybk[:, hh * 64:(hh + 1) * 64], lhsT=MT[:, hh * 128:(hh + 1) * 128],
                    rhs=XDg[:, hh * 64:(hh + 1) * 64], start=False, stop=True, skip_group_check=True),
                     reads=[ares[4 + gi], r_xd], writes=[ybr])
            obk, obr = banks[gi * 2 + 1], bres[gi * 2 + 1]
            S.op("pe", lambda e: e.matmul(obk[:, :], lhsT=CT[:, g, cs],
                                          rhs=Sbf[:, g * 512:(g + 1) * 512], start=True, stop=True),
                 reads=[res(f"CT{g}"), sbres[g]], writes=[obr])
            ctx[(c, g)] = (ybk, ybr, obk, obr)

        def stage2(c, g):
            cs = slice(c * 128, (c + 1) * 128)
            last_chunk = seq_last_tile and c == TCH - 1
            gi = g % 2
            hs = slice(c * 32 + g * 8, c * 32 + (g + 1) * 8)
            ybk, ybr, obk, obr = ctx.pop((c, g))
            yn = aslot_bf(12 + gi, 512, 512)
            r_yn = res(f"yn{gi}")
            if not last_chunk:
                Xg2 = X[:, c * 2048 + g * 512:c * 2048 + (g + 1) * 512].rearrange("p (h d) -> p h d", h=8)
                S.op("pool", lambda e: e.tensor_tensor(
                    out=aslot_bf(12 + gi, 0, 512).rearrange("p (h d) -> p h d", h=8), in0=Xg2,
                    in1=dtdte[:, hs].unsqueeze(2).broadcast_to([128, 8, 64]), op=ALU.mult),
                     reads=[Xres[c], res("dtdte")], writes=[ares[12 + gi]])
            t1 = aslot(6 + gi, 512)
            t3 = aslot(8 + gi, 512)
            S.op("dve", lambda e: e.tensor_tensor(
                out=t1.rearrange("p (h d) -> p h d", h=8), in0=obk[:, :].rearrange("p (h d) -> p h d", h=8),
                in1=Ecum[:, hs].unsqueeze(2).broadcast_to([128, 8, 64]), op=ALU.mult),
                 reads=[obr, res("Ecum")], writes=[ares[6 + gi]])
            S.op("dve", lambda e: e.tensor_tensor(out=t1, in0=ybk[:, :], in1=t1, op=ALU.add),
                 reads=[ybr, ares[6 + gi]], writes=[ares[6 + gi]])
            if c == 0 and g == 0:
                dumpit("ypre", t1, [ares[6 + gi]])
            S.op("pool", lambda e: e.tensor_tensor(
                out=t3, in0=t1, in1=SZ[:, c, g * 512:(g + 1) * 512], op=ALU.mult),
                 reads=[ares[6 + gi], res(f"SZ{c}")], writes=[ares[8 + gi]])
            ssr = res(f"ss{gi}")
            S.op("pool", lambda e: e.memset(ss[:, gi:gi + 1], 0.0), writes=[ssr])
            S.op("act", lambda e: e.activation(out=yn, in_=t3, func=AF.Square, scale=float(512.0 ** -0.5),
                                               accum_out=ss[:, gi:gi + 1]),
                 reads=[ares[8 + gi], ssr], writes=[r_yn, ssr])
            S.op("dve", lambda e: e.tensor_scalar(out=ss[:, 2 + gi:3 + gi], in0=ss[:, gi:gi + 1],
                                                  scalar1=RMS_EPS, scalar2=None, op0=ALU.add),
                 reads=[ssr], writes=[res(f"rt{gi}")])
            S.op("pool", lambda e: e.tensor_tensor(out=ss[:, 4 + gi:5 + gi], in0=ss[:, 2 + gi:3 + gi],
                                                   in1=mhalf[:, 0:1], op=ALU.pow),
                 reads=[res(f"rt{gi}"), res("mhalf")], writes=[res(f"rs{gi}")])
            S.op("act", lambda e: e.activation(out=yn, in_=t3, func=AF.Copy, scale=ss[:, 4 + gi:5 + gi]),
                 reads=[ares[8 + gi], res(f"rs{gi}")], writes=[r_yn])
        def stage2b(c, g):
            cs = slice(c * 128, (c + 1) * 128)
            last_chunk = seq_last_tile and c == TCH - 1
            gi = g % 2
            hs = slice(c * 32 + g * 8, c * 32 + (g + 1) * 8)
            Xg = X[:, c * 2048 + g * 512:c * 2048 + (g + 1) * 512].rearrange("p (h d) -> p h d", h=8)
            XDDg = aslot_bf(12 + gi, 0, 512)
            yn = aslot_bf(12 + gi, 512, 512)
            r_xdd = ares[12 + gi]
            r_yn = res(f"yn{gi}")
            tbk, tbr = newbank()
            tv = tbk[:, :].bitcast(BF16)
            for k in range(4):
                S.op("pe", lambda e, k=k: e.transpose(
                    out=tv[:, k * 128:(k + 1) * 128], in_=yn[:, k * 128:(k + 1) * 128], identity=ident_b[:, :]),
                     reads=[r_yn, res("ident")], writes=[tbr])
            S.op("dve", lambda e: e.tensor_tensor(
                out=YNT[:, g * 4:(g + 1) * 4, cs], in0=tv[:, 0:512].rearrange("p (k t) -> p k t", k=4),
                in1=nsg[:, l, g * 4:(g + 1) * 4].unsqueeze(2).broadcast_to([128, 4, 128]), op=ALU.mult),
                 reads=[tbr, cres], writes=[res(f"YNT{g}")])
            if not last_chunk:
                sbk, sbr_ = newbank()
                S.op("pe", lambda e: e.matmul(
                    sbk[:, :], lhsT=Btok[:, c, g * 128:(g + 1) * 128], rhs=XDDg, start=True, stop=True),
                     reads=[res("Btok"), r_xdd], writes=[sbr_])
                Sg = Sl[:, g * 512:(g + 1) * 512]
                S.op("pool", lambda e: e.tensor_tensor(
                    out=Sg.rearrange("p (h d) -> p h d", h=8), in0=Sg.rearrange("p (h d) -> p h d", h=8),
                    in1=CDs[:, hs].unsqueeze(2).broadcast_to([128, 8, 64]), op=ALU.mult),
                     reads=[Sres[g], res("CD")], writes=[Sres[g]])
                S.op("dve", lambda e: e.tensor_tensor(out=Sg, in0=sbk[:, :], in1=Sg, op=ALU.add),
                     reads=[sbr_, Sres[g]], writes=[Sres[g]])
                S.op("pool", lambda e: e.tensor_copy(out=Sbf[:, g * 512:(g + 1) * 512], in_=Sg),
                     reads=[Sres[g]], writes=[sbres[g]])

        seq_cg = [(c, g) for c in range(TCH) for g in range(4)]
        bank_pool["ids"] = [4, 5, 6, 7]
        nst = len(seq_cg)
        for i in range(nst + 3):
            if i < nst:
                if seq_cg[i][1] == 0:
                    chunk_prep(seq_cg[i][0])
                stage1(*seq_cg[i])
            if 0 <= i - 1 < nst:
                stage1b(*seq_cg[i - 1])
            if 0 <= i - 2 < nst:
                stage2(*seq_cg[i - 2])
            if 0 <= i - 3 < nst:
                stage2b(*seq_cg[i - 3])
        bank_pool["ids"] = list(range(8))
        dumpit("YNT", YNT[:, :, :], [res(f"YNT{g}") for g in range(4)])
        if stopped("D"):
            return

        for half in range(2):
            slot, sres = use_block(f"ga{half}")
            for q in range(4):
                bk, br = inproj_fm(slot, sres, q)
                S.op("act", lambda e, bk=bk, q=q: e.activation(out=aslot(q, T), in_=bk[:, 0:T], func=AF.Tanh, scale=0.5),
                     reads=[br], writes=[ares[q]])
            done_block()
            slot, sres = use_block(f"gs{half}")
            for q in range(4):
                bk, br = inproj_fm(slot, sres, q)
                S.op("act", lambda e, bk=bk, q=q: e.activation(out=aslot(4 + q, T), in_=bk[:, 0:T], func=AF.Tanh,
                                                               scale=0.5), reads=[br], writes=[ares[4 + q]])
            done_block()
            slot, sres = use_block(f"wa{half}")
            for q in range(4):
                bk, br = newbank()
                mmgroup(bk[:, 0:T], br, [(slot[:, kc, q * 128:(q + 1) * 128], ya_in[:, kc, :]) for kc in range(8)],
                        reads=[sres] + [res(f"ya{j}") for j in range(8)])
                S.op("dve", lambda e, bk=bk, q=q: e.scalar_tensor_tensor(
                    out=aslot(q, T), in0=aslot(q, T), scalar=1.0, in1=bk[:, 0:T], op0=ALU.add, op1=ALU.mult),
                     reads=[br, ares[q]], writes=[ares[q]])
            done_block()
            slota, sresa = use_block(f"wsa{half}")
            slotb, sresb = use_block(f"wsb{half}")
            for q in range(4):
                bk, br = newbank()
                pairs = [(slota[:, kc, q * 128:(q + 1) * 128], YNT[:, kc, :]) for kc in range(8)] + \
                        [(slotb[:, kc, q * 128:(q + 1) * 128], YNT[:, 8 + kc, :]) for kc in range(8)]
                mmgroup(bk[:, 0:T], br, pairs, reads=[sresa, sresb] + [res(f"YNT{g}") for g in range(4)])
                S.op("dve", lambda e, bk=bk, q=q: e.scalar_tensor_tensor(
                    out=aslot(4 + q, T), in0=aslot(4 + q, T), scalar=1.0, in1=bk[:, 0:T], op0=ALU.add, op1=ALU.mult),
                     reads=[br, ares[4 + q]], writes=[ares[4 + q]])
                j = half * 4 + q
                S.op("pool", lambda e, q=q, j=j: e.tensor_tensor(out=mixedT[:, j, :], in0=aslot(q, T),
                                                                 in1=aslot(4 + q, T), op=ALU.add),
                     reads=[ares[q], ares[4 + q]], writes=mixres)
            done_block()
            done_block()
        dumpit("mixed", mixedT, mixres)
        slot0, sres0 = use_block("wo0")
        slot1, sres1 = use_block("wo1")
        lnres = res("lnbuf")
        for c in range(TCH):
            bks = []
            for half, (slot, sres) in enumerate(((slot0, sres0), (slot1, sres1))):
                bk, br = newbank()
                mmgroup(bk[:, :], br, [(mixedT[:, kc, c * 128:(c + 1) * 128], slot[:, kc, :]) for kc in range(8)],
                        reads=[sres] + mixres)
                bks.append((bk, br))
            Hres = res(f"H{c}")
            for half, (bk, br) in enumerate(bks):
                S.op("dve", lambda e, bk=bk, half=half, c=c: e.scalar_tensor_tensor(
                    out=aslot(8 + half, 512), in0=H[:, c, half * 512:(half + 1) * 512], scalar=2.0 * ALPHA,
                    in1=bk[:, :], op0=ALU.mult, op1=ALU.add), reads=[br, Hres], writes=[ares[8 + half]])
            layernorm(lambda k: aslot(8 + k, 512), [ares[8], ares[9]], c, lnres, 4.0 * LN_EPS,
                      l_next=not is_last_layer)
            if is_last_layer:
                S.op("pool", lambda e, c=c: e.dma_start(out=out_d[row0 + c * 128:row0 + (c + 1) * 128, :], in_=H[:, c, :]),
                     reads=[Hres], writes=[res(f"out{c}")], dma=f"st{c}")
            else:
                if c == 0:
                    dumpit("h1", H[:, 0, :], [Hres])
        done_block()
        done_block()

    def load_x(ti, c):
        s_, t_ = tiles[ti]
        row = (s_ * NT + t_) * T + c * 128
        i = (ti * TCH + c) % 2
        S.op("pool", lambda e: e.dma_start(out=xst[i][:, :], in_=x_d[row:row + 128, :]), writes=[res(f"xst{i}")],
             dma=f"xl{i}")
        return i

    for ti, (s_, t_) in enumerate(tiles):
        cur["ti"] = ti
        row0 = (s_ * NT + t_) * T
        if t_ == 0:
            for l in range(n_layers):
                S.op("pool", lambda e, l=l: e.memset(S32[l][:, :], 0.0), writes=[res(f"S32_{l}_{g}") for g in range(4)])
                S.op("pool", lambda e, l=l: e.memset(hist_a[:, l, :, :], 0.0),
                     writes=[res(f"hista{l}_{j}") for j in range(8)])
                S.op("pool", lambda e, l=l: e.memset(hist_s[:, l, :, :], 0.0),
                     writes=[res(f"hists{l}_{j}") for j in range(24)])
        load_ln(0)
        for c in range(TCH):
            i = load_x(ti, c)
            layernorm(lambda k, i=i: xst[i][:, k * 512:(k + 1) * 512], [res(f"xst{i}")], c, res("lnbuf"), LN_EPS,
                      l_next=True)
        if ti == 0:
            dumpit("h0", H[:, 0, :], [res("H0")])
        if stopped("ln"):
            break
        for l in range(n_layers):
            load_ln(1 + l)
            cur["l"] = l
            layer(l, t_ == 0, t_ == NT - 1, l == n_layers - 1, row0)
            cur["l"] = 0
            if stop_ph is not None and ti == stop_ti and stop_ph != "L0" and l == stop_l:
                break
            if stopped("L0"):
                break
        if stop_ph is not None and ti == stop_ti:
            break

    S.wait_all("pool", [(k, v) for k, v in S.cnt.items() if k.startswith("st") or k == "dbg"])

    sems = {k: es.enter_context(nc.semaphore(k)) for k in S.cnt.keys()}
    for e in ("pe", "act", "dve", "pool"):
        if e not in sems:
            sems[e] = es.enter_context(nc.semaphore(e))

    S.finalize()

    def replay(name, eng):
        for waits, fn, sk, step in S.final[name]:
            for (wsk, v) in waits:
                eng.wait_ge(sems[wsk], v)
            if fn is None:
                continue
            ins = fn(eng)
            if step:
                ins.then_inc(sems[sk], step)

    with nc.Block() as block:
        @block.tensor
        def _(pe):
            replay("pe", pe)

        @block.scalar
        def _(a):
            replay("act", a)

        @block.vector
        def _(v):
            replay("dve", v)

        @block.gpsimd
        def _(g):
            replay("pool", g)

        @block.sync
        def _(s):
            replay("sp", s)

    es.close()
    return nc, S


def host_consts(inp):
    f = np.float32
    cpar = np.zeros((128, CP_N), f)
    p = np.arange(128)
    caw = np.asarray(inp["conv_a_w"], f)
    csw = np.asarray(inp["conv_s_w"], f)
    csb = np.asarray(inp["conv_s_b"], f)
    nsg = np.asarray(inp["norm_s_g"], f)
    cpar[:, CP_CAW:CP_CAW + 48] = caw.reshape(2, 3, 8, 128).transpose(3, 0, 2, 1).reshape(128, 48)
    cpar[:, CP_CSW:CP_CSW + 192] = csw.reshape(2, 4, 24, 128).transpose(3, 0, 2, 1).reshape(128, 192)
    cpar[:, CP_CSB:CP_CSB + 48] = csb.reshape(2, 24, 128).transpose(2, 0, 1).reshape(128, 48)
    cpar[:, CP_NSG:CP_NSG + 32] = nsg.reshape(2, 16, 128).transpose(2, 0, 1).reshape(128, 32)
    cpar[:, CP_DTB:CP_DTB + 64] = np.broadcast_to(np.asarray(inp["dt_bias"], f).reshape(1, 64), (128, 64))
    cpar[:, CP_ALOG:CP_ALOG + 64] = np.broadcast_to(np.asarray(inp["a_log"], f).reshape(1, 64), (128, 64))
    cpar[:, CP_DSK:CP_DSK + 64] = np.broadcast_to(np.asarray(inp["d_skip"], f).reshape(1, 64), (128, 64))
    lnpar = np.zeros((3, 2 * D), f)
    lnpar[0, :D] = inp["ln_in_g"]
    lnpar[0, D:] = inp["ln_in_b"]
    for l in range(DEPTH):
        lnpar[1 + l, :D] = inp["ln_g"][l]
        lnpar[1 + l, D:] = inp["ln_b"][l]
    tri = (p[:, None] <= p[None, :]).astype(f)
    ident = np.eye(128, dtype=f)
    k = np.arange(96)
    esel = np.zeros((96, 32, 128), f)
    esel[k, k % 32, :] = 1.0
    return {"cpar": cpar, "lnpar": lnpar, "tri": tri, "ident": ident, "esel": esel.reshape(96, 32 * 128)}


_CACHE = {}


def kernel(**inputs):
    x = np.ascontiguousarray(np.asarray(inputs["x"], np.float32))
    nseq = BATCH // NCORES
    TCH = 4
    key = ("full", nseq, TCH)
    if key not in _CACHE:
        _CACHE[key] = build(NSEQ=nseq, NT=SEQ // (TCH * 128), TCH=TCH)[0]
    nc = _CACHE[key]
    consts = host_consts(inputs)
    shared = {
        "w_in": np.ascontiguousarray(np.asarray(inputs["w_in"], np.float32)),
        "w_a_out": np.ascontiguousarray(np.asarray(inputs["w_a_out"], np.float32)),
        "w_s_out": np.ascontiguousarray(np.asarray(inputs["w_s_out"], np.float32)),
        "w_o": np.ascontiguousarray(np.asarray(inputs["w_o"], np.float32)),
    }
    shared.update(consts)
    in_maps = []
    for i in range(NCORES):
        m = dict(shared)
        m["x"] = x[i * nseq:(i + 1) * nseq].reshape(nseq * SEQ, D)
        in_maps.append(m)
    res = run_bass_kernel_spmd(nc, in_maps, core_ids=list(range(NCORES)))
    outs = [np.asarray(r["out"], np.float32).reshape(nseq, SEQ, D) for r in res.results]
    return np.concatenate(outs, axis=0)
```

```python
import numpy as np
from contextlib import ExitStack
import concourse.bass as bass
import concourse.mybir as mybir
from concourse.bass_utils import run_bass_kernel_spmd

F32 = mybir.dt.float32
BF16 = mybir.dt.bfloat16
AF = mybir.ActivationFunctionType
ALU = mybir.AluOpType

D = 1024
DEPTH = 2
SEQ = 2048
BATCH = 32
NCORES = 8
NIN = 11296
OFF_U, OFF_B, OFF_C, OFF_ZA, OFF_ZS, OFF_XBC, OFF_DT, OFF_GA, OFF_GS = (
    0, 1024, 2048, 3072, 4096, 6144, 9216, 9248, 10272)
ALPHA = float((2 * DEPTH) ** 0.25)
LN_EPS = 1e-5
RMS_EPS = 1e-5
NSLOT = 3
SLW = 516

CP_CAW, CP_CSW, CP_CSB, CP_NSG, CP_DTB, CP_ALOG, CP_DSK, CP_N = 0, 48, 240, 288, 320, 384, 448, 512


class Res:
    __slots__ = ("name", "w", "r", "excl")

    def __init__(self, name, excl=False):
        self.name = name
        self.w = None
        self.r = {}
        self.excl = excl


class Sched:
    def __init__(self):
        self.engs = ("pe", "act", "dve", "pool", "sp")
        self.prog = {e: [] for e in self.engs}
        self.cnt = {}
        self.waited = {e: {} for e in self.engs}
        self.bank_i = 0
        self.dma_keys = set()

    def op(self, eng, fn, reads=(), writes=(), dma=None, inc=True):
        deps = {}

        def add(tok, raw):
            if tok is None:
                return
            sk, v = tok
            if sk == eng and dma is None:
                if eng == "pe" or not raw:
                    return
            if deps.get(sk, 0) < v:
                deps[sk] = v

        for r in reads:
            add(r.w, True)
            if r.excl:
                for sk, v in r.r.items():
                    add((sk, v), False)
        for w in writes:
            add(w.w, False)
            for sk, v in w.r.items():
                add((sk, v), False)
        wd = self.waited[eng]
        waits = []
        for sk, v in deps.items():
            if wd.get(sk, 0) < v:
                wd[sk] = v
                waits.append((sk, v))
        if dma is None:
            sk, step = eng, 1
        else:
            sk, step = dma, 16
            self.dma_keys.add(dma)
        n = self.cnt.get(sk, 0) + step
        self.cnt[sk] = n
        tok = (sk, n)
        self.prog[eng].append([waits, fn, sk, n])
        for r in reads:
            if r.r.get(sk, 0) < n:
                r.r[sk] = n
        for w in writes:
            w.w = tok
            w.r = {}
        return tok

    def wait_all(self, eng, toks):
        wd = self.waited[eng]
        waits = []
        for sk, v in toks:
            if wd.get(sk, 0) < v:
                wd[sk] = v
                waits.append((sk, v))
        self.prog[eng].append([waits, None, None, 0])

    def finalize(self):
        import bisect
        ref = {}
        for e in self.engs:
            for waits, fn, sk, n in self.prog[e]:
                for (wsk, v) in waits:
                    if wsk not in self.dma_keys:
                        ref.setdefault(wsk, set()).add(v)
        rank = {k: sorted(v) for k, v in ref.items()}
        out = {e: [] for e in self.engs}
        for e in self.engs:
            for waits, fn, sk, n in self.prog[e]:
                w2 = []
                for (wsk, v) in waits:
                    if wsk in self.dma_keys:
                        w2.append((wsk, v))
                    else:
                        w2.append((wsk, bisect.bisect_left(rank[wsk], v) + 1))
                if fn is None:
                    out[e].append((w2, None, None, 0))
                elif sk in self.dma_keys:
                    out[e].append((w2, fn, sk, 16))
                else:
                    out[e].append((w2, fn, sk, 1 if n in ref.get(sk, ()) else 0))
        self.final = out
        return out


def build(NSEQ=4, NT=4, TCH=4, dump=None, n_layers=DEPTH, stop=None):
    T = TCH * 128
    cur = {"ti": 0}
    if stop is not None and ":" in stop:
        stop_ti, stop_ph = int(stop.split(":")[0]), stop.split(":")[1]
        stop_l = int(stop.split(":")[2]) if stop.count(":") > 1 else 0
    else:
        stop_ti, stop_ph, stop_l = 0, stop, 0

    def stopped(ph):
        return stop_ph == ph and cur["ti"] == stop_ti and cur.get("l", 0) == stop_l
    NTOK = NSEQ * NT * T
    NB = TCH * 32
    nc = bass.Bass("TRN2", target_bir_lowering=False)
    S = Sched()

    def dram(name, shape, dt, kind):
        return nc.dram_tensor(name, list(shape), dt, kind=kind).ap()

    x_d = dram("x", [NTOK, D], F32, "ExternalInput")
    out_d = dram("out", [NTOK, D], F32, "ExternalOutput")
    win_d = dram("w_in", [DEPTH, D, NIN], F32, "ExternalInput")
    wa_d = dram("w_a_out", [DEPTH, D, D], F32, "ExternalInput")
    ws_d = dram("w_s_out", [DEPTH, 2 * D, D], F32, "ExternalInput")
    wo_d = dram("w_o", [DEPTH, D, D], F32, "ExternalInput")
    cpar_d = dram("cpar", [128, CP_N], F32, "ExternalInput")
    lnpar_d = dram("lnpar", [3, 2 * D], F32, "ExternalInput")
    tri_d = dram("tri", [128, 128], F32, "ExternalInput")
    ident_d = dram("ident", [128, 128], F32, "ExternalInput")
    esel_d = dram("esel", [96, 32 * 128], F32, "ExternalInput")
    win_b = dram("win_b", [DEPTH, D, NIN], BF16, "Internal")
    wa_b = dram("wa_b", [DEPTH, D, D], BF16, "Internal")
    ws_b = dram("ws_b", [DEPTH, 2 * D, D], BF16, "Internal")
    wo_b = dram("wo_b", [DEPTH, D, D], BF16, "Internal")
    dump_d = {}
    if dump:
        for name, shape in dump.items():
            dump_d[name] = dram("dbg_" + name, shape, F32, "ExternalOutput")

    es = ExitStack()

    def sb(name, shape, dt):
        return es.enter_context(nc.sbuf_tensor(name, list(shape), dt))

    cpar = sb("cpar_s", [128, CP_N], F32)
    lnbuf = sb("lnbuf", [128, 2 * D], F32)
    tri_f = sb("tri_f", [128, 128], F32)
    ones_f = sb("ones_f", [128, 128], F32)
    ident_b = sb("ident_b", [128, 128], BF16)
    esel = sb("esel_s", [96, 32 * 128], BF16)
    abc = sb("abc", [128, DEPTH * 32], F32)
    wsl = [sb(f"wsl{i}", [128, 8, 512], BF16) for i in range(NSLOT)]
    xst = [sb(f"xst{i}", [128, D], F32) for i in range(2)]
    H = sb("H", [128, TCH, D], F32)
    hT = sb("hT", [128, 8, T], BF16)
    hb = sb("hb", [128, D], BF16)
    ya_in = sb("ya_in", [128, 8, T], BF16)
    SZ = sb("SZ", [128, TCH, 2048], BF16)
    X = sb("X", [128, TCH * 2048], BF16)
    BT = sb("BT", [128, 4, T], BF16)
    CT = sb("CT", [128, 4, T], BF16)
    Btok = sb("Btok", [128, TCH, 512], BF16)
    S32 = [sb(f"S32_{l}", [128, 2048], F32) for l in range(DEPTH)]
    Sbf = sb("Sbf", [128, 2048], BF16)
    YNT = sb("YNT", [128, 16, T], BF16)
    hist_a = sb("hist_a", [128, DEPTH, 8, 2], F32)
    hist_s = sb("hist_s", [128, DEPTH, 24, 3], F32)
    arena = sb("arena", [128, 14 * SLW], F32)
    dtv = sb("dtv", [128, NB], F32)
    dte_ = sb("dte", [128, NB], F32)
    dt_ = sb("dt", [128, NB], F32)
    dtA = sb("dtA", [128, NB], F32)
    dtdte = sb("dtdte", [128, NB], F32)
    dtA3 = sb("dtA3", [128, 3, TCH, 96], BF16)
    dsp = sb("dsp", [128, 3, NB], BF16)
    dr1 = sb("dr1", [128, NB], F32)
    dr2 = sb("dr2", [128, NB], F32)
    tri_b = sb("tri_b", [128, 128], BF16)
    ones_b = sb("ones_b", [128, 128], BF16)
    negA = sb("negA", [128, NB], F32)
    Ecum = sb("Ecum", [128, NB], F32)
    CDs = sb("CD", [128, NB], F32)
    dtend = sb("dtend", [128, NB], F32)
    A3 = sb("A3", [96, T], BF16)
    A3m = sb("A3m", [96, T], BF16)
    stt = sb("stt", [128, 2, 6], F32)
    mv = sb("mv", [128, 4], F32)
    ss = sb("ss", [128, 8], F32)
    mhalf = sb("mhalf", [128, 8], F32)
    banks = [es.enter_context(nc.psum_tensor(f"bank{i}", [128, 512], F32)) for i in range(8)]

    R = {}

    def res(name):
        if name not in R:
            R[name] = Res(name)
        return R[name]

    bres = [res(f"bank{i}") for i in range(8)]
    for b_ in bres:
        b_.excl = True
    slres = [res(f"wsl{i}") for i in range(NSLOT)]
    ares = [res(f"ar{i}") for i in range(14)]

    def aslot(i, w=SLW):
        return arena[:, i * SLW:i * SLW + w]

    def aslot_bf(i, lo, n):
        v = arena[:, i * SLW:i * SLW + 512].bitcast(BF16)
        return v[:, lo:lo + n]

    bank_pool = {"ids": list(range(8))}

    def newbank():
        ids = bank_pool["ids"]
        i = ids[S.bank_i % len(ids)]
        S.bank_i += 1
        return banks[i], bres[i]

    def cp(off, n):
        return cpar[:, off:off + n]

    cst_res = [res("cpar"), res("tri"), res("ident"), res("esel"), res("ones"), res("abc")]
    S.op("pool", lambda e: e.dma_start(out=cpar[:, :], in_=cpar_d[:, :]), writes=[res("cpar")], dma="cst")
    S.op("pool", lambda e: e.dma_start(out=tri_f[:, :], in_=tri_d[:, :]), writes=[res("tri")], dma="cst")
    S.op("pool", lambda e: e.dma_start(out=ident_b[:, :], in_=ident_d[:, :]), writes=[res("ident")], dma="cst")
    S.op("pool", lambda e: e.dma_start(out=tri_b[:, :], in_=tri_d[:, :]), writes=[res("trib")], dma="cst")
    S.op("pool", lambda e: e.dma_start(out=esel[:, :], in_=esel_d[:, :]), writes=[res("esel")], dma="cst")
    tot = ("cst", S.cnt["cst"])
    for nm in ("cpar", "tri", "ident", "esel", "trib"):
        res(nm).w = tot
    S.op("pool", lambda e: e.memset(ones_f[:, :], 1.0), writes=[res("ones")])
    S.op("pool", lambda e: e.memset(mhalf[:, :], -0.5), writes=[res("mhalf")])
    S.op("pool", lambda e: e.memset(ones_b[:, :], 1.0), writes=[res("onesb")])
    S.op("act", lambda e: e.activation(out=abc[:, :], in_=cp(CP_ALOG, 64), func=AF.Exp),
         reads=[res("cpar")], writes=[res("abc")])
    S.op("dve", lambda e: e.tensor_scalar(out=abc[:, :], in0=abc[:, :], scalar1=-1.0, scalar2=None, op0=ALU.mult),
         reads=[res("abc")], writes=[res("abc")])

    def cast(l, grp, dst, src):
        S.op("pool", lambda e: e.dma_start(out=dst, in_=src), writes=[res(f"cv{l}{grp}")], dma=f"cv{l}{grp}")

    cast_q = []
    cast_left = {}

    for l in range(n_layers):
        for (grp, c0, c1) in ((0, 0, OFF_XBC), (1, OFF_XBC, NIN)):
            for kc in range(8):
                cast_q.append((l, grp, win_b[l, kc * 128:(kc + 1) * 128, c0:c1], win_d[l, kc * 128:(kc + 1) * 128, c0:c1]))
        for kc in range(8):
            cast_q.append((l, 2, wa_b[l, kc * 128:(kc + 1) * 128, :], wa_d[l, kc * 128:(kc + 1) * 128, :]))
        for kc in range(16):
            cast_q.append((l, 2, ws_b[l, kc * 128:(kc + 1) * 128, :], ws_d[l, kc * 128:(kc + 1) * 128, :]))
        for kc in range(8):
            cast_q.append((l, 3, wo_b[l, kc * 128:(kc + 1) * 128, :], wo_d[l, kc * 128:(kc + 1) * 128, :]))
    for (l_, g_, _, _) in cast_q:
        cast_left[(l_, g_)] = cast_left.get((l_, g_), 0) + 1

    def emit_casts(n):
        for _ in range(n):
            if not cast_q:
                return
            l_, g_, dst, src = cast_q.pop(0)
            cast(l_, g_, dst, src)
            cast_left[(l_, g_)] -= 1

    emit_casts(8)

    def layer_blocks(l):
        bl = []
        for nm, off in (("u", OFF_U), ("C", OFF_C), ("B", OFF_B), ("z", OFF_ZA)):
            for q in range(2):
                bl.append((nm + str(q), win_b, l, 0, off + q * 512, 512, 0))
        for q in range(6):
            bl.append((f"xbc{q}", win_b, l, 0, OFF_XBC + q * 512, 512, 1))
            if q < 4:
                bl.append((f"zs{q}", win_b, l, 0, OFF_ZS + q * 512, 512, 0))
            elif q == 4:
                bl.append(("dt", win_b, l, 0, OFF_DT, 32, 1))
        for h in range(2):
            bl.append((f"ga{h}", win_b, l, 0, OFF_GA + h * 512, 512, 1))
            bl.append((f"gs{h}", win_b, l, 0, OFF_GS + h * 512, 512, 1))
            bl.append((f"wa{h}", wa_b, l, 0, h * 512, 512, 2))
            bl.append((f"wsa{h}", ws_b, l, 0, h * 512, 512, 2))
            bl.append((f"wsb{h}", ws_b, l, 1024, h * 512, 512, 2))
        for h in range(2):
            bl.append((f"wo{h}", wo_b, l, 0, h * 512, 512, 3))
        return bl

    tiles = [(s, t) for s in range(NSEQ) for t in range(NT)]
    gblocks = []
    for _ in tiles:
        for l in range(n_layers):
            gblocks.extend(layer_blocks(l))
    st = {"next_load": 0, "next_use": 0}

    def prefetch():
        i = st["next_load"]
        if i >= len(gblocks):
            return
        st["next_load"] += 1
        nm, src, l, r0, c0, ncol, grp = gblocks[i]
        while cast_left[(l, grp)] > 0:
            emit_casts(1)
        s = i % NSLOT
        src_ap = src[l, r0:r0 + 1024, c0:c0 + ncol].rearrange("(kc p) c -> p kc c", p=128)
        dst_ap = wsl[s][:, :, 0:ncol]
        S.op("sp", lambda e: e.dma_start(out=dst_ap, in_=src_ap), reads=[res(f"cv{l}{grp}")],
             writes=[slres[s]], dma=f"wl{s}")

    def use_block(name):
        i = st["next_use"]
        assert gblocks[i][0] == name, (gblocks[i][0], name)
        st["next_use"] += 1
        s = i % NSLOT
        return wsl[s], slres[s]

    def done_block():
        prefetch()
        emit_casts(3)

    for _ in range(NSLOT):
        prefetch()

    def mmgroup(out_ap, bres_, pairs, reads, first_start=True, last_inc=True):
        n = len(pairs)
        for i, (lt, rh) in enumerate(pairs):
            S.op("pe", lambda e, lt=lt, rh=rh, i=i: e.matmul(out_ap, lhsT=lt, rhs=rh, start=(first_start and i == 0),
                                                          stop=(i == n - 1)),
                 reads=reads, writes=[bres_], inc=(last_inc and i == n - 1))

    def layernorm(src_fn, src_res, c, lnres, eps, l_next):
        Hc = H[:, c, :]
        Hres = res(f"H{c}")
        for k in range(2):
            S.op("dve", lambda e, k=k: e.bn_stats(out=stt[:, k, :], in_=src_fn(k)), reads=src_res, writes=[res("stt")])
        S.op("dve", lambda e: e.bn_aggr(out=mv[:, 0:2], in_=stt[:, :, :]), reads=[res("stt")], writes=[res("mv")])
        S.op("dve", lambda e: e.tensor_scalar(out=mv[:, 3:4], in0=mv[:, 1:2], scalar1=float(eps), scalar2=None,
                                              op0=ALU.add), reads=[res("mv")], writes=[res("mv3")])
        S.op("pool", lambda e: e.tensor_tensor(out=mv[:, 2:3], in0=mv[:, 3:4], in1=mhalf[:, 0:1], op=ALU.pow),
             reads=[res("mv3"), res("mhalf")], writes=[res("mv2")])
        for k in range(2):
            S.op("dve", lambda e, k=k: e.tensor_scalar(out=Hc[:, k * 512:(k + 1) * 512], in0=src_fn(k),
                                                       scalar1=mv[:, 0:1], scalar2=mv[:, 2:3],
                                                       op0=ALU.subtract, op1=ALU.mult),
                 reads=src_res + [res("mv"), res("mv2")], writes=[Hres])
        S.op("dve", lambda e: e.tensor_tensor(out=Hc, in0=Hc, in1=lnbuf[:, 0:D], op=ALU.mult),
             reads=[Hres, lnres], writes=[Hres])
        S.op("dve", lambda e: e.tensor_tensor(out=Hc, in0=Hc, in1=lnbuf[:, D:2 * D], op=ALU.add),
             reads=[Hres, lnres], writes=[Hres])
        if l_next:
            S.op("act", lambda e: e.activation(out=hb[:, :], in_=Hc, func=AF.Copy), reads=[Hres], writes=[res("hb")])
            bk, br = newbank()
            bv = bk[:, :].bitcast(BF16)
            for kc in range(8):
                S.op("pe", lambda e, kc=kc: e.transpose(out=bv[:, kc * 128:(kc + 1) * 128],
                                                        in_=hb[:, kc * 128:(kc + 1) * 128], identity=ident_b[:, :]),
                     reads=[res("hb"), res("ident")], writes=[br], inc=(kc == 7))
            S.op("act", lambda e: e.activation(out=hT[:, :, c * 128:(c + 1) * 128],
                                               in_=bv.rearrange("p (a b) -> p a b", a=8), func=AF.Copy),
                 reads=[br], writes=[res(f"hT{c}")])

    def load_ln(i):
        src = lnpar_d[i:i + 1, :].partition_broadcast(128)
        S.op("pool", lambda e: e.dma_start(out=lnbuf[:, :], in_=src), writes=[res("lnbuf")], dma="lnl")

    hTres = [res(f"hT{c}") for c in range(TCH)]
    Xres = [res(f"X{c}") for c in range(TCH)]
    mixres = Xres[0:max(1, T // 256)]
    mixedT = X[:, 0:8 * T].rearrange("p (j t) -> p j t", j=8)

    def dumpit(name, ap, rs):
        if dump and name in dump_d and not st.get("dumped_" + name):
            st["dumped_" + name] = True
            S.op("pool", lambda e: e.dma_start(out=dump_d[name], in_=ap), reads=rs, dma="dbg")

    def layer(l, seq_first, seq_last_tile, is_last_layer, row0):
        caw = cp(CP_CAW, 48).rearrange("p (l c k) -> p l c k", l=2, c=8)
        csw = cp(CP_CSW, 192).rearrange("p (l c k) -> p l c k", l=2, c=24)
        csb = cp(CP_CSB, 48).rearrange("p (l c) -> p l c", l=2)
        nsg = cp(CP_NSG, 32).rearrange("p (l c) -> p l c", l=2)
        dtb = cp(CP_DTB, 64)[:, l * 32:(l + 1) * 32]
        dsk = cp(CP_DSK, 64)[:, l * 32:(l + 1) * 32]
        A_l = abc[:, l * 32:(l + 1) * 32]
        cres = res("cpar")

        def inproj_fm(slot, sres, q):
            bk, br = newbank()
            mmgroup(bk[:, 0:T], br, [(slot[:, kc, q * 128:(q + 1) * 128], hT[:, kc, :]) for kc in range(8)],
                    reads=[sres] + hTres)
            return bk, br

        for q2 in range(2):
            slot, sres = use_block(f"u{q2}")
            for q in range(4):
                j = q2 * 4 + q
                bk, br = inproj_fm(slot, sres, q)
                S.op("act", lambda e, bk=bk, j=j: e.activation(out=aslot(j, T), in_=bk[:, 0:T], func=AF.Copy),
                     reads=[br], writes=[ares[j]])
            done_block()
        for q2 in range(2):
            slot, sres = use_block(f"C{q2}")
            for q in range(4):
                j = q2 * 4 + q
                bk, br = inproj_fm(slot, sres, q)
                ci = 8 + (j % 2)
                cu = aslot(ci)
                hres = res(f"hista{l}_{j}")
                S.op("pool", lambda e, cu=cu, j=j: e.tensor_copy(out=cu[:, 0:2], in_=hist_a[:, l, j, :]),
                     reads=[hres], writes=[ares[ci]])
                S.op("dve", lambda e, cu=cu, bk=bk, j=j: e.tensor_tensor(out=cu[:, 2:2 + T], in0=bk[:, 0:T],
                                                                        in1=aslot(j, T), op=ALU.mult),
                     reads=[br, ares[j]], writes=[ares[ci]])
                S.op("pool", lambda e, cu=cu, j=j: e.tensor_copy(out=hist_a[:, l, j, :], in_=cu[:, T:T + 2]),
                     reads=[ares[ci]], writes=[hres])
                S.op("dve", lambda e, cu=cu, j=j: e.tensor_scalar(out=aslot(j, T), in0=cu[:, 0:T],
                                                                  scalar1=caw[:, l, j, 0:1], scalar2=None, op0=ALU.mult),
                     reads=[ares[ci], cres], writes=[ares[j]])
                for k in (1, 2):
                    S.op("dve", lambda e, cu=cu, j=j, k=k: e.scalar_tensor_tensor(
                        out=aslot(j, T), in0=cu[:, k:k + T], scalar=caw[:, l, j, k:k + 1], in1=aslot(j, T),
                        op0=ALU.mult, op1=ALU.add), reads=[ares[ci], ares[j], cres], writes=[ares[j]])
            done_block()
        for q2 in range(2):
            slot, sres = use_block(f"B{q2}")
            for q in range(4):
                j = q2 * 4 + q
                bk, br = inproj_fm(slot, sres, q)
                S.op("dve", lambda e, bk=bk, j=j: e.tensor_tensor(out=aslot(j, T), in0=bk[:, 0:T], in1=aslot(j, T),
                                                                  op=ALU.mult), reads=[br, ares[j]], writes=[ares[j]])
            done_block()
        for q2 in range(2):
            slot, sres = use_block(f"z{q2}")
            for q in range(4):
                j = q2 * 4 + q
                bk, br = inproj_fm(slot, sres, q)
                si = 10 + (j % 2)
                S.op("act", lambda e, bk=bk, si=si: e.activation(out=aslot(si, T), in_=bk[:, 0:T], func=AF.Silu),
                     reads=[br], writes=[ares[si]])
                S.op("dve", lambda e, j=j, si=si: e.tensor_tensor(out=ya_in[:, j, :], in0=aslot(j, T), in1=aslot(si, T),
                                                                  op=ALU.mult),
                     reads=[ares[j], ares[si]], writes=[res(f"ya{j}")])
            done_block()
        dumpit("ya_in", ya_in[:, :, :], [res(f"ya{j}") for j in range(8)])
        if stopped("A"):
            return

        def phaseB_block(q):
            slot, sres = use_block(f"zs{q}")
            for c in range(TCH):
                bk, br = newbank()
                mmgroup(bk[:, :], br, [(hT[:, kc, c * 128:(c + 1) * 128], slot[:, kc, :]) for kc in range(8)],
                        reads=[sres, hTres[c]])
                S.op("act", lambda e, bk=bk, c=c, q=q: e.activation(out=SZ[:, c, q * 512:(q + 1) * 512], in_=bk[:, :],
                                                                    func=AF.Silu), reads=[br], writes=[res(f"SZ{c}")])
            done_block()
        dt_late_list = []

        def phaseB_dt():
            slot, sres = use_block("dt")
            bk, br = newbank()
            for c in range(TCH):
                mmgroup(bk[:, c * 32:(c + 1) * 32], br,
                        [(hT[:, kc, c * 128:(c + 1) * 128], slot[:, kc, 0:32]) for kc in range(8)],
                        reads=[sres, hTres[c]], last_inc=(c == TCH - 1))
            done_block()
            dres = res("dtsmall")
            S.op("dve", lambda e, bk=bk: e.tensor_tensor(out=dtv[:, :].rearrange("p (c h) -> p c h", c=TCH),
                                                         in0=bk[:, 0:NB].rearrange("p (c h) -> p c h", c=TCH),
                                                         in1=dtb.unsqueeze(1).broadcast_to([128, TCH, 32]), op=ALU.add),
                 reads=[br, cres], writes=[res("dtv")])
            S.op("act", lambda e: e.activation(out=dtv[:, :], in_=dtv[:, :], func=AF.Exp), reads=[res("dtv")],
                 writes=[res("dtv")])
            S.op("act", lambda e: e.activation(out=dt_[:, :], in_=dtv[:, :], func=AF.Ln, bias=1.0), reads=[res("dtv")],
                 writes=[res("dt")])
            S.op("dve", lambda e: e.tensor_tensor(out=dtA[:, :].rearrange("p (c h) -> p c h", c=TCH),
                                                  in0=dt_[:, :].rearrange("p (c h) -> p c h", c=TCH),
                                                  in1=A_l.unsqueeze(1).broadcast_to([128, TCH, 32]), op=ALU.mult),
                 reads=[res("dt"), res("abc")], writes=[res("dtA")])
            S.op("dve", lambda e: e.tensor_copy(out=dsp[:, 0, :], in_=dtA[:, :]), reads=[res("dtA")], writes=[res("dsp0")])
            S.op("dve", lambda e: e.tensor_tensor(out=dr1[:, :], in0=dtA[:, :], in1=dsp[:, 0, :], op=ALU.subtract),
                 reads=[res("dtA"), res("dsp0")], writes=[res("dr1")])
            S.op("dve", lambda e: e.tensor_copy(out=dsp[:, 1, :], in_=dr1[:, :]), reads=[res("dr1")], writes=[res("dsp1")])
            S.op("dve", lambda e: e.tensor_tensor(out=dr2[:, :], in0=dr1[:, :], in1=dsp[:, 1, :], op=ALU.subtract),
                 reads=[res("dr1"), res("dsp1")], writes=[res("dr2")])
            S.op("dve", lambda e: e.tensor_copy(out=dsp[:, 2, :], in_=dr2[:, :]), reads=[res("dr2")], writes=[res("dsp2")])
            for i3 in range(3):
                S.op("dve", lambda e, i3=i3: e.tensor_copy(
                    out=dtA3[:, i3, :, :].rearrange("p c (r h) -> p c r h", r=3),
                    in_=dsp[:, i3, :].rearrange("p (c h) -> p c h", c=TCH).unsqueeze(2).broadcast_to([128, TCH, 3, 32])),
                     reads=[res(f"dsp{i3}")], writes=[res("dtA3")])
            dspres = [res("dsp0"), res("dsp1"), res("dsp2")]

            def dt_late():
                bk1, br1 = newbank()
                for c in range(TCH):
                    for i3 in range(3):
                        S.op("pe", lambda e, c=c, i3=i3, bk1=bk1: e.matmul(bk1[:, c * 32:(c + 1) * 32], lhsT=tri_b[:, :],
                                                                          rhs=dsp[:, i3, c * 32:(c + 1) * 32], start=(i3 == 0),
                                                                          stop=(i3 == 2)),
                             reads=[res("trib")] + dspres, writes=[br1], inc=False)
                for c in range(TCH):
                    for i3 in range(3):
                        S.op("pe", lambda e, c=c, i3=i3, bk1=bk1: e.matmul(bk1[:, 256 + c * 32:256 + (c + 1) * 32],
                                                                          lhsT=ones_b[:, :], rhs=dsp[:, i3, c * 32:(c + 1) * 32],
                                                                          start=(i3 == 0), stop=(i3 == 2)),
                             reads=[res("onesb")] + dspres, writes=[br1], inc=(c == TCH - 1 and i3 == 2))
                bk2, br2 = newbank()
                for c in range(TCH):
                    for i3 in range(3):
                        S.op("pe", lambda e, c=c, i3=i3, bk2=bk2: e.matmul(bk2[0:96, c * 128:(c + 1) * 128],
                                                                          lhsT=dtA3[:, i3, c, :], rhs=tri_b[:, :],
                                                                          start=(i3 == 0), stop=(i3 == 2)),
                             reads=[res("trib"), res("dtA3")], writes=[br2], inc=(c == TCH - 1 and i3 == 2))
                S.op("dve", lambda e, bk1=bk1: e.tensor_scalar(out=negA[:, :], in0=bk1[:, 0:NB], scalar1=-1.0, scalar2=None,
                                                               op0=ALU.mult), reads=[br1], writes=[res("negA")])
                S.op("act", lambda e, bk1=bk1: e.activation(out=Ecum[:, :], in_=bk1[:, 0:NB], func=AF.Exp), reads=[br1],
                     writes=[res("Ecum")])
                S.op("act", lambda e, bk1=bk1: e.activation(out=CDs[:, :], in_=bk1[:, 256:256 + NB], func=AF.Exp), reads=[br1],
                     writes=[res("CD")])
                S.op("dve", lambda e, bk1=bk1: e.tensor_tensor(out=dtend[:, :], in0=bk1[:, 256:256 + NB], in1=negA[:, :],
                                                               op=ALU.add), reads=[br1, res("negA")], writes=[res("dtend")])
                S.op("act", lambda e: e.activation(out=dte_[:, :], in_=dtend[:, :], func=AF.Exp), reads=[res("dtend")],
                     writes=[res("dte")])
                S.op("dve", lambda e: e.tensor_tensor(out=dtdte[:, :], in0=dt_[:, :], in1=dte_[:, :], op=ALU.mult),
                     reads=[res("dt"), res("dte")], writes=[res("dtdte")])
                dumpit("negA", negA[:, :], [res("negA")])
                S.op("act", lambda e, bk2=bk2: e.activation(out=A3[0:96, :], in_=bk2[0:96, 0:T], func=AF.Copy), reads=[br2],
                     writes=[res("A3")])
                r1 = arena[0:96, 12 * SLW:12 * SLW + T]
                r2 = arena[0:96, 13 * SLW:13 * SLW + T]
                S.op("dve", lambda e, bk2=bk2: e.tensor_tensor(out=r1[0:96, :], in0=bk2[0:96, 0:T], in1=A3[0:96, :],
                                                               op=ALU.subtract), reads=[br2, res("A3")], writes=[ares[12]])
                S.op("dve", lambda e: e.tensor_copy(out=A3m[0:96, :], in_=r1[0:96, :]), reads=[ares[12]], writes=[res("A3m")])
                S.op("dve", lambda e: e.tensor_tensor(out=r2[0:96, :], in0=r1[0:96, :], in1=A3m[0:96, :], op=ALU.subtract),
                     reads=[ares[12], res("A3m")], writes=[ares[13]])
                S.op("dve", lambda e: e.tensor_copy(out=A3[64:96, :], in_=r2[64:96, :]), reads=[ares[13]], writes=[res("A3")])
                S.op("dve", lambda e: e.tensor_copy(out=A3[32:64, :], in_=A3m[32:64, :]), reads=[res("A3m")], writes=[res("A3")])
                dumpit("dt", dt_[:, :], [res("dt")])
                dumpit("negA", negA[:, :], [res("negA")])

            dt_late_list.append(dt_late)

        pend = []

        def phaseC_block(q2):
            slot, sres = use_block(f"xbc{q2}")
            for q in range(4):
                j = q2 * 4 + q
                bk, br = inproj_fm(slot, sres, q)
                while len(pend) > 2:
                    pend.pop(0)()
                ri = j % 2
                raw = aslot(ri)
                hres = res(f"hists{l}_{j}")
                S.op("pool", lambda e, raw=raw, j=j: e.tensor_copy(out=raw[:, 0:3], in_=hist_s[:, l, j, :]),
                     reads=[hres], writes=[ares[ri]])
                S.op("act", lambda e, raw=raw, bk=bk: e.activation(out=raw[:, 3:3 + T], in_=bk[:, 0:T], func=AF.Copy),
                     reads=[br], writes=[ares[ri]])
                S.op("pool", lambda e, raw=raw, j=j: e.tensor_copy(out=hist_s[:, l, j, :], in_=raw[:, T:T + 3]),
                     reads=[ares[ri]], writes=[hres])
                ai = 2 + ri
                acc = aslot(ai, T)
                S.op("act", lambda e, bk=bk, acc=acc, j=j: e.activation(
                    out=acc, in_=bk[:, 0:T], func=AF.Identity, scale=csw[:, l, j, 3:4], bias=csb[:, l, j:j + 1]),
                     reads=[br, cres], writes=[ares[ai]])
                for k in (0, 1, 2):
                    S.op("dve", lambda e, raw=raw, acc=acc, j=j, k=k: e.scalar_tensor_tensor(
                        out=acc, in0=raw[:, k:k + T], scalar=csw[:, l, j, k:k + 1], in1=acc,
                        op0=ALU.mult, op1=ALU.add), reads=[ares[ri], ares[ai], cres], writes=[ares[ai]])
                if j < 16:
                    xi = 4 + (j % 4)
                    xc = aslot_bf(xi, 0, T)
                    S.op("act", lambda e, acc=acc, xc=xc: e.activation(out=xc, in_=acc, func=AF.Silu),
                         reads=[ares[ai]], writes=[ares[xi]])

                    def tr_x(j=j, xi=xi, xc=xc):
                        tbk, tbr = newbank()
                        tv = tbk[:, :].bitcast(BF16)
                        for c in range(TCH):
                            S.op("pe", lambda e, c=c: e.transpose(
                                out=tv[:, c * 128:(c + 1) * 128], in_=xc[:, c * 128:(c + 1) * 128],
                                identity=ident_b[:, :]), reads=[ares[xi], res("ident")], writes=[tbr])
                        Xv = X[:, :].rearrange("p (c f) -> p c f", c=TCH)
                        S.op("dve", lambda e: e.tensor_copy(
                            out=Xv[:, :, j * 128:(j + 1) * 128], in_=tv[:, 0:T].rearrange("p (c f) -> p c f", c=TCH)),
                             reads=[tbr], writes=Xres)
                    pend.append(tr_x)
                elif j < 20:
                    g = j - 16
                    S.op("act", lambda e, acc=acc, g=g: e.activation(out=BT[:, g, :], in_=acc, func=AF.Silu),
                         reads=[ares[ai]], writes=[res(f"BT{g}")])

                    def tr_b(g=g):
                        tbk, tbr = newbank()
                        tv = tbk[:, :].bitcast(BF16)
                        for c in range(TCH):
                            S.op("pe", lambda e, c=c: e.transpose(
                                out=tv[:, c * 128:(c + 1) * 128], in_=BT[:, g, c * 128:(c + 1) * 128],
                                identity=ident_b[:, :]), reads=[res(f"BT{g}"), res("ident")], writes=[tbr])
                        S.op("dve", lambda e: e.tensor_copy(
                            out=Btok[:, :, g * 128:(g + 1) * 128], in_=tv[:, 0:T].rearrange("p (c f) -> p c f", c=TCH)),
                             reads=[tbr], writes=[res("Btok")])
                    pend.append(tr_b)
                else:
                    g = j - 20
                    S.op("act", lambda e, acc=acc, g=g: e.activation(out=CT[:, g, :], in_=acc, func=AF.Silu),
                         reads=[ares[ai]], writes=[res(f"CT{g}")])
            done_block()
        for q2 in range(6):
            phaseC_block(q2)
            if q2 < 4:
                phaseB_block(q2)
            elif q2 == 4:
                phaseB_dt()
        while pend:
            pend.pop(0)()
        for f_ in dt_late_list:
            f_()
        dumpit("X", X[:, :], Xres)
        dumpit("BT", BT[:, :, :], [res(f"BT{g}") for g in range(4)])
        dumpit("CT", CT[:, :, :], [res(f"CT{g}") for g in range(4)])
        if stopped("C"):
            return

        Sl = S32[l]
        Sres = [res(f"S32_{l}_{g}") for g in range(4)]
        sbres = [res(f"Sbf{g}") for g in range(4)]
        for g in range(4):
            S.op("pool", lambda e, g=g: e.tensor_copy(out=Sbf[:, g * 512:(g + 1) * 512], in_=Sl[:, g * 512:(g + 1) * 512]),
                 reads=[Sres[g]], writes=[sbres[g]])
        ctx = {}

        def chunk_prep(c):
            cs = slice(c * 128, (c + 1) * 128)
            cbk, cbr = newbank()
            for g in range(4):
                S.op("pe", lambda e, g=g, cbk=cbk, cs=cs: e.matmul(cbk[:, g * 128:(g + 1) * 128], lhsT=BT[:, g, cs],
                                                                   rhs=CT[:, g, cs], start=True, stop=True),
                     reads=[res(f"BT{g}"), res(f"CT{g}")], writes=[cbr])
            cbi = c % 2
            CBm = aslot(cbi, 512)
            S.op("dve", lambda e, cbk=cbk, CBm=CBm: e.tensor_tensor(
                out=CBm.rearrange("p (g l) -> p g l", g=4), in0=cbk[:, :].rearrange("p (g l) -> p g l", g=4),
                in1=tri_f[:, :].unsqueeze(1).broadcast_to([128, 4, 128]), op=ALU.mult),
                 reads=[cbr, res("tri")], writes=[ares[cbi]])

        def stage1(c, g):
            cs = slice(c * 128, (c + 1) * 128)
            last_chunk = seq_last_tile and c == TCH - 1
            cbi = c % 2
            CBm = aslot(cbi, 512)
            gi = g % 2
            hs = slice(c * 32 + g * 8, c * 32 + (g + 1) * 8)
            Xg = X[:, c * 2048 + g * 512:c * 2048 + (g + 1) * 512].rearrange("p (h d) -> p h d", h=8)
            XDg = aslot_bf(10 + gi, 0, 512)
            XSg = aslot_bf(10 + gi, 512, 512)
            XDDg = aslot_bf(12 + gi, 0, 512)
            r_xd, r_xdd = ares[10 + gi], ares[12 + gi]
            S.op("pool", lambda e: e.tensor_tensor(
                out=XDg.rearrange("p (h d) -> p h d", h=8), in0=Xg,
                in1=dt_[:, hs].unsqueeze(2).broadcast_to([128, 8, 64]), op=ALU.mult),
                 reads=[Xres[c], res("dt")], writes=[r_xd])
            S.op("pool", lambda e: e.tensor_tensor(
                out=XSg.rearrange("p (h d) -> p h d", h=8), in0=Xg,
                in1=dsk[:, g * 8:(g + 1) * 8].unsqueeze(2).broadcast_to([128, 8, 64]), op=ALU.mult),
                 reads=[Xres[c], cres], writes=[r_xd])
            MT = aslot_bf(4 + gi, 0, 1024)
            for half in range(2):
                lbk, lbr = newbank()
                for hh in range(4):
                    h = g * 8 + half * 4 + hh
                    S.op("pe", lambda e, lbk=lbk, hh=hh, h=h: e.matmul(
                        lbk[:, hh * 128:(hh + 1) * 128], lhsT=esel[:, h * 128:(h + 1) * 128], rhs=A3[:, cs],
                        start=True, stop=True), reads=[res("esel"), res("A3")], writes=[lbr])
                li = 2 + half
                LT = aslot(li, 512)
                for hh in range(4):
                    h = g * 8 + half * 4 + hh
                    S.op("act", lambda e, lbk=lbk, LT=LT, hh=hh, h=h: e.activation(
                        out=LT[:, hh * 128:(hh + 1) * 128], in_=lbk[:, hh * 128:(hh + 1) * 128], func=AF.Exp,
                        bias=negA[:, c * 32 + h:c * 32 + h + 1]), reads=[lbr, res("negA")], writes=[ares[li]])
                S.op("dve", lambda e, LT=LT, half=half: e.scalar_tensor_tensor(
                    out=MT[:, half * 512:(half + 1) * 512].rearrange("p (h l) -> p h l", h=4),
                    in0=LT.rearrange("p (h l) -> p h l", h=4), scalar=1.0,
                    in1=CBm[:, g * 128:(g + 1) * 128].unsqueeze(1).broadcast_to([128, 4, 128]),
                    op0=ALU.min, op1=ALU.mult), reads=[ares[li], ares[cbi]], writes=[ares[4 + gi]])
        def stage1b(c, g):
            cs = slice(c * 128, (c + 1) * 128)
            gi = g % 2
            XDg = aslot_bf(10 + gi, 0, 512)
            XSg = aslot_bf(10 + gi, 512, 512)
            MT = aslot_bf(4 + gi, 0, 1024)
            r_xd = ares[10 + gi]
            ybk, ybr = banks[gi * 2], bres[gi * 2]
            S.op("pe", lambda e: e.matmul(ybk[:, :], lhsT=ident_b[:, :], rhs=XSg, start=True,
                                          stop=False, skip_group_check=True),
                 reads=[res("ident"), r_xd], writes=[ybr])
            for hh in range(8):
                S.op("pe", lambda e, hh=hh: e.matmul(
                    ybk[:, hh * 64:(hh + 1) * 64], lhsT=MT[:, hh * 128:(hh + 1) * 128],
                    rhs=XDg[:, hh * 64:(hh + 1) * 64], start=False, stop=True, skip_group_check=True),
                     reads=[ares[4 + gi], r_xd], writes=[ybr])
            obk, obr = banks[gi * 2 + 1], bres[gi * 2 + 1]
            S.op("pe", lambda e: e.matmul(obk[:, :], lhsT=CT[:, g, cs],
                                          rhs=Sbf[:, g * 512:(g + 1) * 512], start=True, stop=True),
                 reads=[res(f"CT{g}"), sbres[g]], writes=[obr])
            ctx[(c, g)] = (ybk, ybr, obk, obr)

        def stage2(c, g):
            cs = slice(c * 128, (c + 1) * 128)
            last_chunk = seq_last_tile and c == TCH - 1
            gi = g % 2
            hs = slice(c * 32 + g * 8, c * 32 + (g + 1) * 8)
            ybk, ybr, obk, obr = ctx.pop((c, g))
            yn = aslot_bf(12 + gi, 512, 512)
            r_yn = res(f"yn{gi}")
            if not last_chunk:
                Xg2 = X[:, c * 2048 + g * 512:c * 2048 + (g + 1) * 512].rearrange("p (h d) -> p h d", h=8)
                S.op("pool", lambda e: e.tensor_tensor(
                    out=aslot_bf(12 + gi, 0, 512).rearrange("p (h d) -> p h d", h=8), in0=Xg2,
                    in1=dtdte[:, hs].unsqueeze(2).broadcast_to([128, 8, 64]), op=ALU.mult),
                     reads=[Xres[c], res("dtdte")], writes=[ares[12 + gi]])
            t1 = aslot(6 + gi, 512)
            t3 = aslot(8 + gi, 512)
            S.op("dve", lambda e: e.tensor_tensor(
                out=t1.rearrange("p (h d) -> p h d", h=8), in0=obk[:, :].rearrange("p (h d) -> p h d", h=8),
                in1=Ecum[:, hs].unsqueeze(2).broadcast_to([128, 8, 64]), op=ALU.mult),
                 reads=[obr, res("Ecum")], writes=[ares[6 + gi]])
            S.op("dve", lambda e: e.tensor_tensor(out=t1, in0=ybk[:, :], in1=t1, op=ALU.add),
                 reads=[ybr, ares[6 + gi]], writes=[ares[6 + gi]])
            if c == 0 and g == 0:
                dumpit("ypre", t1, [ares[6 + gi]])
            S.op("pool", lambda e: e.tensor_tensor(
                out=t3, in0=t1, in1=SZ[:, c, g * 512:(g + 1) * 512], op=ALU.mult),
                 reads=[ares[6 + gi], res(f"SZ{c}")], writes=[ares[8 + gi]])
            ssr = res(f"ss{gi}")
            S.op("pool", lambda e: e.memset(ss[:, gi:gi + 1], 0.0), writes=[ssr])
            S.op("act", lambda e: e.activation(out=yn, in_=t3, func=AF.Square, scale=float(512.0 ** -0.5),
                                               accum_out=ss[:, gi:gi + 1]),
                 reads=[ares[8 + gi], ssr], writes=[r_yn, ssr])
            S.op("dve", lambda e: e.tensor_scalar(out=ss[:, 2 + gi:3 + gi], in0=ss[:, gi:gi + 1],
                                                  scalar1=RMS_EPS, scalar2=None, op0=ALU.add),
                 reads=[ssr], writes=[res(f"rt{gi}")])
            S.op("pool", lambda e: e.tensor_tensor(out=ss[:, 4 + gi:5 + gi], in0=ss[:, 2 + gi:3 + gi],
                                                   in1=mhalf[:, 0:1], op=ALU.pow),
                 reads=[res(f"rt{gi}"), res("mhalf")], writes=[res(f"rs{gi}")])
            S.op("act", lambda e: e.activation(out=yn, in_=t3, func=AF.Copy, scale=ss[:, 4 + gi:5 + gi]),
                 reads=[ares[8 + gi], res(f"rs{gi}")], writes=[r_yn])
        def stage2b(c, g):
            cs = slice(c * 128, (c + 1) * 128)
            last_chunk = seq_last_tile and c == TCH - 1
            gi = g % 2
            hs = slice(c * 32 + g * 8, c * 32 + (g + 1) * 8)
            Xg = X[:, c * 2048 + g * 512:c * 2048 + (g + 1) * 512].rearrange("p (h d) -> p h d", h=8)
            XDDg = aslot_bf(12 + gi, 0, 512)
            yn = aslot_bf(12 + gi, 512, 512)
            r_xdd = ares[12 + gi]
            r_yn = res(f"yn{gi}")
            tbk, tbr = newbank()
            tv = tbk[:, :].bitcast(BF16)
            for k in range(4):
                S.op("pe", lambda e, k=k: e.transpose(
                    out=tv[:, k * 128:(k + 1) * 128], in_=yn[:, k * 128:(k + 1) * 128], identity=ident_b[:, :]),
                     reads=[r_yn, res("ident")], writes=[tbr])
            S.op("dve", lambda e: e.tensor_tensor(
                out=YNT[:, g * 4:(g + 1) * 4, cs], in0=tv[:, 0:512].rearrange("p (k t) -> p k t", k=4),
                in1=nsg[:, l, g * 4:(g + 1) * 4].unsqueeze(2).broadcast_to([128, 4, 128]), op=ALU.mult),
                 reads=[tbr, cres], writes=[res(f"YNT{g}")])
            if not last_chunk:
                sbk, sbr_ = newbank()
                S.op("pe", lambda e: e.matmul(
                    sbk[:, :], lhsT=Btok[:, c, g * 128:(g + 1) * 128], rhs=XDDg, start=True, stop=True),
                     reads=[res("Btok"), r_xdd], writes=[sbr_])
                Sg = Sl[:, g * 512:(g + 1) * 512]
                S.op("pool", lambda e: e.tensor_tensor(
                    out=Sg.rearrange("p (h d) -> p h d", h=8), in0=Sg.rearrange("p (h d) -> p h d", h=8),
                    in1=CDs[:, hs].unsqueeze(2).broadcast_to([128, 8, 64]), op=ALU.mult),
                     reads=[Sres[g], res("CD")], writes=[Sres[g]])
                S.op("dve", lambda e: e.tensor_tensor(out=Sg, in0=sbk[:, :], in1=Sg, op=ALU.add),
                     reads=[sbr_, Sres[g]], writes=[Sres[g]])
                S.op("pool", lambda e: e.tensor_copy(out=Sbf[:, g * 512:(g + 1) * 512], in_=Sg),
                     reads=[Sres[g]], writes=[sbres[g]])

        seq_cg = [(c, g) for c in range(TCH) for g in range(4)]
        bank_pool["ids"] = [4, 5, 6, 7]
        nst = len(seq_cg)
        for i in range(nst + 3):
            if i < nst:
                if seq_cg[i][1] == 0:
                    chunk_prep(seq_cg[i][0])
                stage1(*seq_cg[i])
            if 0 <= i - 1 < nst:
                stage1b(*seq_cg[i - 1])
            if 0 <= i - 2 < nst:
                stage2(*seq_cg[i - 2])
            if 0 <= i - 3 < nst:
                stage2b(*seq_cg[i - 3])
        bank_pool["ids"] = list(range(8))
        dumpit("YNT", YNT[:, :, :], [res(f"YNT{g}") for g in range(4)])
        if stopped("D"):
            return

        for half in range(2):
            slot, sres = use_block(f"ga{half}")
            for q in range(4):
                bk, br = inproj_fm(slot, sres, q)
                S.op("act", lambda e, bk=bk, q=q: e.activation(out=aslot(q, T), in_=bk[:, 0:T], func=AF.Tanh, scale=0.5),
                     reads=[br], writes=[ares[q]])
            done_block()
            slot, sres = use_block(f"gs{half}")
            for q in range(4):
                bk, br = inproj_fm(slot, sres, q)
                S.op("act", lambda e, bk=bk, q=q: e.activation(out=aslot(4 + q, T), in_=bk[:, 0:T], func=AF.Tanh,
                                                               scale=0.5), reads=[br], writes=[ares[4 + q]])
            done_block()
            slot, sres = use_block(f"wa{half}")
            for q in range(4):
                bk, br = newbank()
                mmgroup(bk[:, 0:T], br, [(slot[:, kc, q * 128:(q + 1) * 128], ya_in[:, kc, :]) for kc in range(8)],
                        reads=[sres] + [res(f"ya{j}") for j in range(8)])
                S.op("dve", lambda e, bk=bk, q=q: e.scalar_tensor_tensor(
                    out=aslot(q, T), in0=aslot(q, T), scalar=1.0, in1=bk[:, 0:T], op0=ALU.add, op1=ALU.mult),
                     reads=[br, ares[q]], writes=[ares[q]])
            done_block()
            slota, sresa = use_block(f"wsa{half}")
            slotb, sresb = use_block(f"wsb{half}")
            for q in range(4):
                bk, br = newbank()
                pairs = [(slota[:, kc, q * 128:(q + 1) * 128], YNT[:, kc, :]) for kc in range(8)] + \
                        [(slotb[:, kc, q * 128:(q + 1) * 128], YNT[:, 8 + kc, :]) for kc in range(8)]
                mmgroup(bk[:, 0:T], br, pairs, reads=[sresa, sresb] + [res(f"YNT{g}") for g in range(4)])
                S.op("dve", lambda e, bk=bk, q=q: e.scalar_tensor_tensor(
                    out=aslot(4 + q, T), in0=aslot(4 + q, T), scalar=1.0, in1=bk[:, 0:T], op0=ALU.add, op1=ALU.mult),
                     reads=[br, ares[4 + q]], writes=[ares[4 + q]])
                j = half * 4 + q
                S.op("pool", lambda e, q=q, j=j: e.tensor_tensor(out=mixedT[:, j, :], in0=aslot(q, T),
                                                                 in1=aslot(4 + q, T), op=ALU.add),
                     reads=[ares[q], ares[4 + q]], writes=mixres)
            done_block()
            done_block()
        dumpit("mixed", mixedT, mixres)
        slot0, sres0 = use_block("wo0")
        slot1, sres1 = use_block("wo1")
        lnres = res("lnbuf")
        for c in range(TCH):
            bks = []
            for half, (slot, sres) in enumerate(((slot0, sres0), (slot1, sres1))):
                bk, br = newbank()
                mmgroup(bk[:, :], br, [(mixedT[:, kc, c * 128:(c + 1) * 128], slot[:, kc, :]) for kc in range(8)],
                        reads=[sres] + mixres)
                bks.append((bk, br))
            Hres = res(f"H{c}")
            for half, (bk, br) in enumerate(bks):
                S.op("dve", lambda e, bk=bk, half=half, c=c: e.scalar_tensor_tensor(
                    out=aslot(8 + half, 512), in0=H[:, c, half * 512:(half + 1) * 512], scalar=2.0 * ALPHA,
                    in1=bk[:, :], op0=ALU.mult, op1=ALU.add), reads=[br, Hres], writes=[ares[8 + half]])
            layernorm(lambda k: aslot(8 + k, 512), [ares[8], ares[9]], c, lnres, 4.0 * LN_EPS,
                      l_next=not is_last_layer)
            if is_last_layer:
                S.op("pool", lambda e, c=c: e.dma_start(out=out_d[row0 + c * 128:row0 + (c + 1) * 128, :], in_=H[:, c, :]),
                     reads=[Hres], writes=[res(f"out{c}")], dma=f"st{c}")
            else:
                if c == 0:
                    dumpit("h1", H[:, 0, :], [Hres])
        done_block()
        done_block()

    def load_x(ti, c):
        s_, t_ = tiles[ti]
        row = (s_ * NT + t_) * T + c * 128
        i = (ti * TCH + c) % 2
        S.op("pool", lambda e: e.dma_start(out=xst[i][:, :], in_=x_d[row:row + 128, :]), writes=[res(f"xst{i}")],
             dma=f"xl{i}")
        return i

    for ti, (s_, t_) in enumerate(tiles):
        cur["ti"] = ti
        row0 = (s_ * NT + t_) * T
        if t_ == 0:
            for l in range(n_layers):
                S.op("pool", lambda e, l=l: e.memset(S32[l][:, :], 0.0), writes=[res(f"S32_{l}_{g}") for g in range(4)])
                S.op("pool", lambda e, l=l: e.memset(hist_a[:, l, :, :], 0.0),
                     writes=[res(f"hista{l}_{j}") for j in range(8)])
                S.op("pool", lambda e, l=l: e.memset(hist_s[:, l, :, :], 0.0),
                     writes=[res(f"hists{l}_{j}") for j in range(24)])
        load_ln(0)
        for c in range(TCH):
            i = load_x(ti, c)
            layernorm(lambda k, i=i: xst[i][:, k * 512:(k + 1) * 512], [res(f"xst{i}")], c, res("lnbuf"), LN_EPS,
                      l_next=True)
        if ti == 0:
            dumpit("h0", H[:, 0, :], [res("H0")])
        if stopped("ln"):
            break
        for l in range(n_layers):
            load_ln(1 + l)
            cur["l"] = l
            layer(l, t_ == 0, t_ == NT - 1, l == n_layers - 1, row0)
            cur["l"] = 0
            if stop_ph is not None and ti == stop_ti and stop_ph != "L0" and l == stop_l:
                break
            if stopped("L0"):
                break
        if stop_ph is not None and ti == stop_ti:
            break

    S.wait_all("pool", [(k, v) for k, v in S.cnt.items() if k.startswith("st") or k == "dbg"])

    sems = {k: es.enter_context(nc.semaphore(k)) for k in S.cnt.keys()}
    for e in ("pe", "act", "dve", "pool"):
        if e not in sems:
            sems[e] = es.enter_context(nc.semaphore(e))

    S.finalize()

    def replay(name, eng):
        for waits, fn, sk, step in S.final[name]:
            for (wsk, v) in waits:
                eng.wait_ge(sems[wsk], v)
            if fn is None:
                continue
            ins = fn(eng)
            if step:
                ins.then_inc(sems[sk], step)

    with nc.Block() as block:
        @block.tensor
        def _(pe):
            replay("pe", pe)

        @block.scalar
        def _(a):
            replay("act", a)

        @block.vector
        def _(v):
            replay("dve", v)

        @block.gpsimd
        def _(g):
            replay("pool", g)

        @block.sync
        def _(s):
            replay("sp", s)

    es.close()
    return nc, S


def host_consts(inp):
    f = np.float32
    cpar = np.zeros((128, CP_N), f)
    p = np.arange(128)
    caw = np.asarray(inp["conv_a_w"], f)
    csw = np.asarray(inp["conv_s_w"], f)
    csb = np.asarray(inp["conv_s_b"], f)
    nsg = np.asarray(inp["norm_s_g"], f)
    cpar[:, CP_CAW:CP_CAW + 48] = caw.reshape(2, 3, 8, 128).transpose(3, 0, 2, 1).reshape(128, 48)
    cpar[:, CP_CSW:CP_CSW + 192] = csw.reshape(2, 4, 24, 128).transpose(3, 0, 2, 1).reshape(128, 192)
    cpar[:, CP_CSB:CP_CSB + 48] = csb.reshape(2, 24, 128).transpose(2, 0, 1).reshape(128, 48)
    cpar[:, CP_NSG:CP_NSG + 32] = nsg.reshape(2, 16, 128).transpose(2, 0, 1).reshape(128, 32)
    cpar[:, CP_DTB:CP_DTB + 64] = np.broadcast_to(np.asarray(inp["dt_bias"], f).reshape(1, 64), (128, 64))
    cpar[:, CP_ALOG:CP_ALOG + 64] = np.broadcast_to(np.asarray(inp["a_log"], f).reshape(1, 64), (128, 64))
    cpar[:, CP_DSK:CP_DSK + 64] = np.broadcast_to(np.asarray(inp["d_skip"], f).reshape(1, 64), (128, 64))
    lnpar = np.zeros((3, 2 * D), f)
    lnpar[0, :D] = inp["ln_in_g"]
    lnpar[0, D:] = inp["ln_in_b"]
    for l in range(DEPTH):
        lnpar[1 + l, :D] = inp["ln_g"][l]
        lnpar[1 + l, D:] = inp["ln_b"][l]
    tri = (p[:, None] <= p[None, :]).astype(f)
    ident = np.eye(128, dtype=f)
    k = np.arange(96)
    esel = np.zeros((96, 32, 128), f)
    esel[k, k % 32, :] = 1.0
    return {"cpar": cpar, "lnpar": lnpar, "tri": tri, "ident": ident, "esel": esel.reshape(96, 32 * 128)}


_CACHE = {}


def kernel(**inputs):
    x = np.ascontiguousarray(np.asarray(inputs["x"], np.float32))
    nseq = BATCH // NCORES
    TCH = 4
    key = ("full", nseq, TCH)
    if key not in _CACHE:
        _CACHE[key] = build(NSEQ=nseq, NT=SEQ // (TCH * 128), TCH=TCH)[0]
    nc = _CACHE[key]
    consts = host_consts(inputs)
    shared = {
        "w_in": np.ascontiguousarray(np.asarray(inputs["w_in"], np.float32)),
        "w_a_out": np.ascontiguousarray(np.asarray(inputs["w_a_out"], np.float32)),
        "w_s_out": np.ascontiguousarray(np.asarray(inputs["w_s_out"], np.float32)),
        "w_o": np.ascontiguousarray(np.asarray(inputs["w_o"], np.float32)),
    }
    shared.update(consts)
    in_maps = []
    for i in range(NCORES):
        m = dict(shared)
        m["x"] = x[i * nseq:(i + 1) * nseq].reshape(nseq * SEQ, D)
        in_maps.append(m)
    res = run_bass_kernel_spmd(nc, in_maps, core_ids=list(range(NCORES)))
    outs = [np.asarray(r["out"], np.float32).reshape(nseq, SEQ, D) for r in res.results]
    return np.concatenate(outs, axis=0)
```
